# Optimizing a Trainium2 kernel written in Bass

```python
import jax, jax.numpy as jnp
from jax import lax
import numpy as np

D_MODEL = 1024
BATCH = 32
SEQ = 256
DEPTH = 4
DEC_BATCH = 2
DEC_SEQ = 2048
PAST_LEN = 512

GRID_W = 64
BLOCK = 128
EPS = 1e-6
ROPE_BASE = 10000.0
NEG = -1e30
MLA_HEADS = 8
MLA_NOPE = 64
MLA_ROPE = 32
MLA_V = 64
MLA_Q_RANK = 256
MLA_KV_RANK = 128
MLA_SCALE = (MLA_NOPE + MLA_ROPE) ** -0.5
SWA_HEADS = 4
SWA_KV_HEADS = 2
SWA_GROUP = SWA_HEADS // SWA_KV_HEADS
SWA_DIM = 64
SWA_WINDOW = 128
SWA_SCALE = SWA_DIM ** -0.5
CONV_CH = 256
CONV_WIDTH = 31
MIX_MLA = MLA_HEADS * MLA_V
MIX_SWA = SWA_HEADS * SWA_DIM
MIX_WIDTH = MIX_MLA + MIX_SWA + CONV_CH
IN_SPLITS = (MLA_Q_RANK, MLA_KV_RANK, MLA_ROPE, SWA_HEADS * SWA_DIM, SWA_KV_HEADS * SWA_DIM, SWA_KV_HEADS * SWA_DIM, 2 * CONV_CH)
IN_COLS = sum(IN_SPLITS)
D_FF = 2816
FFN_WIDTH = 3

kernel_name = "hybrid_mla_swa_conv_diffusion_step"


def rmsnorm(x, g):
    xf = x.astype(jnp.float32)
    y = xf * lax.rsqrt(jnp.mean(xf * xf, axis=-1, keepdims=True) + EPS)
    return (y * g.astype(jnp.float32)).astype(x.dtype)


def layernorm(x, g, b):
    xf = x.astype(jnp.float32)
    mu = jnp.mean(xf, axis=-1, keepdims=True)
    var = jnp.mean(jnp.square(xf - mu), axis=-1, keepdims=True)
    y = (xf - mu) * lax.rsqrt(var + EPS)
    return (y * g.astype(jnp.float32) + b.astype(jnp.float32)).astype(x.dtype)


def grid_positions(T):
    rows = T // GRID_W
    row = jnp.repeat(jnp.arange(rows), GRID_W).astype(jnp.float32)
    col = jnp.tile(jnp.arange(GRID_W), rows).astype(jnp.float32)
    return row, col


def rope_2d(x, row, col):
    d = x.shape[-1]
    qd = d // 4
    T = x.shape[1]
    extra = x.ndim - 3
    freqs = ROPE_BASE ** (-jnp.arange(qd, dtype=jnp.float32) / qd)

    def rot(xh, pos):
        ang = pos[:, None] * freqs[None, :]
        cos = jnp.cos(ang).reshape((1, T) + (1,) * extra + (qd,)).astype(x.dtype)
        sin = jnp.sin(ang).reshape((1, T) + (1,) * extra + (qd,)).astype(x.dtype)
        x1, x2 = xh[..., :qd], xh[..., qd:]
        return jnp.concatenate([x1 * cos - x2 * sin, x2 * cos + x1 * sin], axis=-1)

    return jnp.concatenate([rot(x[..., : d // 2], row), rot(x[..., d // 2:], col)], axis=-1)


def dwconv(x, w, b):
    K, C = w.shape
    pad = (K - 1) // 2
    y = lax.conv_general_dilated(x, w[:, None, :].astype(x.dtype), window_strides=(1,), padding=[(pad, pad)],
                                 dimension_numbers=("NWC", "WIO", "NWC"), feature_group_count=C)
    return y + b


def dense_attention(q, k, v, scale, sink=None):
    B, Tq, KH, G, dk = q.shape
    Tk = k.shape[1]
    nq = Tq // BLOCK
    qb = jnp.moveaxis(q.reshape(B, nq, BLOCK, KH, G, dk), 1, 0)

    def one_block(qi):
        s = jnp.einsum("bqhgd,bkhd->bhgqk", qi, k).astype(jnp.float32) * scale
        if sink is not None:
            sk = jnp.broadcast_to(sink.astype(jnp.float32)[None, :, :, None, None], s.shape[:-1] + (1,))
            s = jnp.concatenate([s, sk], axis=-1)
        p = jax.nn.softmax(s, axis=-1)[..., :Tk]
        return jnp.einsum("bhgqk,bkhd->bqhgd", p.astype(v.dtype), v)

    ob = lax.map(one_block, qb)
    return jnp.moveaxis(ob, 0, 1).reshape(B, Tq, KH, G, v.shape[-1])


def banded_window_attention(q, k, v, kc, vc, sink, scale):
    B, T, KH, G, d = q.shape
    L = kc.shape[1]
    nb = T // BLOCK
    nk = 3 * BLOCK
    qb = q.reshape(B, nb, BLOCK, KH, G, d)

    def band(a):
        ap = jnp.pad(a, ((0, 0), (BLOCK, BLOCK), (0, 0), (0, 0))).reshape(B, nb + 2, BLOCK, KH, d)
        return jnp.concatenate([ap[:, :-2], ap[:, 1:-1], ap[:, 2:]], axis=2)

    kb, vb = band(k), band(v)
    qpos = jnp.arange(nb)[:, None] * BLOCK + jnp.arange(BLOCK)[None, :]
    kpos = (jnp.arange(nb)[:, None] - 1) * BLOCK + jnp.arange(nk)[None, :]
    valid = (kpos[:, None, :] >= 0) & (kpos[:, None, :] < T) & (jnp.abs(qpos[:, :, None] - kpos[:, None, :]) <= SWA_WINDOW)
    s_band = jnp.einsum("bnqhgd,bnkhd->bnhgqk", qb, kb).astype(jnp.float32) * scale
    s_band = jnp.where(valid[None, :, None, None], s_band, NEG)
    s_ctx = jnp.einsum("bnqhgd,bkhd->bnhgqk", qb, kc).astype(jnp.float32) * scale
    s_sink = jnp.broadcast_to(sink.astype(jnp.float32)[None, None, :, :, None, None], s_band.shape[:-1] + (1,))
    p = jax.nn.softmax(jnp.concatenate([s_band, s_ctx, s_sink], axis=-1), axis=-1).astype(v.dtype)
    o = jnp.einsum("bnhgqk,bnkhd->bnqhgd", p[..., :nk], vb) + jnp.einsum("bnhgqk,bkhd->bnqhgd", p[..., nk:nk + L], vc)
    return o.reshape(B, T, KH, G, d)


def modulation(cvec, p):
    m = jax.nn.silu(cvec) @ p["ada_w"] + p["ada_b"]
    return [t[:, None, :] for t in jnp.split(m, 6, axis=-1)]


def modulate(x, g, shift, scale):
    return rmsnorm(x, g) * (1 + scale) + shift


def layer_mixer_inputs(h, p):
    B, T, _ = h.shape
    z = h @ p["w_in"]
    points, acc = [], 0
    for s in IN_SPLITS[:-1]:
        acc += s
        points.append(acc)
    cq, ckv, kr, qs, ks, vs, cu = jnp.split(z, points, axis=-1)
    q_m = (rmsnorm(cq, p["mla_q_norm_g"]) @ p["mla_w_uq"]).reshape(B, T, MLA_HEADS, MLA_NOPE + MLA_ROPE)
    ckv_n = rmsnorm(ckv, p["mla_kv_norm_g"])
    q_s = qs.reshape(B, T, SWA_KV_HEADS, SWA_GROUP, SWA_DIM)
    k_s = ks.reshape(B, T, SWA_KV_HEADS, SWA_DIM)
    v_s = vs.reshape(B, T, SWA_KV_HEADS, SWA_DIM)
    return q_m, ckv_n, kr, q_s, k_s, v_s, cu


def mla_expand(ckv_n, kr, w_ukv):
    B, L, _ = ckv_n.shape
    kv = (ckv_n @ w_ukv).reshape(B, L, MLA_HEADS, MLA_NOPE + MLA_V)
    k = jnp.concatenate([kv[..., :MLA_NOPE], jnp.broadcast_to(kr[:, :, None, :], (B, L, MLA_HEADS, MLA_ROPE))], axis=-1)
    return k, kv[..., MLA_NOPE:]


def conv_module(u, p):
    y = u[..., :CONV_CH] * jax.nn.sigmoid(u[..., CONV_CH:])
    y = dwconv(y, p["conv_dw_w"], p["conv_dw_b"])
    y = jax.nn.silu(layernorm(y, p["conv_ln_g"], p["conv_ln_b"]))
    return y @ p["conv_w_pw2"]


def mixer_output(o_mla, o_swa, cu, p):
    B, T = o_mla.shape[:2]
    o = jnp.concatenate([o_mla.reshape(B, T, MIX_MLA), o_swa.reshape(B, T, MIX_SWA), conv_module(cu, p)], axis=-1)
    return o @ p["w_out"]


def conv_ffn(h, p):
    z = dwconv(h @ p["ffn_w_up"], p["ffn_dw_w"], p["ffn_dw_b"])
    return (jax.nn.silu(z[..., D_FF:]) * z[..., :D_FF]) @ p["ffn_w_down"]


def context_layer(x, c_ctx, p):
    sh1, sc1, g1, sh2, sc2, g2 = modulation(c_ctx[None, :], p)
    h = modulate(x, p["norm1_g"], sh1, sc1)
    q_m, ckv_n, kr, q_s, k_s, v_s, cu = layer_mixer_inputs(h, p)
    k_m, v_m = mla_expand(ckv_n, kr, p["mla_w_ukv"])
    o_mla = dense_attention(q_m[:, :, :, None, :], k_m, v_m, MLA_SCALE)
    o_swa = dense_attention(q_s, k_s, v_s, SWA_SCALE, p["swa_sink"].reshape(SWA_KV_HEADS, SWA_GROUP))
    x = x + g1 * mixer_output(o_mla, o_swa, cu, p)
    x = x + g2 * conv_ffn(modulate(x, p["norm2_g"], sh2, sc2), p)
    return x, ckv_n, kr, k_s, v_s


def latent_layer(x, c, ckv_c, kr_c, k_c, v_c, p, row, col):
    sh1, sc1, g1, sh2, sc2, g2 = modulation(c, p)
    h = modulate(x, p["norm1_g"], sh1, sc1)
    q_m, ckv_n, kr, q_s, k_s, v_s, cu = layer_mixer_inputs(h, p)
    q_m = jnp.concatenate([q_m[..., :MLA_NOPE], rope_2d(q_m[..., MLA_NOPE:], row, col)], axis=-1)
    kr = rope_2d(kr, row, col)
    k_lat, v_lat = mla_expand(ckv_n, kr, p["mla_w_ukv"])
    k_ctx, v_ctx = mla_expand(ckv_c, kr_c, p["mla_w_ukv"])
    o_mla = dense_attention(q_m[:, :, :, None, :], jnp.concatenate([k_lat, k_ctx], axis=1),
                            jnp.concatenate([v_lat, v_ctx], axis=1), MLA_SCALE)
    o_swa = banded_window_attention(rope_2d(q_s, row, col), rope_2d(k_s, row, col), v_s, k_c, v_c,
                                    p["swa_sink"].reshape(SWA_KV_HEADS, SWA_GROUP), SWA_SCALE)
    x = x + g1 * mixer_output(o_mla, o_swa, cu, p)
    x = x + g2 * conv_ffn(modulate(x, p["norm2_g"], sh2, sc2), p)
    return x


def setup_inputs(seed: int = 0) -> dict:
    key = jax.random.key(seed)
    ks = jax.random.split(key, 32)
    f32 = jnp.float32
    nrm = lambda k, shape, s: jax.random.normal(k, shape, f32) * s
    gain = lambda k, shape: 1.0 + 0.05 * jax.random.normal(k, shape, f32)
    return {
        "x_prompt": nrm(ks[0], (BATCH, SEQ, D_MODEL), 1.0),
        "x_sample": nrm(ks[1], (DEC_BATCH, DEC_SEQ, D_MODEL), 1.0),
        "cache_mla_ckv": nrm(ks[2], (DEC_BATCH, DEPTH, PAST_LEN, MLA_KV_RANK), 1.0),
        "cache_mla_krope": nrm(ks[3], (DEC_BATCH, DEPTH, PAST_LEN, MLA_ROPE), 1.0),
        "cache_swa_k": nrm(ks[4], (DEC_BATCH, DEPTH, PAST_LEN, SWA_KV_HEADS, SWA_DIM), 1.0),
        "cache_swa_v": nrm(ks[5], (DEC_BATCH, DEPTH, PAST_LEN, SWA_KV_HEADS, SWA_DIM), 1.0),
        "c": nrm(ks[6], (DEC_BATCH, D_MODEL), 1.0),
        "c_ctx": nrm(ks[7], (D_MODEL,), 1.0),
        "ada_w": nrm(ks[8], (DEPTH, D_MODEL, 6 * D_MODEL), 0.5 * D_MODEL ** -0.5),
        "ada_b": nrm(ks[9], (DEPTH, 6 * D_MODEL), 0.01),
        "norm1_g": gain(ks[10], (DEPTH, D_MODEL)),
        "norm2_g": gain(ks[11], (DEPTH, D_MODEL)),
        "w_in": nrm(ks[12], (DEPTH, D_MODEL, IN_COLS), D_MODEL ** -0.5),
        "mla_q_norm_g": gain(ks[13], (DEPTH, MLA_Q_RANK)),
        "mla_w_uq": nrm(ks[14], (DEPTH, MLA_Q_RANK, MLA_HEADS * (MLA_NOPE + MLA_ROPE)), MLA_Q_RANK ** -0.5),
        "mla_kv_norm_g": gain(ks[15], (DEPTH, MLA_KV_RANK)),
        "mla_w_ukv": nrm(ks[16], (DEPTH, MLA_KV_RANK, MLA_HEADS * (MLA_NOPE + MLA_V)), MLA_KV_RANK ** -0.5),
        "swa_sink": nrm(ks[17], (DEPTH, SWA_HEADS), 0.5),
        "conv_dw_w": nrm(ks[18], (DEPTH, CONV_WIDTH, CONV_CH), CONV_WIDTH ** -0.5),
        "conv_dw_b": nrm(ks[19], (DEPTH, CONV_CH), 0.01),
        "conv_ln_g": gain(ks[20], (DEPTH, CONV_CH)),
        "conv_ln_b": nrm(ks[21], (DEPTH, CONV_CH), 0.01),
        "conv_w_pw2": nrm(ks[22], (DEPTH, CONV_CH, CONV_CH), CONV_CH ** -0.5),
        "w_out": nrm(ks[23], (DEPTH, MIX_WIDTH, D_MODEL), MIX_WIDTH ** -0.5),
        "ffn_w_up": nrm(ks[24], (DEPTH, D_MODEL, 2 * D_FF), D_MODEL ** -0.5),
        "ffn_dw_w": nrm(ks[25], (DEPTH, FFN_WIDTH, 2 * D_FF), FFN_WIDTH ** -0.5),
        "ffn_dw_b": nrm(ks[26], (DEPTH, 2 * D_FF), 0.01),
        "ffn_w_down": nrm(ks[27], (DEPTH, D_FF, D_MODEL), D_FF ** -0.5),
        "final_norm_g": gain(ks[28], (D_MODEL,)),
    }


def reference(x_prompt, x_sample, cache_mla_ckv, cache_mla_krope, cache_swa_k, cache_swa_v, c, c_ctx,
              ada_w, ada_b, norm1_g, norm2_g, w_in, mla_q_norm_g, mla_w_uq, mla_kv_norm_g, mla_w_ukv,
              swa_sink, conv_dw_w, conv_dw_b, conv_ln_g, conv_ln_b, conv_w_pw2, w_out,
              ffn_w_up, ffn_dw_w, ffn_dw_b, ffn_w_down, final_norm_g):
    row, col = grid_positions(x_sample.shape[1])
    xc, xl = x_prompt, x_sample
    new_ckv, new_kr, new_k, new_v = [], [], [], []
    for l in range(DEPTH):
        p = {
            "ada_w": ada_w[l], "ada_b": ada_b[l], "norm1_g": norm1_g[l], "norm2_g": norm2_g[l],
            "w_in": w_in[l], "mla_q_norm_g": mla_q_norm_g[l], "mla_w_uq": mla_w_uq[l],
            "mla_kv_norm_g": mla_kv_norm_g[l], "mla_w_ukv": mla_w_ukv[l], "swa_sink": swa_sink[l],
            "conv_dw_w": conv_dw_w[l], "conv_dw_b": conv_dw_b[l], "conv_ln_g": conv_ln_g[l],
            "conv_ln_b": conv_ln_b[l], "conv_w_pw2": conv_w_pw2[l], "w_out": w_out[l],
            "ffn_w_up": ffn_w_up[l], "ffn_dw_w": ffn_dw_w[l], "ffn_dw_b": ffn_dw_b[l],
            "ffn_w_down": ffn_w_down[l],
        }
        xc, ckv_n, kr, k_s, v_s = context_layer(xc, c_ctx, p)
        new_ckv.append(ckv_n)
        new_kr.append(kr)
        new_k.append(k_s)
        new_v.append(v_s)
        xl = latent_layer(xl, c, cache_mla_ckv[:, l], cache_mla_krope[:, l], cache_swa_k[:, l], cache_swa_v[:, l], p, row, col)
    y_prompt = rmsnorm(xc, final_norm_g)
    y_sample = rmsnorm(xl, final_norm_g)
    state_mla_ckv = jnp.stack(new_ckv, axis=1)
    state_mla_krope = jnp.stack(new_kr, axis=1)
    state_swa_k = jnp.stack(new_k, axis=1)
    state_swa_v = jnp.stack(new_v, axis=1)
    return (y_prompt, y_sample, state_mla_ckv, state_mla_krope, state_swa_k, state_swa_v)
```

```python
import numpy as np
import concourse.bass as bass
import concourse.mybir as mybir
from concourse.bass_utils import run_bass_kernel_spmd

F32 = mybir.dt.float32
BF16 = mybir.dt.bfloat16
AF = mybir.ActivationFunctionType
ALU = mybir.AluOpType

D = 1024
T = 2048
DEPTH = 4
L = 512
NKEY = T + L
NKT = NKEY // 128
EPS = 1e-6
MLA_SCALE = 96.0 ** -0.5
SWA_SCALE = 64.0 ** -0.5
DFF = 2816
NEG = -30000.0
BIGM = 512.0

C_CQ, C_CKV, C_KRT, C_KRPT, C_QS, C_QSP, C_K0, C_K0P, C_K1, C_K1P, C_VS, C_CU = (
    0, 256, 384, 480, 576, 832, 1088, 1216, 1344, 1472, 1600, 1728)
NWIN = 2240
V_ADAB, V_N1G, V_N2G, V_QNG, V_KVG, V_CDW, V_CDB, V_CLG, V_CLB, V_FDW, V_FDB, V_SINK = (
    0, 48, 56, 64, 66, 67, 129, 131, 133, 135, 267, 311)
NL = 315
V_FNG, V_CVEC, V_FLAG, V_CTXB = 4 * NL, 4 * NL + 8, 4 * NL + 16, 4 * NL + 17
NV = 4 * NL + 18

ENGS = ("pe", "act", "dve", "pool", "sp")


class _Stop(Exception):
    pass


class V:
    def __init__(self, ap, t):
        self.ap, self.t = ap, t

    def __getitem__(self, k):
        return V(self.ap[k], self.t)

    def re(self, pat, **kw):
        return V(self.ap.rearrange(pat, **kw), self.t)

    def bc(self, axis, shape):
        return V(self.ap.unsqueeze(axis).to_broadcast(list(shape)), self.t)


class Tile(V):
    def __init__(self, ap, space, lo, hi, ranges=None):
        V.__init__(self, ap, self)
        self.space, self.lo, self.hi = space, lo, hi
        self.ranges = ranges if ranges is not None else [(lo, hi)]
        self.w, self.r = {}, {}
        self.ov = [self]
        self.dsem = None
        self.dcum = 0


def _ap(x):
    return x.ap if isinstance(x, V) else x


class Prog:
    def __init__(self, nc):
        self.nc = nc
        self.ins = {e: [] for e in ENGS}
        self.waited = {e: {} for e in ENGS}
        self.tiles = []
        self.ndsem = 0
        self.dstate = {}
        self.count = 0
        self.limit = None

    def register(self, t):
        for u in self.tiles:
            if u.space == t.space and u.lo < t.hi and t.lo < u.hi:
                if any(a < d and c < b for (a, b) in u.ranges for (c, d) in t.ranges):
                    u.ov.append(t)
                    t.ov.append(u)
        self.tiles.append(t)
        return t

    def unregister(self, ts):
        pass

    def _deps(self, reads, writes):
        d = {}

        def add(dic):
            for k, v in dic.items():
                if d.get(k, -1) < v:
                    d[k] = v
        for t in reads:
            for u in t.ov:
                add(u.w)
        for t in writes:
            for u in t.ov:
                add(u.w)
                add(u.r)
        return d

    def _waits(self, eng, d):
        waits = []
        for k, v in d.items():
            if k == eng and eng in ("pe", "sp"):
                continue
            if self.waited[eng].get(k, -1) >= v:
                continue
            self.waited[eng][k] = v
            waits.append((k, v))
            if k in ENGS:
                self.ins[k][v]["marked"] = True
        return waits

    def _tick(self):
        self.count += 1
        if self.limit is not None and self.count > self.limit:
            raise _Stop

    def op(self, eng, fn, reads, writes):
        self._tick()
        reads = [x.t for x in reads if isinstance(x, V)]
        writes = [x.t for x in writes if isinstance(x, V)]
        waits = self._waits(eng, self._deps(reads, writes))
        idx = len(self.ins[eng])
        self.ins[eng].append(dict(fn=fn, waits=waits, marked=False, dma=None))
        for t in reads:
            t.r[eng] = idx
        for t in writes:
            t.w[eng] = idx

    def dma(self, q, out, in_):
        self._tick()
        reads = [in_.t] if isinstance(in_, V) else []
        writes = [out.t] if isinstance(out, V) else []
        waits = self._waits(q, self._deps(reads, writes))
        st = writes[0] if writes else reads[0]
        rk = (st.space, st.lo, st.hi)
        if rk not in self.dstate:
            self.dstate[rk] = [self.ndsem, 0]
            self.ndsem += 1
        ds = self.dstate[rk]
        ds[1] += 16
        st.dsem, st.dcum = ds[0], ds[1]
        key = ("d", st.dsem)
        o, i = _ap(out), _ap(in_)
        self.ins[q].append(dict(fn=lambda e: e.dma_start(out=o, in_=i), waits=waits, marked=False, dma=st.dsem))
        for t in reads:
            t.r[key] = max(t.r.get(key, 0), st.dcum)
        for t in writes:
            t.w[key] = max(t.w.get(key, 0), st.dcum)

    def emit(self, stack):
        nc = self.nc
        sems = {e: stack.enter_context(nc.semaphore("s_" + e)) for e in ENGS}
        dsems = [stack.enter_context(nc.semaphore("d%d" % i)) for i in range(self.ndsem)]
        for e in ENGS:
            c = 0
            for rec in self.ins[e]:
                if rec["marked"]:
                    c += 1
                rec["sv"] = c
        final = {v[0]: v[1] for v in self.dstate.values()}
        block = stack.enter_context(nc.Block())

        def run(e, eo):
            for rec in self.ins[e]:
                for k, v in rec["waits"]:
                    if k in ENGS:
                        eo.wait_ge(sems[k], self.ins[k][v]["sv"])
                    else:
                        eo.wait_ge(dsems[k[1]], v)
                ins = rec["fn"](eo)
                if rec["dma"] is not None:
                    ins.then_inc(dsems[rec["dma"]], 16)
                elif rec["marked"]:
                    ins.then_inc(sems[e], 1)
            if e == "sp":
                for k, v in final.items():
                    eo.wait_ge(dsems[k], v)
                for k in ("pe", "act", "dve", "pool"):
                    n = self.ins[k][-1]["sv"] if self.ins[k] else 0
                    if n:
                        eo.wait_ge(sems[k], n)

        @block.tensor
        def _(eo):
            run("pe", eo)

        @block.scalar
        def _(eo):
            run("act", eo)

        @block.vector
        def _(eo):
            run("dve", eo)

        @block.gpsimd
        def _(eo):
            run("pool", eo)

        @block.sync
        def _(eo):
            run("sp", eo)


def build_program(depth=DEPTH, stop=None):
    nc = bass.Bass("TRN2", target_bir_lowering=False)
    P = Prog(nc)
    if stop is not None and stop.startswith('n'):
        P.limit = int(stop[1:])

    def din(name, shape):
        return nc.dram_tensor(name, list(shape), F32, kind="ExternalInput").ap()

    def dout(name, shape):
        return nc.dram_tensor(name, list(shape), F32, kind="ExternalOutput").ap()

    x_d = din("x", [T, D])
    vecs_d = din("vecs", [128, NV])
    ident_d = din("ident", [128, 128])
    ropeS_d = din("ropeS", [128, 2, T])
    ropeM_d = din("ropeM", [32, 2, T])
    qaug_d = din("qaug", [9, T])
    kaug_d = din("kaug", [9, NKEY])
    smask_d = din("smask", [128, 4, 256])
    cckv_d = din("cckv", [DEPTH, L, 128])
    ckr_d = din("ckr", [DEPTH, L, 32])
    ck_d = din("ck", [DEPTH, L, 128])
    cv_d = din("cv", [DEPTH, L, 128])
    adaw_d = din("ada_w", [DEPTH, D, 6 * D])
    win_d = din("w_in_ext", [DEPTH, D, NWIN])
    wuq_d = din("w_uq_ext", [DEPTH, 256, 1536])
    wukv_d = din("w_ukv", [DEPTH, 128, 1024])
    pw2_d = din("pw2", [DEPTH, 256, 256])
    wout_d = din("w_out", [DEPTH, D, D])
    wup_d = din("w_up", [DEPTH, D, 2 * DFF])
    wdn_d = din("w_down", [DEPTH, DFF, D])
    y_d = dout("y", [T, D])
    sckv_d = dout("st_ckv", [DEPTH, T, 128])
    skr_d = dout("st_kr", [DEPTH, T, 32])
    sk_d = dout("st_k", [DEPTH, T, 128])
    sv_d = dout("st_v", [DEPTH, T, 128])

    from contextlib import ExitStack
    stack = ExitStack()
    ARENA = 210688
    arena_h = stack.enter_context(nc.sbuf_tensor("arena", [128, ARENA // 4], F32))
    base0 = nc.sbuf_base - (ARENA // 4) * 4
    arena_t = P.register(Tile(arena_h[:], "sb", 0, ARENA))
    ps_t = stack.enter_context(nc.psum_tensor("ps", [128, 4096], F32))
    ps_all = ps_t[:]

    class Mem:
        def __init__(self):
            self.top = 0
            self.n = 0

        def alloc(self, shape, dt):
            nb = int(np.prod(shape[1:])) * (2 if dt == BF16 else 4)
            nb = (nb + 63) // 64 * 64
            off = self.top
            self.top += nb
            assert self.top <= ARENA, ("SBUF overflow", self.top)
            self.n += 1
            h = nc.alloc_sbuf_tensor_at("t%d" % self.n, list(shape), dt, offset=base0 + off)
            ap = h.ap() if hasattr(h, "ap") else h[:]
            return P.register(Tile(ap, "sb", off, off + nb))

        def alloc_at(self, off, shape, dt):
            nb = int(np.prod(shape[1:])) * (2 if dt == BF16 else 4)
            nb = (nb + 63) // 64 * 64
            assert off % 64 == 0 and off + nb <= ARENA
            self.n += 1
            h = nc.alloc_sbuf_tensor_at("t%d" % self.n, list(shape), dt, offset=base0 + off)
            ap = h.ap() if hasattr(h, "ap") else h[:]
            return P.register(Tile(ap, "sb", off, off + nb))

    M = Mem()

    def psum(b0, nb=1, c0=0, c1=512):
        if nb == 1:
            ap = ps_all[:, b0 * 512 + c0: b0 * 512 + c1]
            return P.register(Tile(ap, "ps", b0 * 2048 + c0 * 4, b0 * 2048 + c1 * 4))
        ap = ps_all.rearrange("p (b n) -> p b n", b=8)[:, b0:b0 + nb, :]
        return P.register(Tile(ap, "ps", b0 * 2048, (b0 + nb) * 2048,
                               ranges=[(b * 2048 + c0 * 4, b * 2048 + c1 * 4) for b in range(b0, b0 + nb)]))

    PB = [psum(b) for b in range(8)]
    PBN = [psum(b, 1, 0, 258) for b in range(8)]
    PS03 = psum(0, 4, 0, 258)
    PS47 = psum(4, 4, 0, 258)

    def MM(out, lhsT, rhs, start=True, stop=True):
        o, l, r = _ap(out), _ap(lhsT), _ap(rhs)
        P.op("pe", lambda e: e.matmul(o, lhsT=l, rhs=r, start=start, stop=stop), [lhsT, rhs], [out])

    def TR(out, in_, ident):
        o, i, d = _ap(out), _ap(in_), _ap(ident)
        P.op("pe", lambda e: e.transpose(o, i, d), [in_, ident], [out])

    def ACT(out, in_, func, bias=None, scale=None):
        o, i = _ap(out), _ap(in_)
        kw = {}
        if bias is not None:
            kw["bias"] = _ap(bias)
        if scale is not None:
            kw["scale"] = _ap(scale)
        P.op("act", lambda e: e.activation(out=o, in_=i, func=func, **kw), [in_, bias, scale], [out])

    EO = {"dve": "dve", "pool": "pool"}

    def TT(eng, out, in0, in1, op):
        o, a, b = _ap(out), _ap(in0), _ap(in1)
        P.op(eng, lambda e: e.tensor_tensor(out=o, in0=a, in1=b, op=op), [in0, in1], [out])

    def TS(eng, out, in0, s1, op0, s2=None, op1=None):
        o, a, c1, c2 = _ap(out), _ap(in0), _ap(s1), _ap(s2)
        if op1 is None:
            P.op(eng, lambda e: e.tensor_scalar(out=o, in0=a, scalar1=c1, scalar2=None, op0=op0), [in0, s1], [out])
        else:
            P.op(eng, lambda e: e.tensor_scalar(out=o, in0=a, scalar1=c1, scalar2=c2, op0=op0, op1=op1),
                 [in0, s1, s2], [out])

    def STT(eng, out, in0, scalar, in1, op0, op1):
        o, a, s, b = _ap(out), _ap(in0), _ap(scalar), _ap(in1)
        P.op(eng, lambda e: e.scalar_tensor_tensor(out=o, in0=a, scalar=s, in1=b, op0=op0, op1=op1),
             [in0, scalar, in1], [out])

    def CP(eng, out, in_):
        o, i = _ap(out), _ap(in_)
        P.op(eng, lambda e: e.tensor_copy(out=o, in_=i), [in_], [out])

    def RECIP(out, in_):
        o, i = _ap(out), _ap(in_)
        P.op("dve", lambda e: e.reciprocal(out=o, in_=i), [in_], [out])

    def MEMSET(eng, out, val):
        o = _ap(out)
        P.op(eng, lambda e: e.memset(o, val), [], [out])

    def DMA(q, out, in_):
        P.dma(q, out, in_)

    cpt = [0]

    def EVAC(out, in_):
        cpt[0] += 1
        if cpt[0] % 2:
            ACT(out, in_, AF.Identity)
        else:
            CP("dve", out, in_)

    x = M.alloc([128, 8, T], F32)
    vecs = M.alloc([128, NV], F32)
    ident = M.alloc([128, 128], F32)
    identb = M.alloc([128, 128], BF16)
    ones_f = M.alloc([128, 128], F32)
    smask = M.alloc([128, 4, 256], BF16)
    modall = M.alloc([128, DEPTH * 48], F32)
    lv = M.alloc([128, 48], F32)
    esink = M.alloc([128, 16], F32)
    scb = M.alloc([128, 8], BF16)
    epsc = M.alloc([128, 1], F32)
    PERSIST_TOP = M.top

    for a0 in range(0, ARENA // 4, 8192):
        a1 = min(ARENA // 4, a0 + 8192)
        eng_ = "dve" if (a0 // 8192) % 2 == 0 else "pool"
        o_ = arena_h[:, a0:a1]
        P.op(eng_, (lambda o: (lambda e: e.memset(o, 0.0)))(o_), [], [arena_t])
    DMA("sp", vecs, vecs_d)
    DMA("sp", ident, ident_d)
    DMA("pool", identb, ident_d)
    DMA("pool", smask, smask_d)
    MEMSET("dve", ones_f, 1.0)
    MEMSET("dve", epsc, EPS)

    def vcol(c, n=1):
        return vecs[:, c:c + n]

    flag = vcol(V_FLAG)
    ctxb = vcol(V_CTXB)

    ACT(scb, vcol(V_CVEC, 8), AF.Silu)
    TOPB = (ARENA // 64) * 64
    o_ = TOPB
    top_tiles = {}
    for nm, shp in (("wdn0", [128, 22, 256]), ("wdn1", [128, 22, 256]), ("wup0", [128, 8, 512]), ("wup1", [128, 8, 512]),
                    ("ada0", [128, 8, 256]), ("ada1", [128, 8, 256])):
        nb_ = int(np.prod(shp[1:])) * 2
        o_ -= nb_
        top_tiles[nm] = M.alloc_at(o_, shp, BF16)
    TOP_LIMIT = o_
    wup = [top_tiles["wup0"], top_tiles["wup1"]]
    wdn = [top_tiles["wdn0"], top_tiles["wdn1"]]
    ada_slots = [top_tiles["ada0"], top_tiles["ada1"]]
    modps = P.register(Tile(ps_all[:, 7 * 512 + 300: 7 * 512 + 300 + DEPTH * 48], "ps", 7 * 2048, 8 * 2048))
    adak = [0]

    def mod_steps(l):
        aw = adaw_d[l].rearrange("(kt p) c -> p kt c", p=128)
        k0 = adak[0]
        adak[0] += 24

        def dma(ch):
            DMA("pool", ada_slots[(k0 + ch) % 2], aw[:, :, ch * 256:(ch + 1) * 256])

        def mm(ch):
            sl = ada_slots[(k0 + ch) % 2]
            for j in range(2):
                col = l * 48 + ch * 2 + j
                for kt in range(8):
                    MM(modps[:, col:col + 1], sl[:, kt, j * 128:(j + 1) * 128], scb[:, kt:kt + 1],
                       start=(kt == 0), stop=(kt == 7))
        steps = [lambda: (dma(0), dma(1))]
        for ch in range(24):
            def f(ch=ch):
                mm(ch)
                if ch + 2 < 24:
                    dma(ch + 2)
                if ch == 23:
                    o_, a_, b_ = modall[:, l * 48:(l + 1) * 48], modps[:, l * 48:(l + 1) * 48], vcol(l * NL + V_ADAB, 48)
                    P.op("dve", lambda e: e.tensor_tensor(out=o_.ap, in0=a_.ap, in1=b_.ap, op=ALU.add),
                         [a_, b_], [o_, PB[7]])
            steps.append(f)
        return steps

    def modulation(l):
        for f_ in mod_steps(l):
            f_()
    M.top = PERSIST_TOP
    xs = [M.alloc([128, D], F32) for _ in range(2)]
    for tb in range(16):
        st = xs[tb % 2]
        DMA("sp", st, x_d[tb * 128:(tb + 1) * 128, :])
        for hb in range(2):
            pb = PB[(tb * 2 + hb) % 4]
            for j in range(4):
                ft = hb * 4 + j
                TR(pb[:, j * 128:(j + 1) * 128], st[:, ft * 128:(ft + 1) * 128], ident)
            EVAC(x[:, hb * 4:(hb + 1) * 4, tb * 128:(tb + 1) * 128], pb.re("p (j n) -> p j n", j=4))

    modulation(0)
    for l in range(depth):
        ACT(esink[:, l * 4:(l + 1) * 4], vcol(l * NL + V_SINK, 4), AF.Exp)

    def rstd_from(ps_sum, out, nfeat):
        ACT(out, ps_sum, AF.Ln, bias=epsc, scale=1.0 / nfeat)
        ACT(out, out, AF.Exp, scale=-0.5)

    def norm_chunk(tok0, A, B, hdst, sq, tmp, rs, psn):
        for ft in range(8):
            if ft % 2 == 0:
                ACT(sq[ft % 2], x[:, ft, tok0:tok0 + 512], AF.Square)
            else:
                TT("dve", sq[ft % 2], x[:, ft, tok0:tok0 + 512], x[:, ft, tok0:tok0 + 512], ALU.mult)
            MM(psn, ones_f, sq[ft % 2], start=(ft == 0), stop=(ft == 7))
        rstd_from(psn, rs, float(D))
        for ft in range(8):
            TT("dve", tmp[ft % 2], x[:, ft, tok0:tok0 + 512], rs, ALU.mult)
            ACT(hdst(ft), tmp[ft % 2].re("p (s n) -> p s n", s=2), AF.Identity, bias=B[:, ft:ft + 1],
                scale=A[:, ft:ft + 1])

    def out_transposed(src, rows, r0, dst_d, tok0, ost, pbank):
        for b in range(4):
            TR(pbank[:, b * rows:(b + 1) * rows], src[r0:r0 + rows, b * 128:(b + 1) * 128], ident[r0:r0 + rows, r0:r0 + rows])
        EVAC(ost[:, :, 0:rows], pbank[:, 0:4 * rows].re("p (b f) -> p b f", b=4))
        DMA("sp", dst_d[tok0:tok0 + 512, :].rearrange("(b p) f -> p b f", p=128), ost[:, :, 0:rows])

    def layers():
      for l in range(depth):
        if stop == 'pro':
            raise _Stop
        vb = l * NL
        mod = modall[:, l * 48:(l + 1) * 48]
        B1, G1, G2 = mod[:, 0:8], mod[:, 16:24], mod[:, 40:48]
        A1, A2, A2f, B2f = lv[:, 0:8], lv[:, 8:16], lv[:, 16:24], lv[:, 24:32]
        B2 = mod[:, 24:32]
        STT("dve", A1, mod[:, 8:16], 1.0, vcol(vb + V_N1G, 8), ALU.add, ALU.mult)
        STT("dve", A2, mod[:, 32:40], 1.0, vcol(vb + V_N2G, 8), ALU.add, ALU.mult)
        TS("dve", A2f, A2, flag, ALU.mult)
        TS("dve", B2f, B2, flag, ALU.mult)

        M.top = PERSIST_TOP
        omix = M.alloc([128, 8, T], BF16)
        OMIX_TOP = M.top
        rs1 = M.alloc([128, T], F32)
        RS1_TOP = M.top
        cqn = M.alloc([128, 2, T], BF16)
        ckvT = M.alloc([128, NKEY], BF16)
        krT = M.alloc([128, NKEY], BF16)
        wuq = M.alloc([128, 2, 1536], BF16)
        wukv = M.alloc([128, 1024], BF16)
        cst = M.alloc([128, 4, 128], F32)
        cst2 = M.alloc([128, 4, 96], F32)
        A1OUT_TOP = M.top
        win1 = M.alloc([128, 8, 576], BF16)
        hb = [M.alloc([128, 8, 2, 258], BF16) for _ in range(2)]
        tabMc = M.alloc([128, 2, 512], F32)
        sq = [M.alloc([128, 512], F32) for _ in range(2)]
        tmp = [M.alloc([128, 512], F32) for _ in range(2)]
        rs = M.alloc([128, 512], F32)
        cqf = M.alloc([128, 2, 512], F32)
        stf = M.alloc([128, 512], F32)
        uu = M.alloc([128, 512], F32)
        vv = M.alloc([128, 512], F32)
        ost = [M.alloc([128, 4, 128], F32) for _ in range(2)]
        DMA("pool", win1, win_d[l].rearrange("(kt p) c -> p kt c", p=128)[:, :, 0:576])
        DMA("pool", wuq, wuq_d[l].rearrange("(kt p) c -> p kt c", p=128))
        DMA("pool", wukv, wukv_d[l])
        DMA("sp", cst, cckv_d[l].rearrange("(b p) f -> p b f", p=128))
        MEMSET("dve", cst2, 0.0)
        DMA("sp", cst2[:, :, 64:96], ckr_d[l].rearrange("(b p) f -> p b f", p=128))
        for b in range(4):
            TR(PB[7][:, b * 128:(b + 1) * 128], cst[:, b, :], ident)
        EVAC(ckvT[:, T:NKEY], PB[7])
        for b in range(4):
            TR(PB[6][0:96, b * 128:(b + 1) * 128], cst2[:, b, :], ident)
        EVAC(krT[64:96, T:NKEY], PB[6][64:96, :])

        def normc1(g_, c_):
            t0_ = g_ * 1024 + c_ * 512
            norm_chunk(t0_, A1, B1, lambda ft: hb[c_][:, ft, :, 1:257], sq, tmp, rs1[:, t0_:t0_ + 512], PB[4])
        normc1(0, 0)
        normc1(0, 1)
        for g in range(2):
            for c in range(2):
                tok0 = g * 1024 + c * 512
                rhs = lambda kt: hb[c][:, kt, :, 1:257]
                for j in range(2):
                    pb = PB[j]
                    for kt in range(8):
                        MM(pb, win1[:, kt, C_CQ + j * 128:C_CQ + (j + 1) * 128], rhs(kt), kt == 0, kt == 7)
                    CP("dve", cqf[:, j, :], pb)
                    ACT(sq[j], cqf[:, j, :], AF.Square)
                for j in range(2):
                    MM(PB[5], ones_f, sq[j], j == 0, j == 1)
                rstd_from(PB[5], rs, 256.0)
                for j in range(2):
                    STT("dve", cqn[:, j, tok0:tok0 + 512], cqf[:, j, :], vcol(vb + V_QNG + j), rs, ALU.mult, ALU.mult)
                if stop in ('A1b', 'cnt'):
                    print('count at A1b', P.count)
                if stop == 'A1b':
                    raise _Stop
                pb = PB[2]
                for kt in range(8):
                    MM(pb, win1[:, kt, C_CKV:C_CKV + 128], rhs(kt), kt == 0, kt == 7)
                ACT(sq[0], pb, AF.Square)
                MM(PB[5], ones_f, sq[0], True, True)
                rstd_from(PB[5], rs, 128.0)
                STT("dve", stf, pb, vcol(vb + V_KVG), rs, ALU.mult, ALU.mult)
                CP("pool", ckvT[:, tok0:tok0 + 512], stf)
                out_transposed(stf, 128, 0, sckv_d[l], tok0, ost[0], PB[6])
                if stop in ('A1c', 'cnt'):
                    print('count at A1c', P.count)
                if stop == 'A1c':
                    raise _Stop
                pa, pbb = PB[3], PB[7]
                for kt in range(8):
                    MM(pa[0:96, :], win1[:, kt, C_KRT:C_KRT + 96], rhs(kt), kt == 0, kt == 7)
                for kt in range(8):
                    MM(pbb[0:96, :], win1[:, kt, C_KRPT:C_KRPT + 96], rhs(kt), kt == 0, kt == 7)
                DMA("sp", tabMc[64:96], ropeM_d[:, :, tok0:tok0 + 512])
                CP("dve", stf[64:96, :], pa[64:96, :])
                TT("dve", uu[64:96, :], pa[64:96, :], tabMc[64:96, 0, :], ALU.mult)
                TT("dve", vv[64:96, :], pbb[64:96, :], tabMc[64:96, 1, :], ALU.mult)
                TT("pool", krT[64:96, tok0:tok0 + 512], uu[64:96, :], vv[64:96, :], ALU.add)
                out_transposed(stf, 32, 64, skr_d[l], tok0, ost[1], PB[6])
                if g == 0:
                    normc1(1, c)

        if stop in ('A1', 'cnt'):
            print('count at A1', P.count)
        if stop == 'A1':
            raise _Stop
        M.top = A1OUT_TOP
        tabM = M.alloc([128, 2, T], F32)
        Qs_ = [M.alloc([128, T], BF16) for _ in range(2)]
        Ks_ = [M.alloc([128, NKEY], BF16) for _ in range(2)]
        Vs_ = [M.alloc([128, NKT, 128], BF16) for _ in range(2)]
        Pb = [M.alloc([128, 512], BF16) for _ in range(3)]
        Rb = M.alloc([128, 512], F32)
        uu = M.alloc([128, 512], F32)
        vv = M.alloc([128, 512], F32)
        DMA("sp", tabM[64:96], ropeM_d)
        for s in range(2):
            MEMSET("pool", Qs_[s][96:128, :], 0.0)
            MEMSET("pool", Ks_[s][96:128, :], 0.0)
            DMA("pool", Qs_[s][96:105, :], qaug_d)
            DMA("pool", Ks_[s][96:105, :], kaug_d)
            MEMSET("pool", Vs_[s][:, :, 64:128], 1.0)
        def build_steps(h):
            s_ = h % 2
            Qh, Kh, Vh = Qs_[s_], Ks_[s_], Vs_[s_]
            steps = []
            for kc in range(5):
                def f(kc=kc):
                    pb = PB[5 + kc % 2]
                    MM(pb[0:64, :], wukv[:, h * 128:h * 128 + 64], ckvT[:, kc * 512:(kc + 1) * 512])
                    CP("dve", Kh[0:64, kc * 512:(kc + 1) * 512], pb[0:64, :])
                steps.append(f)
            steps.append(lambda: CP("pool", Kh[64:96, :], krT[64:96, :]))
            for k0 in range(0, NKT, 8):
                def f(k0=k0):
                    n = min(8, NKT - k0)
                    pb = PB[7]
                    for j in range(n):
                        MM(pb[:, j * 64:(j + 1) * 64], ckvT[:, (k0 + j) * 128:(k0 + j + 1) * 128],
                           wukv[:, h * 128 + 64:h * 128 + 128])
                    CP("dve", Vh[:, k0:k0 + n, 0:64], pb[:, 0:n * 64].re("p (j d) -> p j d", j=n))
                steps.append(f)
            for c in range(4):
                def f(c=c):
                    pa, pbb = PB[5], PB[6]
                    for kt in range(2):
                        MM(pa[0:96, :], wuq[:, kt, h * 192:h * 192 + 96], cqn[:, kt, c * 512:(c + 1) * 512], kt == 0, kt == 1)
                    for kt in range(2):
                        MM(pbb[0:96, :], wuq[:, kt, h * 192 + 96:h * 192 + 192], cqn[:, kt, c * 512:(c + 1) * 512], kt == 0, kt == 1)
                    CP("dve", Qh[0:64, c * 512:(c + 1) * 512], pa[0:64, :])
                    TT("dve", uu[64:96, :], pa[64:96, :], tabM[64:96, 0, c * 512:(c + 1) * 512], ALU.mult)
                    TT("dve", vv[64:96, :], pbb[64:96, :], tabM[64:96, 1, c * 512:(c + 1) * 512], ALU.mult)
                    TT("pool", Qh[64:96, c * 512:(c + 1) * 512], uu[64:96, :], vv[64:96, :], ALU.add)
                steps.append(f)
            return steps

        for f_ in build_steps(0):
            f_()
        for h in range(8):
            s = h % 2
            Qh, Kh, Vh = Qs_[s], Ks_[s], Vs_[s]
            pending = build_steps(h + 1) if h + 1 < 8 else []
            it = 0
            for c in range(4):
                po = PB[3 + (h * 4 + c) % 2]
                qv = Qh[:, c * 512:(c + 1) * 512]

                def score(kt):
                    MM(PB[kt % 3], Kh[:, kt * 128:(kt + 1) * 128], qv)
                score(0)
                score(1)
                for kt in range(NKT):
                    ACT(Pb[kt % 3], PB[kt % 3], AF.Exp, scale=MLA_SCALE)
                    if kt + 2 < NKT:
                        score(kt + 2)
                    MM(po, Vh[:, kt, :], Pb[kt % 3], kt == 0, kt == NKT - 1)
                    it += 1
                    if pending and it % 5 == 0:
                        pending.pop(0)()
                RECIP(Rb[64:128, :], po[64:128, :])
                p0 = (h % 2) * 64
                TT("dve", omix[p0:p0 + 64, h // 2, c * 512:(c + 1) * 512], po[0:64, :], Rb[64:128, :], ALU.mult)
            while pending:
                pending.pop(0)()

        if stop == 'BM':
            raise _Stop
        M.top = RS1_TOP
        qsw = M.alloc([128, 2, T], BF16)
        ksd = M.alloc([128, 2, NKEY], BF16)
        vsx = M.alloc([128, NKT, 2, 128], BF16)
        ypad = M.alloc([128, 2, 8, 286], BF16)
        A2OUT_TOP = M.top
        win2 = M.alloc([128, 8, 1152], BF16)
        hb = [M.alloc([128, 8, 2, 258], BF16) for _ in range(2)]
        tabS = [M.alloc([128, 2, 512], F32) for _ in range(1)]
        tmp = [M.alloc([128, 512], F32) for _ in range(2)]
        stf = M.alloc([128, 512], F32)
        uu = M.alloc([128, 512], F32)
        vv = M.alloc([128, 512], F32)
        ost = [M.alloc([128, 4, 128], F32) for _ in range(2)]
        winv = win_d[l].rearrange("(kt p) c -> p kt c", p=128)
        MEMSET("pool", vsx[:, :, :, 64:128], 1.0)
        MEMSET("pool", ypad[:, :, 0, 0:15], 0.0)
        MEMSET("pool", ypad[:, :, 7, 271:286], 0.0)
        tsi = 0
        def normc2(g_, c_):
            t0_ = g_ * 1024 + c_ * 512
            for ft in range(8):
                TT("dve", tmp[ft % 2], x[:, ft, t0_:t0_ + 512], rs1[:, t0_:t0_ + 512], ALU.mult)
                ACT(hb[c_][:, ft, :, 1:257], tmp[ft % 2].re("p (s n) -> p s n", s=2), AF.Identity,
                    bias=B1[:, ft:ft + 1], scale=A1[:, ft:ft + 1])
        normc2(0, 0)
        normc2(0, 1)
        for g in range(2):
            DMA("pool", win2, winv[:, :, C_QS:C_QS + 1152])
            for c in range(2):
                tok0 = g * 1024 + c * 512
                cb = g * 2 + c
                rhs = lambda kt: hb[c][:, kt, :, 1:257]
                tb_ = tabS[0]
                tsi += 1
                DMA("sp", tb_, ropeS_d[:, :, tok0:tok0 + 512])

                rtn = [0]

                def rope_tile(ca, cbb, dst, keep=None):
                    pa, pbb = (PB[0], PB[1]) if rtn[0] % 2 == 0 else (PB[2], PB[3])
                    rtn[0] += 1
                    for kt in range(8):
                        MM(pa, win2[:, kt, ca - C_QS:ca - C_QS + 128], rhs(kt), kt == 0, kt == 7)
                    for kt in range(8):
                        MM(pbb, win2[:, kt, cbb - C_QS:cbb - C_QS + 128], rhs(kt), kt == 0, kt == 7)
                    if keep is not None:
                        CP("dve", stf[keep:keep + 64, :], pa[keep:keep + 64, :])
                    u_, v_ = (uu, vv) if rtn[0] % 2 == 0 else (tmp[0], tmp[1])
                    TT("dve", u_, pa, tb_[:, 0, :], ALU.mult)
                    TT("dve", v_, pbb, tb_[:, 1, :], ALU.mult)
                    TT("pool", dst, u_, v_, ALU.add)
                rope_tile(C_QS, C_QSP, qsw[:, 0, tok0:tok0 + 512])
                rope_tile(C_QS + 128, C_QSP + 128, qsw[:, 1, tok0:tok0 + 512])
                rope_tile(C_K0, C_K0P, ksd[:, 0, tok0:tok0 + 512], keep=0)
                rope_tile(C_K1, C_K1P, ksd[:, 1, tok0:tok0 + 512], keep=64)
                out_transposed(stf, 128, 0, sk_d[l], tok0, ost[0], PB[6])
            DMA("pool", win2[:, :, 0:640], winv[:, :, C_VS:C_VS + 640])
            for c in range(2):
                tok0 = g * 1024 + c * 512
                rhs = lambda kt: hb[c][:, kt, :, 1:257]
                pb = PB[2]
                for kt in range(8):
                    MM(pb, win2[:, kt, 0:128], rhs(kt), kt == 0, kt == 7)
                CP("dve", stf, pb)
                for b in range(4):
                    TR(PB[6][:, b * 128:(b + 1) * 128], stf[:, b * 128:(b + 1) * 128], ident)
                ACT(ost[1], PB[6].re("p (b f) -> p b f", b=4), AF.Identity)
                DMA("sp", sv_d[l][tok0:tok0 + 512, :].rearrange("(b p) f -> p b f", p=128), ost[1])
                kt0 = tok0 // 128
                CP("dve", vsx[:, kt0:kt0 + 4, :, 0:64], ost[1].re("p b (k d) -> p b k d", k=2))
                for j in range(2):
                    pu, pg = (PB[0], PB[1]) if j == 0 else (PB[3], PB[4])
                    for kt in range(8):
                        MM(pu, win2[:, kt, 128 + j * 128:256 + j * 128], rhs(kt), kt == 0, kt == 7)
                    for kt in range(8):
                        MM(pg, win2[:, kt, 384 + j * 128:512 + j * 128], rhs(kt), kt == 0, kt == 7)
                    ACT(uu, pg, AF.Sigmoid)
                    sg0 = tok0 // 256
                    TT("dve", ypad[:, j, sg0:sg0 + 2, 15:271], pu.re("p (s n) -> p s n", s=2),
                       uu.re("p (s n) -> p s n", s=2), ALU.mult)
                if g == 0:
                    normc2(1, c)

        if stop == 'A2':
            raise _Stop
        M.top = A2OUT_TOP
        wout = M.alloc_at(TOPB - 16384, [128, 8, D], BF16)
        DMA("pool", wout, wout_d[l].rearrange("(kt p) c -> p kt c", p=128))
        cst = M.alloc([128, 4, 2, 2, 64], F32)
        P2 = [M.alloc([128, 256], BF16) for _ in range(3)]
        R2 = M.alloc([128, 256], F32)
        qzs = [M.alloc([128, 2, 128], BF16) for _ in range(2)]
        for q_ in qzs:
            MEMSET("pool", q_, 0.0)
        diag_t = [M.alloc([128, 128], BF16) for _ in range(62)]
        pw2 = M.alloc([128, 2, 256], BF16)
        ycv = M.alloc([128, 2, 512], F32)
        sqc = M.alloc([128, 1, 512], F32)
        mean = M.alloc([128, 512], F32)
        var = M.alloc([128, 512], F32)
        dd = M.alloc([128, 512], F32)
        yact = M.alloc([128, 2, 512], BF16)
        ckv_ = ck_d[l].rearrange("(b p) (k d) -> p b k d", p=128, k=2)
        for du in range(2):
            for kh in range(2):
                DMA("sp", cst[:, :, kh, du, :], ckv_[:, :, kh, :])
        for kh in range(2):
            for b in range(4):
                TR(PB[7][:, b * 128:(b + 1) * 128], cst[:, b, kh].re("p u d -> p (u d)"), ident)
            EVAC(ksd[:, kh, T:NKEY], PB[7])
        for kh in range(2):
            DMA("pool", vsx[:, 16:20, kh, 0:64], cv_d[l].rearrange("(b p) (k d) -> p b k d", p=128, k=2)[:, :, kh, :])
        DMA("pool", pw2, pw2_d[l].rearrange("(kt p) c -> p kt c", p=128))
        for tap in range(31):
            for ct in range(2):
                if (tap + ct) % 2 == 0:
                    ACT(diag_t[tap * 2 + ct], identb, AF.Identity, scale=vcol(vb + V_CDW + tap * 2 + ct))
                else:
                    TS("dve", diag_t[tap * 2 + ct], identb, vcol(vb + V_CDW + tap * 2 + ct), ALU.mult)
        if stop == 'cnt':
            print('count at BS-prep-end', P.count)
        si = 0
        for kh in range(2):
            for i in range(16):
                if stop == 'cnt' and kh == 0 and i < 2:
                    print('count at swa block', i, P.count)
                keys = []
                if i > 0:
                    keys.append((i - 1, (i % 2) * 2 + 0, False))
                keys.append((i, None, False))
                if i < 15:
                    keys.append((i + 1, (i % 2) * 2 + 1, False))
                for cb in range(4):
                    keys.append((16 + cb, None, True))
                po = PB[3 + (kh * 16 + i) % 2]
                nk = len(keys)

                qz = qzs[(kh * 16 + i) % 2]
                CP("pool", qz[0:64, 0, :], qsw[0:64, kh, i * 128:(i + 1) * 128])
                CP("pool", qz[64:128, 1, :], qsw[64:128, kh, i * 128:(i + 1) * 128])

                def score(n, sb):
                    kt, m, isctx = keys[n]
                    MM(sb, ksd[:, kh, kt * 128:(kt + 1) * 128], qz.re("p g n -> p (g n)"), True, m is None)
                    if m is not None:
                        MM(sb, identb, smask[:, m, :], False, True)
                sbs = [PB[0][:, 0:256], PB[1][:, 0:256], PB[2][:, 0:256]]
                score(0, sbs[si % 3])
                score(1, sbs[(si + 1) % 3])
                for n in range(nk):
                    kt, m, isctx = keys[n]
                    if isctx:
                        ACT(P2[(si + n) % 3], sbs[(si + n) % 3], AF.Exp, bias=ctxb, scale=SWA_SCALE)
                    else:
                        ACT(P2[(si + n) % 3], sbs[(si + n) % 3], AF.Exp, scale=SWA_SCALE)
                    if n + 2 < nk:
                        score(n + 2, sbs[(si + n + 2) % 3])
                    MM(po[:, 0:256], vsx[:, kt, kh, :], P2[(si + n) % 3], n == 0, n == nk - 1)
                si += nk
                for g in range(2):
                    TS("dve", R2[64:128, g * 128:(g + 1) * 128], po[64:128, g * 128:(g + 1) * 128],
                       esink[64:128, l * 4 + kh * 2 + g:l * 4 + kh * 2 + g + 1], ALU.add)
                RECIP(R2[64:128, :], R2[64:128, :])
                for g in range(2):
                    TT("dve", omix[g * 64:(g + 1) * 64, 4 + kh, i * 128:(i + 1) * 128],
                       po[0:64, g * 128:(g + 1) * 128], R2[64:128, g * 128:(g + 1) * 128], ALU.mult)
        if stop == 'cnt':
            print('count at swa-end', P.count)
        for ct in range(2):
            TS("pool", ypad[:, ct, 1:8, 0:15], ypad[:, ct, 0:7, 256:271], flag, ALU.mult)
            TS("pool", ypad[:, ct, 0:7, 271:286], ypad[:, ct, 1:8, 15:30], flag, ALU.mult)
        for sp_ in range(4):
            for ct in range(2):
                pc = PB[(5 if sp_ % 2 == 0 else 3) + ct]
                for tap in range(31):
                    MM(pc, diag_t[tap * 2 + ct], ypad[:, ct, 2 * sp_:2 * sp_ + 2, tap:tap + 256], tap == 0, tap == 30)
                ACT(ycv[:, ct, :], pc, AF.Identity, bias=vcol(vb + V_CDB + ct))
            for ct in range(2):
                MM(PB[0], ones_f, ycv[:, ct, :], ct == 0, ct == 1)
            for ct in range(2):
                ACT(sqc[:, 0, :], ycv[:, ct, :], AF.Square)
                MM(PB[1], ones_f, sqc[:, 0, :], ct == 0, ct == 1)
            TS("dve", mean, PB[0], 1.0 / 256.0, ALU.mult)
            TT("dve", dd, mean, mean, ALU.mult)
            STT("dve", var, PB[1], 1.0 / 256.0, dd, ALU.mult, ALU.subtract)
            ACT(var, var, AF.Ln, bias=epsc)
            ACT(var, var, AF.Exp, scale=-0.5)
            for ct in range(2):
                TT("dve", dd, ycv[:, ct, :], mean, ALU.subtract)
                TT("dve", dd, dd, var, ALU.mult)
                ACT(yact[:, ct, :], dd, AF.Silu, bias=vcol(vb + V_CLB + ct), scale=vcol(vb + V_CLG + ct))
            for co in range(2):
                pp = PB[2]
                for ct in range(2):
                    MM(pp, pw2[:, ct, co * 128:(co + 1) * 128], yact[:, ct, :], ct == 0, ct == 1)
                EVAC(omix[:, 6 + co, sp_ * 512:(sp_ + 1) * 512], pp)

        if stop == 'cnt':
            print('count at BS-end', P.count)
        if stop == 'BS':
            raise _Stop
        M.top = OMIX_TOP
        wupv = wup_d[l].rearrange("(kt p) c -> p kt c", p=128)
        wdnv = wdn_d[l].rearrange("(kt p) c -> p kt c", p=128)
        upseq = [(g_, j_) for g_ in range(2) for j_ in range(11)]
        dnseq = [(g_, q_) for g_ in range(2) for q_ in range(4)]
        upn = [0]
        dnn = [0]

        def up_dma():
            if upn[0] < len(upseq):
                g_, j_ = upseq[upn[0]]
                ws_ = wup[upn[0] % 2]
                upn[0] += 1
                DMA("pool", ws_[:, :, 0:256], wupv[:, :, j_ * 256:(j_ + 1) * 256])
                DMA("pool", ws_[:, :, 256:512], wupv[:, :, DFF + j_ * 256:DFF + (j_ + 1) * 256])

        def dn_dma():
            if dnn[0] < len(dnseq):
                g_, q_ = dnseq[dnn[0]]
                ws_ = wdn[dnn[0] % 2]
                dnn[0] += 1
                DMA("pool", ws_, wdnv[:, :, q_ * 256:(q_ + 1) * 256])
        up_dma()
        up_dma()
        modq = mod_steps(l + 1) if l + 1 < depth else []
        if modq:
            modq.pop(0)()
        n = 0
        rsall = M.alloc_at(PERSIST_TOP + 16512 + 45056, [128, T], F32)
        sqc_ = [M.alloc([128, 512], F32) for _ in range(2)]
        for c in range(4):
            for dt in range(8):
                pb = PB[n % 4]
                n += 1
                for kt in range(8):
                    MM(pb, wout[:, kt, dt * 128:(dt + 1) * 128], omix[:, kt, c * 512:(c + 1) * 512], kt == 0, kt == 7)
                STT("dve", x[:, dt, c * 512:(c + 1) * 512], pb, G1[:, dt:dt + 1], x[:, dt, c * 512:(c + 1) * 512],
                    ALU.mult, ALU.add)
            for ft in range(8):
                if ft % 2 == 0:
                    ACT(sqc_[0], x[:, ft, c * 512:(c + 1) * 512], AF.Square)
                else:
                    TT("pool", sqc_[1], x[:, ft, c * 512:(c + 1) * 512], x[:, ft, c * 512:(c + 1) * 512], ALU.mult)
                MM(PB[4 + c % 2], ones_f, sqc_[ft % 2], ft == 0, ft == 7)
            rstd_from(PB[4 + c % 2], rsall[:, c * 512:(c + 1) * 512], float(D))

        dn_dma()
        dn_dma()
        if stop == 'C':
            raise _Stop
        M.top = PERSIST_TOP
        hbuf = M.alloc([128, 8, 4, 258], BF16)
        pbuf = M.alloc([128, 22, 1024], BF16)
        assert M.top == PERSIST_TOP + 16512 + 45056
        M.top += 8192
        a1us = [M.alloc([128, 4, 256], F32) for _ in range(2)]
        a1gs = [M.alloc([128, 4, 256], F32) for _ in range(2)]
        sq = [M.alloc([128, 512], F32)] * 2
        tmp = sq
        ht = M.alloc([128, 8], F32)
        hsave = M.alloc([128, 16], F32)
        assert M.top <= TOP_LIMIT, M.top
        for hi_, t in enumerate((1023, 1024)):
            TT("dve", ht, x[:, :, t], V(rsall.ap[:, t:t + 1].to_broadcast([128, 8]), rsall), ALU.mult)
            TT("dve", ht, ht, A2f, ALU.mult)
            TT("dve", hsave[:, hi_ * 8:(hi_ + 1) * 8], ht, B2f, ALU.add)
        wi = 0
        di = 0
        def norm_apply(g):
            for c in range(2):
                tok0 = g * 1024 + c * 512
                for ft in range(8):
                    TT("dve", tmp[ft % 2], x[:, ft, tok0:tok0 + 512], rsall[:, tok0:tok0 + 512], ALU.mult)
                    ACT(hbuf[:, ft, 2 * c:2 * c + 2, 1:257], tmp[ft % 2].re("p (s n) -> p s n", s=2), AF.Identity,
                        bias=B2[:, ft:ft + 1], scale=A2[:, ft:ft + 1])
            TS("pool", hbuf[:, :, 1:4, 0:1], hbuf[:, :, 0:3, 256:257], flag, ALU.mult)
            TS("pool", hbuf[:, :, 0:3, 257:258], hbuf[:, :, 1:4, 1:2], flag, ALU.mult)
            if g == 0:
                MEMSET("pool", hbuf[:, :, 0, 0:1], 0.0)
                CP("dve", hbuf[:, :, 3, 257], hsave[:, 8:16])
            else:
                CP("dve", hbuf[:, :, 0, 0], hsave[:, 0:8])
                MEMSET("pool", hbuf[:, :, 3, 257:258], 0.0)
        norm_apply(0)
        for g in range(2):
            for j in range(11):
                ws = wup[wi % 2]
                wi += 1
                if wi >= 2:
                    pass
                for jj in range(2):
                    f = 2 * j + jj
                    a1u, a1g = a1us[f % 2], a1gs[f % 2]
                    for s in range(4):
                        for kt in range(8):
                            MM(PBN[s], ws[:, kt, jj * 128:(jj + 1) * 128], hbuf[:, kt, s, :], kt == 0, kt == 7)
                    if modq:
                        modq.pop(0)()
                    for s in range(4):
                        for kt in range(8):
                            MM(PBN[4 + s], ws[:, kt, 256 + jj * 128:256 + (jj + 1) * 128], hbuf[:, kt, s, :],
                               kt == 0, kt == 7)
                    for (ps_, a1, fc) in ((PS03, a1u, f), (PS47, a1g, 22 + f)):
                        ACT(a1, ps_[:, :, 1:257], AF.Identity, bias=vcol(vb + V_FDB + fc),
                            scale=vcol(vb + V_FDW + 44 + fc))
                        STT("dve", a1, ps_[:, :, 0:256], vcol(vb + V_FDW + fc), a1, ALU.mult, ALU.add)
                        STT("dve", a1, ps_[:, :, 2:258], vcol(vb + V_FDW + 88 + fc), a1, ALU.mult, ALU.add)
                    ACT(a1g, a1g, AF.Silu)
                    if jj == 1:
                        up_dma()
                    TT("pool", pbuf[:, f, :].re("p (s n) -> p s n", s=4), a1u, a1g, ALU.mult)
            if g == 0:
                norm_apply(1)
            for dq in range(4):
                ws = wdn[di % 2]
                di += 1
                for dd_ in range(2):
                    dt = 2 * dq + dd_
                    for c in range(2):
                        pb = PB[(dt * 2 + c) % 4]
                        tok0 = g * 1024 + c * 512
                        for f in range(22):
                            MM(pb, ws[:, f, dd_ * 128:(dd_ + 1) * 128], pbuf[:, f, c * 512:(c + 1) * 512], f == 0, f == 21)
                        STT("dve", x[:, dt, tok0:tok0 + 512], pb, G2[:, dt:dt + 1], x[:, dt, tok0:tok0 + 512],
                            ALU.mult, ALU.add)
                dn_dma()
            if g == 1:
                while modq:
                    modq.pop(0)()

    try:
        layers()
    except _Stop:
        P.limit = None

    M.top = PERSIST_TOP
    sq = [M.alloc([128, 512], F32) for _ in range(2)]
    rs = M.alloc([128, 512], F32)
    yfc = M.alloc([128, 8, 512], F32)
    yst = [M.alloc([128, D], F32) for _ in range(2)]
    oi = 0
    for c in range(4):
        for ft in range(8):
            ACT(sq[ft % 2], x[:, ft, c * 512:(c + 1) * 512], AF.Square)
            MM(PB[4], ones_f, sq[ft % 2], ft == 0, ft == 7)
        rstd_from(PB[4], rs, float(D))
        for ft in range(8):
            STT("dve", yfc[:, ft, :], x[:, ft, c * 512:(c + 1) * 512], vcol(V_FNG + ft), rs, ALU.mult, ALU.mult)
        for b in range(4):
            st = yst[oi % 2]
            oi += 1
            for hb in range(2):
                pb = PB[(b * 2 + hb) % 4]
                for j in range(4):
                    TR(pb[:, j * 128:(j + 1) * 128], yfc[:, hb * 4 + j, b * 128:(b + 1) * 128], ident)
                EVAC(st[:, hb * 512:(hb + 1) * 512], pb)
            tok = c * 512 + b * 128
            DMA("sp", y_d[tok:tok + 128, :], st)

    P.emit(stack)
    stack.close()
    return nc


def _rope_tables(sample):
    t = np.arange(T)
    row = (t // 64).astype(np.float32)
    col = (t % 64).astype(np.float32)

    def tab(nd, blk):
        qd = blk
        freqs = (10000.0 ** (-np.arange(qd, dtype=np.float32) / qd)).astype(np.float32)
        cos = np.ones((nd, T), np.float32)
        sin = np.zeros((nd, T), np.float32)
        if sample:
            for i in range(nd):
                b, j = i // blk, i % blk
                pos = row if b < 2 else col
                ang = (pos * freqs[j]).astype(np.float32)
                cos[i] = np.cos(ang)
                sin[i] = np.sin(ang) * (-1.0 if b % 2 == 0 else 1.0)
        return cos, sin
    cs, ss = tab(64, 16)
    cm, sm = tab(32, 8)
    ropeS = np.zeros((128, 2, T), np.float32)
    ropeS[0:64, 0], ropeS[64:128, 0] = cs, cs
    ropeS[0:64, 1], ropeS[64:128, 1] = ss, ss
    ropeM = np.stack([cm, sm], axis=1).astype(np.float32)
    return ropeS, ropeM


def _perm(n, blk):
    idx = np.arange(n)
    b = idx // blk
    return np.where(b % 2 == 0, idx + blk, idx - blk)


def _host_prep(inp):
    f = lambda a: np.ascontiguousarray(np.asarray(a, dtype=np.float32))
    w_in = f(inp["w_in"])
    pS = _perm(64, 16)
    pM = _perm(32, 8)
    cols = []
    cols += list(range(0, 256))
    cols += list(range(256, 384))
    cols += list(range(320, 384)) + list(range(384, 416))
    cols += list(range(320, 384)) + [384 + int(p) for p in pM]
    qs = np.arange(416, 672)
    cols += list(qs)
    cols += [416 + (i // 64) * 64 + int(pS[i % 64]) for i in range(256)]
    for kh in range(2):
        k0 = 672 + kh * 64
        cols += list(range(k0, k0 + 64)) * 2
        cols += [k0 + int(p) for p in pS] * 2
    cols += list(range(800, 928))
    cols += list(range(928, 1440))
    assert len(cols) == NWIN
    w_in_ext = np.ascontiguousarray(w_in[:, :, np.array(cols)])
    w_uq = f(inp["mla_w_uq"])
    ucols = []
    for h in range(8):
        b0 = h * 96
        ucols += list(range(b0, b0 + 96))
        ucols += list(range(b0, b0 + 64)) + [b0 + 64 + int(p) for p in pM]
    w_uq_ext = np.ascontiguousarray(w_uq[:, :, np.array(ucols)])

    def vec_layers():
        out = np.zeros((128, DEPTH, NL), np.float32)
        for l in range(DEPTH):
            out[:, l, V_ADAB:V_ADAB + 48] = f(inp["ada_b"])[l].reshape(48, 128).T
            out[:, l, V_N1G:V_N1G + 8] = f(inp["norm1_g"])[l].reshape(8, 128).T
            out[:, l, V_N2G:V_N2G + 8] = f(inp["norm2_g"])[l].reshape(8, 128).T
            out[:, l, V_QNG:V_QNG + 2] = f(inp["mla_q_norm_g"])[l].reshape(2, 128).T
            out[:, l, V_KVG] = f(inp["mla_kv_norm_g"])[l]
            out[:, l, V_CDW:V_CDW + 62] = f(inp["conv_dw_w"])[l].reshape(31, 2, 128).transpose(2, 0, 1).reshape(128, 62)
            out[:, l, V_CDB:V_CDB + 2] = f(inp["conv_dw_b"])[l].reshape(2, 128).T
            out[:, l, V_CLG:V_CLG + 2] = f(inp["conv_ln_g"])[l].reshape(2, 128).T
            out[:, l, V_CLB:V_CLB + 2] = f(inp["conv_ln_b"])[l].reshape(2, 128).T
            out[:, l, V_FDW:V_FDW + 132] = f(inp["ffn_dw_w"])[l].reshape(3, 44, 128).transpose(2, 0, 1).reshape(128, 132)
            out[:, l, V_FDB:V_FDB + 44] = f(inp["ffn_dw_b"])[l].reshape(44, 128).T
            out[:, l, V_SINK:V_SINK + 4] = f(inp["swa_sink"])[l][None, :]
        return out.reshape(128, DEPTH * NL)
    vl = vec_layers()
    fng = f(inp["final_norm_g"]).reshape(8, 128).T

    shared = dict(
        ident=np.eye(128, dtype=np.float32),
        ada_w=f(inp["ada_w"]), w_in_ext=w_in_ext, w_uq_ext=w_uq_ext, w_ukv=f(inp["mla_w_ukv"]),
        pw2=f(inp["conv_w_pw2"]), w_out=f(inp["w_out"]), w_up=f(inp["ffn_w_up"]), w_down=f(inp["ffn_w_down"]),
    )
    kaug = np.zeros((9, NKEY), np.float32)
    for j in range(8):
        kaug[j, j * 256:(j + 1) * 256] = 1.0
    kaug[8, :] = -BIGM
    kl = np.arange(128)[:, None]
    ql = np.arange(128)[None, :]

    def core_inputs(sample, xtok, cvec, caches):
        vecs = np.zeros((128, NV), np.float32)
        vecs[:, :DEPTH * NL] = vl
        vecs[:, V_FNG:V_FNG + 8] = fng
        vecs[:, V_CVEC:V_CVEC + 8] = cvec.reshape(8, 128).T
        vecs[:, V_FLAG] = 1.0 if sample else 0.0
        vecs[:, V_CTXB] = 0.0 if sample else NEG
        ropeS, ropeM = _rope_tables(sample)
        qaug = np.zeros((9, T), np.float32)
        smask = np.zeros((128, 4, 128), np.float32)
        if sample:
            lo = np.where(ql <= kl, 0.0, NEG).astype(np.float32)
            hi = np.where(kl <= ql, 0.0, NEG).astype(np.float32)
            smask[:, 0], smask[:, 1], smask[:, 2], smask[:, 3] = lo, hi, lo, hi
        else:
            for j in range(8):
                qaug[j, j * 256:(j + 1) * 256] = BIGM
            qaug[8, :] = 1.0
            smask[:, 0] = NEG
            smask[:, 3] = NEG
        smask = np.ascontiguousarray(np.concatenate([smask, smask], axis=2))
        d = dict(x=np.ascontiguousarray(xtok), vecs=vecs, ropeS=ropeS, ropeM=ropeM, qaug=qaug, kaug=kaug,
                 smask=smask, cckv=caches[0], ckr=caches[1], ck=caches[2], cv=caches[3])
        d.update(shared)
        return d

    xp = f(inp["x_prompt"])
    xs = f(inp["x_sample"])
    cc = [f(inp["cache_mla_ckv"]), f(inp["cache_mla_krope"]),
          f(inp["cache_swa_k"]).reshape(2, DEPTH, L, 128), f(inp["cache_swa_v"]).reshape(2, DEPTH, L, 128)]
    zc = [np.zeros((DEPTH, L, 128), np.float32), np.zeros((DEPTH, L, 32), np.float32),
          np.zeros((DEPTH, L, 128), np.float32), np.zeros((DEPTH, L, 128), np.float32)]
    c = f(inp["c"])
    cctx = f(inp["c_ctx"])
    maps = []
    for b in range(2):
        maps.append(core_inputs(True, xs[b], c[b], [a[b] for a in cc]))
    for q in range(4):
        maps.append(core_inputs(False, xp[q * 8:(q + 1) * 8].reshape(T, D), cctx, zc))
    maps.append(maps[2])
    maps.append(maps[3])
    return maps


_NC_CACHE = {}
_TEST_CORES = None


def kernel(**inputs):
    maps = _host_prep(inputs)
    if "nc" not in _NC_CACHE:
        _NC_CACHE["nc"] = build_program(DEPTH)
    nc = _NC_CACHE["nc"]
    if _TEST_CORES is not None:
        sub = [maps[i] for i in _TEST_CORES]
        rr = run_bass_kernel_spmd(nc, sub, core_ids=list(range(len(sub)))).results
        r = [rr[_TEST_CORES.index(i)] if i in _TEST_CORES else rr[0 if i < 2 else len(sub) - 1] for i in range(8)]
    else:
        res = run_bass_kernel_spmd(nc, maps, core_ids=list(range(8)))
        r = res.results
    y_sample = np.stack([r[0]["y"], r[1]["y"]], axis=0).astype(np.float32)
    y_prompt = np.concatenate([r[2 + q]["y"].reshape(8, 256, D) for q in range(4)], axis=0).astype(np.float32)

    def st(name, w):
        parts = []
        for q in range(4):
            a = r[2 + q][name].reshape(DEPTH, 8, 256, w).transpose(1, 0, 2, 3)
            parts.append(a)
        return np.concatenate(parts, axis=0).astype(np.float32)
    s_ckv = st("st_ckv", 128)
    s_kr = st("st_kr", 32)
    s_k = st("st_k", 128).reshape(32, DEPTH, 256, 2, 64)
    s_v = st("st_v", 128).reshape(32, DEPTH, 256, 2, 64)
    return (y_prompt, y_sample, s_ckv, s_kr, s_k, s_v)
```

```python
import numpy as np
import concourse.bass as bass
import concourse.mybir as mybir
from concourse.bass_utils import run_bass_kernel_spmd

F32 = mybir.dt.float32
BF16 = mybir.dt.bfloat16
AF = mybir.ActivationFunctionType
ALU = mybir.AluOpType

D = 1024
T = 2048
DEPTH = 4
L = 512
NKEY = T + L
NKT = NKEY // 128
EPS = 1e-6
MLA_SCALE = 96.0 ** -0.5
SWA_SCALE = 64.0 ** -0.5
DFF = 2816
NEG = -30000.0
BIGM = 512.0

C_CQ, C_CKV, C_KRT, C_KRPT, C_QS, C_QSP, C_K0, C_K0P, C_K1, C_K1P, C_VS, C_CU = (
    0, 256, 384, 480, 576, 832, 1088, 1216, 1344, 1472, 1600, 1728)
NWIN = 2240
V_ADAB, V_N1G, V_N2G, V_QNG, V_KVG, V_CDW, V_CDB, V_CLG, V_CLB, V_FDW, V_FDB, V_SINK = (
    0, 48, 56, 64, 66, 67, 129, 131, 133, 135, 267, 311)
NL = 315
V_FNG, V_CVEC, V_FLAG, V_CTXB = 4 * NL, 4 * NL + 8, 4 * NL + 16, 4 * NL + 17
NV = 4 * NL + 18

ENGS = ("pe", "act", "dve", "pool", "sp")


class _Stop(Exception):
    pass


class V:
    def __init__(self, ap, t):
        self.ap, self.t = ap, t

    def __getitem__(self, k):
        return V(self.ap[k], self.t)

    def re(self, pat, **kw):
        return V(self.ap.rearrange(pat, **kw), self.t)

    def bc(self, axis, shape):
        return V(self.ap.unsqueeze(axis).to_broadcast(list(shape)), self.t)


class Tile(V):
    def __init__(self, ap, space, lo, hi, ranges=None):
        V.__init__(self, ap, self)
        self.space, self.lo, self.hi = space, lo, hi
        self.ranges = ranges if ranges is not None else [(lo, hi)]
        self.w, self.r = {}, {}
        self.ov = [self]
        self.dsem = None
        self.dcum = 0


def _ap(x):
    return x.ap if isinstance(x, V) else x


class Prog:
    def __init__(self, nc):
        self.nc = nc
        self.ins = {e: [] for e in ENGS}
        self.waited = {e: {} for e in ENGS}
        self.tiles = []
        self.ndsem = 0
        self.dstate = {}
        self.count = 0
        self.limit = None

    def register(self, t):
        for u in self.tiles:
            if u.space == t.space and u.lo < t.hi and t.lo < u.hi:
                if any(a < d and c < b for (a, b) in u.ranges for (c, d) in t.ranges):
                    u.ov.append(t)
                    t.ov.append(u)
        self.tiles.append(t)
        return t

    def unregister(self, ts):
        pass

    def _deps(self, reads, writes):
        d = {}

        def add(dic):
            for k, v in dic.items():
                if d.get(k, -1) < v:
                    d[k] = v
        for t in reads:
            for u in t.ov:
                add(u.w)
        for t in writes:
            for u in t.ov:
                add(u.w)
                add(u.r)
        return d

    def _waits(self, eng, d):
        waits = []
        for k, v in d.items():
            if k == eng and eng in ("pe", "sp"):
                continue
            if self.waited[eng].get(k, -1) >= v:
                continue
            self.waited[eng][k] = v
            waits.append((k, v))
            if k in ENGS:
                self.ins[k][v]["marked"] = True
        return waits

    def _tick(self):
        self.count += 1
        if self.limit is not None and self.count > self.limit:
            raise _Stop

    def op(self, eng, fn, reads, writes):
        self._tick()
        reads = [x.t for x in reads if isinstance(x, V)]
        writes = [x.t for x in writes if isinstance(x, V)]
        waits = self._waits(eng, self._deps(reads, writes))
        idx = len(self.ins[eng])
        self.ins[eng].append(dict(fn=fn, waits=waits, marked=False, dma=None))
        for t in reads:
            t.r[eng] = idx
        for t in writes:
            t.w[eng] = idx

    def dma(self, q, out, in_):
        self._tick()
        reads = [in_.t] if isinstance(in_, V) else []
        writes = [out.t] if isinstance(out, V) else []
        waits = self._waits(q, self._deps(reads, writes))
        st = writes[0] if writes else reads[0]
        rk = (st.space, st.lo, st.hi)
        if rk not in self.dstate:
            self.dstate[rk] = [self.ndsem, 0]
            self.ndsem += 1
        ds = self.dstate[rk]
        ds[1] += 16
        st.dsem, st.dcum = ds[0], ds[1]
        key = ("d", st.dsem)
        o, i = _ap(out), _ap(in_)
        self.ins[q].append(dict(fn=lambda e: e.dma_start(out=o, in_=i), waits=waits, marked=False, dma=st.dsem))
        for t in reads:
            t.r[key] = max(t.r.get(key, 0), st.dcum)
        for t in writes:
            t.w[key] = max(t.w.get(key, 0), st.dcum)

    def emit(self, stack):
        nc = self.nc
        sems = {e: stack.enter_context(nc.semaphore("s_" + e)) for e in ENGS}
        dsems = [stack.enter_context(nc.semaphore("d%d" % i)) for i in range(self.ndsem)]
        for e in ENGS:
            c = 0
            for rec in self.ins[e]:
                if rec["marked"]:
                    c += 1
                rec["sv"] = c
        final = {v[0]: v[1] for v in self.dstate.values()}
        block = stack.enter_context(nc.Block())

        def run(e, eo):
            for rec in self.ins[e]:
                for k, v in rec["waits"]:
                    if k in ENGS:
                        eo.wait_ge(sems[k], self.ins[k][v]["sv"])
                    else:
                        eo.wait_ge(dsems[k[1]], v)
                ins = rec["fn"](eo)
                if rec["dma"] is not None:
                    ins.then_inc(dsems[rec["dma"]], 16)
                elif rec["marked"]:
                    ins.then_inc(sems[e], 1)
            if e == "sp":
                for k, v in final.items():
                    eo.wait_ge(dsems[k], v)
                for k in ("pe", "act", "dve", "pool"):
                    n = self.ins[k][-1]["sv"] if self.ins[k] else 0
                    if n:
                        eo.wait_ge(sems[k], n)

        @block.tensor
        def _(eo):
            run("pe", eo)

        @block.scalar
        def _(eo):
            run("act", eo)

        @block.vector
        def _(eo):
            run("dve", eo)

        @block.gpsimd
        def _(eo):
            run("pool", eo)

        @block.sync
        def _(eo):
            run("sp", eo)


def build_program(depth=DEPTH, stop=None):
    nc = bass.Bass("TRN2", target_bir_lowering=False)
    P = Prog(nc)
    if stop is not None and stop.startswith('n'):
        P.limit = int(stop[1:])

    def din(name, shape):
        return nc.dram_tensor(name, list(shape), F32, kind="ExternalInput").ap()

    def dout(name, shape):
        return nc.dram_tensor(name, list(shape), F32, kind="ExternalOutput").ap()

    x_d = din("x", [T, D])
    vecs_d = din("vecs", [128, NV])
    ident_d = din("ident", [128, 128])
    ropeS_d = din("ropeS", [128, 2, T])
    ropeM_d = din("ropeM", [32, 2, T])
    qaug_d = din("qaug", [9, T])
    kaug_d = din("kaug", [9, NKEY])
    smask_d = din("smask", [128, 4, 256])
    cckv_d = din("cckv", [DEPTH, L, 128])
    ckr_d = din("ckr", [DEPTH, L, 32])
    ck_d = din("ck", [DEPTH, L, 128])
    cv_d = din("cv", [DEPTH, L, 128])
    adaw_d = din("ada_w", [DEPTH, D, 6 * D])
    win_d = din("w_in_ext", [DEPTH, D, NWIN])
    wuq_d = din("w_uq_ext", [DEPTH, 256, 1536])
    wukv_d = din("w_ukv", [DEPTH, 128, 1024])
    pw2_d = din("pw2", [DEPTH, 256, 256])
    wout_d = din("w_out", [DEPTH, D, D])
    wup_d = din("w_up", [DEPTH, D, 2 * DFF])
    wdn_d = din("w_down", [DEPTH, DFF, D])
    y_d = dout("y", [T, D])
    sckv_d = dout("st_ckv", [DEPTH, T, 128])
    skr_d = dout("st_kr", [DEPTH, T, 32])
    sk_d = dout("st_k", [DEPTH, T, 128])
    sv_d = dout("st_v", [DEPTH, T, 128])

    from contextlib import ExitStack
    stack = ExitStack()
    ARENA = 210944
    arena_h = stack.enter_context(nc.sbuf_tensor("arena", [128, ARENA // 4], F32))
    base0 = nc.sbuf_base - (ARENA // 4) * 4
    arena_t = P.register(Tile(arena_h[:], "sb", 0, ARENA))
    ps_t = stack.enter_context(nc.psum_tensor("ps", [128, 4096], F32))
    ps_all = ps_t[:]

    class Mem:
        def __init__(self):
            self.top = 0
            self.n = 0

        def alloc(self, shape, dt):
            nb = int(np.prod(shape[1:])) * (2 if dt == BF16 else 4)
            nb = (nb + 63) // 64 * 64
            off = self.top
            self.top += nb
            assert self.top <= ARENA, ("SBUF overflow", self.top)
            self.n += 1
            h = nc.alloc_sbuf_tensor_at("t%d" % self.n, list(shape), dt, offset=base0 + off)
            ap = h.ap() if hasattr(h, "ap") else h[:]
            return P.register(Tile(ap, "sb", off, off + nb))

        def alloc_at(self, off, shape, dt):
            nb = int(np.prod(shape[1:])) * (2 if dt == BF16 else 4)
            nb = (nb + 63) // 64 * 64
            assert off % 64 == 0 and off + nb <= ARENA
            self.n += 1
            h = nc.alloc_sbuf_tensor_at("t%d" % self.n, list(shape), dt, offset=base0 + off)
            ap = h.ap() if hasattr(h, "ap") else h[:]
            return P.register(Tile(ap, "sb", off, off + nb))

    M = Mem()

    def psum(b0, nb=1, c0=0, c1=512):
        if nb == 1:
            ap = ps_all[:, b0 * 512 + c0: b0 * 512 + c1]
            return P.register(Tile(ap, "ps", b0 * 2048 + c0 * 4, b0 * 2048 + c1 * 4))
        ap = ps_all.rearrange("p (b n) -> p b n", b=8)[:, b0:b0 + nb, :]
        return P.register(Tile(ap, "ps", b0 * 2048, (b0 + nb) * 2048,
                               ranges=[(b * 2048 + c0 * 4, b * 2048 + c1 * 4) for b in range(b0, b0 + nb)]))

    PB = [psum(b) for b in range(8)]
    PBN = [psum(b, 1, 0, 258) for b in range(8)]
    PS03 = psum(0, 4, 0, 258)
    PS47 = psum(4, 4, 0, 258)

    def MM(out, lhsT, rhs, start=True, stop=True):
        o, l, r = _ap(out), _ap(lhsT), _ap(rhs)
        P.op("pe", lambda e: e.matmul(o, lhsT=l, rhs=r, start=start, stop=stop), [lhsT, rhs], [out])

    def TR(out, in_, ident):
        o, i, d = _ap(out), _ap(in_), _ap(ident)
        P.op("pe", lambda e: e.transpose(o, i, d), [in_, ident], [out])

    def ACT(out, in_, func, bias=None, scale=None):
        o, i = _ap(out), _ap(in_)
        kw = {}
        if bias is not None:
            kw["bias"] = _ap(bias)
        if scale is not None:
            kw["scale"] = _ap(scale)
        P.op("act", lambda e: e.activation(out=o, in_=i, func=func, **kw), [in_, bias, scale], [out])

    EO = {"dve": "dve", "pool": "pool"}

    def TT(eng, out, in0, in1, op):
        o, a, b = _ap(out), _ap(in0), _ap(in1)
        P.op(eng, lambda e: e.tensor_tensor(out=o, in0=a, in1=b, op=op), [in0, in1], [out])

    def TS(eng, out, in0, s1, op0, s2=None, op1=None):
        o, a, c1, c2 = _ap(out), _ap(in0), _ap(s1), _ap(s2)
        if op1 is None:
            P.op(eng, lambda e: e.tensor_scalar(out=o, in0=a, scalar1=c1, scalar2=None, op0=op0), [in0, s1], [out])
        else:
            P.op(eng, lambda e: e.tensor_scalar(out=o, in0=a, scalar1=c1, scalar2=c2, op0=op0, op1=op1),
                 [in0, s1, s2], [out])

    def STT(eng, out, in0, scalar, in1, op0, op1):
        o, a, s, b = _ap(out), _ap(in0), _ap(scalar), _ap(in1)
        P.op(eng, lambda e: e.scalar_tensor_tensor(out=o, in0=a, scalar=s, in1=b, op0=op0, op1=op1),
             [in0, scalar, in1], [out])

    def CP(eng, out, in_):
        o, i = _ap(out), _ap(in_)
        P.op(eng, lambda e: e.tensor_copy(out=o, in_=i), [in_], [out])

    def RECIP(out, in_):
        o, i = _ap(out), _ap(in_)
        P.op("dve", lambda e: e.reciprocal(out=o, in_=i), [in_], [out])

    def MEMSET(eng, out, val):
        o = _ap(out)
        P.op(eng, lambda e: e.memset(o, val), [], [out])

    def DMA(q, out, in_):
        P.dma(q, out, in_)

    cpt = [0]

    def EVAC(out, in_):
        cpt[0] += 1
        if cpt[0] % 2:
            ACT(out, in_, AF.Identity)
        else:
            CP("dve", out, in_)

    x = M.alloc([128, 8, T], F32)
    vecs = M.alloc([128, NV], F32)
    ident = M.alloc([128, 128], F32)
    identb = M.alloc([128, 128], BF16)
    ones_f = M.alloc([128, 128], F32)
    smask = M.alloc([128, 4, 256], BF16)
    modall = M.alloc([128, DEPTH * 48], F32)
    lv = M.alloc([128, 48], F32)
    esink = M.alloc([128, 16], F32)
    scb = M.alloc([128, 8], BF16)
    epsc = M.alloc([128, 1], F32)
    onesb = M.alloc([128, 128], BF16)
    PERSIST_TOP = M.top

    for a0 in range(0, ARENA // 4, 8192):
        a1 = min(ARENA // 4, a0 + 8192)
        eng_ = "dve" if (a0 // 8192) % 2 == 0 else "pool"
        o_ = arena_h[:, a0:a1]
        P.op(eng_, (lambda o: (lambda e: e.memset(o, 0.0)))(o_), [], [arena_t])
    DMA("sp", vecs, vecs_d)
    DMA("sp", ident, ident_d)
    DMA("pool", identb, ident_d)
    DMA("pool", smask, smask_d)
    MEMSET("dve", ones_f, 1.0)
    MEMSET("dve", epsc, EPS)
    MEMSET("dve", onesb, 1.0)

    def vcol(c, n=1):
        return vecs[:, c:c + n]

    flag = vcol(V_FLAG)
    ctxb = vcol(V_CTXB)

    ACT(scb, vcol(V_CVEC, 8), AF.Silu)
    TOPB = (ARENA // 64) * 64
    o_ = TOPB
    top_tiles = {}
    for nm, shp in (("wdn0", [128, 22, 256]), ("wdn1", [128, 22, 256]), ("wup0", [128, 8, 512]), ("wup1", [128, 8, 512]),
                    ("ada0", [128, 8, 256]), ("ada1", [128, 8, 256])):
        nb_ = int(np.prod(shp[1:])) * 2
        o_ -= nb_
        top_tiles[nm] = M.alloc_at(o_, shp, BF16)
    TOP_LIMIT = o_
    wup = [top_tiles["wup0"], top_tiles["wup1"]]
    wdn = [top_tiles["wdn0"], top_tiles["wdn1"]]
    ada_slots = [top_tiles["ada0"], top_tiles["ada1"]]
    modps = P.register(Tile(ps_all[:, 7 * 512 + 300: 7 * 512 + 300 + DEPTH * 48], "ps", 7 * 2048, 8 * 2048))
    adak = [0]

    def mod_steps(l):
        aw = adaw_d[l].rearrange("(kt p) c -> p kt c", p=128)
        k0 = adak[0]
        adak[0] += 24

        def dma(ch):
            DMA("pool", ada_slots[(k0 + ch) % 2], aw[:, :, ch * 256:(ch + 1) * 256])

        def mm(ch):
            sl = ada_slots[(k0 + ch) % 2]
            for j in range(2):
                col = l * 48 + ch * 2 + j
                for kt in range(8):
                    MM(modps[:, col:col + 1], sl[:, kt, j * 128:(j + 1) * 128], scb[:, kt:kt + 1],
                       start=(kt == 0), stop=(kt == 7))
        steps = [lambda: (dma(0), dma(1))]
        for ch in range(24):
            def f(ch=ch):
                mm(ch)
                if ch + 2 < 24:
                    dma(ch + 2)
                if ch == 23:
                    o_, a_, b_ = modall[:, l * 48:(l + 1) * 48], modps[:, l * 48:(l + 1) * 48], vcol(l * NL + V_ADAB, 48)
                    P.op("dve", lambda e: e.tensor_tensor(out=o_.ap, in0=a_.ap, in1=b_.ap, op=ALU.add),
                         [a_, b_], [o_, PB[7]])
            steps.append(f)
        return steps

    def modulation(l):
        for f_ in mod_steps(l):
            f_()
    M.top = PERSIST_TOP
    xs = [M.alloc([128, D], F32) for _ in range(2)]
    for tb in range(16):
        st = xs[tb % 2]
        DMA("sp", st, x_d[tb * 128:(tb + 1) * 128, :])
        for hb in range(2):
            pb = PB[(tb * 2 + hb) % 4]
            for j in range(4):
                ft = hb * 4 + j
                TR(pb[:, j * 128:(j + 1) * 128], st[:, ft * 128:(ft + 1) * 128], ident)
            EVAC(x[:, hb * 4:(hb + 1) * 4, tb * 128:(tb + 1) * 128], pb.re("p (j n) -> p j n", j=4))

    modulation(0)
    for l in range(depth):
        ACT(esink[:, l * 4:(l + 1) * 4], vcol(l * NL + V_SINK, 4), AF.Exp)

    def rstd_from(ps_sum, out, nfeat):
        ACT(out, ps_sum, AF.Ln, bias=epsc, scale=1.0 / nfeat)
        ACT(out, out, AF.Exp, scale=-0.5)

    def norm_chunk(tok0, A, B, hdst, sq, tmp, rs, psn):
        for ft in range(8):
            if ft % 2 == 0:
                ACT(sq[ft % 4], x[:, ft, tok0:tok0 + 512], AF.Square)
            else:
                TT("dve", sq[ft % 4], x[:, ft, tok0:tok0 + 512], x[:, ft, tok0:tok0 + 512], ALU.mult)
            MM(psn, onesb, sq[ft % 4], start=(ft == 0), stop=(ft == 7))
        rstd_from(psn, rs, float(D))
        for ft in range(8):
            TT("dve", tmp[ft % 2], x[:, ft, tok0:tok0 + 512], rs, ALU.mult)
            ACT(hdst(ft), tmp[ft % 2].re("p (s n) -> p s n", s=2), AF.Identity, bias=B[:, ft:ft + 1],
                scale=A[:, ft:ft + 1])

    def out_transposed(src, rows, r0, dst_d, tok0, ost, pbank):
        for b in range(4):
            TR(pbank[:, b * rows:(b + 1) * rows], src[r0:r0 + rows, b * 128:(b + 1) * 128], ident[r0:r0 + rows, r0:r0 + rows])
        EVAC(ost[:, :, 0:rows], pbank[:, 0:4 * rows].re("p (b f) -> p b f", b=4))
        DMA("sp", dst_d[tok0:tok0 + 512, :].rearrange("(b p) f -> p b f", p=128), ost[:, :, 0:rows])

    def layers():
      for l in range(depth):
        if stop == 'pro':
            raise _Stop
        vb = l * NL
        mod = modall[:, l * 48:(l + 1) * 48]
        B1, G1, G2 = mod[:, 0:8], mod[:, 16:24], mod[:, 40:48]
        A1, A2, A2f, B2f = lv[:, 0:8], lv[:, 8:16], lv[:, 16:24], lv[:, 24:32]
        B2 = mod[:, 24:32]
        STT("dve", A1, mod[:, 8:16], 1.0, vcol(vb + V_N1G, 8), ALU.add, ALU.mult)
        STT("dve", A2, mod[:, 32:40], 1.0, vcol(vb + V_N2G, 8), ALU.add, ALU.mult)
        TS("dve", A2f, A2, flag, ALU.mult)
        TS("dve", B2f, B2, flag, ALU.mult)

        M.top = PERSIST_TOP
        omix = M.alloc([128, 8, T], BF16)
        OMIX_TOP = M.top
        rs1 = M.alloc([128, T], F32)
        RS1_TOP = M.top
        cqn = M.alloc([128, 2, T], BF16)
        ckvT = M.alloc([128, NKEY], BF16)
        krT = M.alloc([128, NKEY], BF16)
        wuq = M.alloc([128, 2, 1536], BF16)
        wukv = M.alloc([128, 1024], BF16)
        cst = M.alloc([128, 4, 128], F32)
        cst2 = M.alloc([128, 4, 96], F32)
        A1OUT_TOP = M.top
        win1 = M.alloc([128, 8, 576], BF16)
        hb = [M.alloc([128, 8, 2, 258], BF16) for _ in range(2)]
        tabMc = M.alloc([128, 2, 512], F32)
        sq = [M.alloc([128, 512], BF16) for _ in range(4)]
        tmp = [M.alloc([128, 512], F32) for _ in range(2)]
        rs = M.alloc([128, 512], F32)
        cqf = M.alloc([128, 2, 512], F32)
        stf = M.alloc([128, 512], F32)
        uu = M.alloc([128, 512], F32)
        vv = M.alloc([128, 512], F32)
        ost = [M.alloc([128, 4, 128], F32) for _ in range(2)]
        DMA("pool", win1, win_d[l].rearrange("(kt p) c -> p kt c", p=128)[:, :, 0:576])
        DMA("pool", wuq, wuq_d[l].rearrange("(kt p) c -> p kt c", p=128))
        DMA("pool", wukv, wukv_d[l])
        DMA("sp", cst, cckv_d[l].rearrange("(b p) f -> p b f", p=128))
        MEMSET("dve", cst2, 0.0)
        DMA("sp", cst2[:, :, 64:96], ckr_d[l].rearrange("(b p) f -> p b f", p=128))
        for b in range(4):
            TR(PB[7][:, b * 128:(b + 1) * 128], cst[:, b, :], ident)
        EVAC(ckvT[:, T:NKEY], PB[7])
        for b in range(4):
            TR(PB[6][0:96, b * 128:(b + 1) * 128], cst2[:, b, :], ident)
        EVAC(krT[64:96, T:NKEY], PB[6][64:96, :])

        def normc1(g_, c_):
            t0_ = g_ * 1024 + c_ * 512
            norm_chunk(t0_, A1, B1, lambda ft: hb[c_][:, ft, :, 1:257], sq, tmp, rs1[:, t0_:t0_ + 512], PB[4])
        normc1(0, 0)
        normc1(0, 1)
        for g in range(2):
            for c in range(2):
                tok0 = g * 1024 + c * 512
                rhs = lambda kt: hb[c][:, kt, :, 1:257]
                for j in range(2):
                    pb = PB[j]
                    for kt in range(8):
                        MM(pb, win1[:, kt, C_CQ + j * 128:C_CQ + (j + 1) * 128], rhs(kt), kt == 0, kt == 7)
                    CP("dve", cqf[:, j, :], pb)
                    ACT(sq[j], cqf[:, j, :], AF.Square)
                for j in range(2):
                    MM(PB[5], onesb, sq[j], j == 0, j == 1)
                rstd_from(PB[5], rs, 256.0)
                for j in range(2):
                    STT("dve", cqn[:, j, tok0:tok0 + 512], cqf[:, j, :], vcol(vb + V_QNG + j), rs, ALU.mult, ALU.mult)
                if stop in ('A1b', 'cnt'):
                    print('count at A1b', P.count)
                if stop == 'A1b':
                    raise _Stop
                pb = PB[2]
                for kt in range(8):
                    MM(pb, win1[:, kt, C_CKV:C_CKV + 128], rhs(kt), kt == 0, kt == 7)
                ACT(sq[0], pb, AF.Square)
                MM(PB[5], onesb, sq[0], True, True)
                rstd_from(PB[5], rs, 128.0)
                STT("dve", stf, pb, vcol(vb + V_KVG), rs, ALU.mult, ALU.mult)
                CP("pool", ckvT[:, tok0:tok0 + 512], stf)
                out_transposed(stf, 128, 0, sckv_d[l], tok0, ost[0], PB[6])
                if stop in ('A1c', 'cnt'):
                    print('count at A1c', P.count)
                if stop == 'A1c':
                    raise _Stop
                pa, pbb = PB[3], PB[7]
                for kt in range(8):
                    MM(pa[0:96, :], win1[:, kt, C_KRT:C_KRT + 96], rhs(kt), kt == 0, kt == 7)
                for kt in range(8):
                    MM(pbb[0:96, :], win1[:, kt, C_KRPT:C_KRPT + 96], rhs(kt), kt == 0, kt == 7)
                DMA("sp", tabMc[64:96], ropeM_d[:, :, tok0:tok0 + 512])
                CP("dve", stf[64:96, :], pa[64:96, :])
                TT("dve", uu[64:96, :], pa[64:96, :], tabMc[64:96, 0, :], ALU.mult)
                TT("dve", vv[64:96, :], pbb[64:96, :], tabMc[64:96, 1, :], ALU.mult)
                TT("pool", krT[64:96, tok0:tok0 + 512], uu[64:96, :], vv[64:96, :], ALU.add)
                out_transposed(stf, 32, 64, skr_d[l], tok0, ost[1], PB[6])
                if g == 0:
                    normc1(1, c)

        if stop in ('A1', 'cnt'):
            print('count at A1', P.count)
        if stop == 'A1':
            raise _Stop
        M.top = A1OUT_TOP
        tabM = M.alloc([128, 2, T], F32)
        Qs_ = [M.alloc([128, T], BF16) for _ in range(2)]
        Ks_ = [M.alloc([128, NKEY], BF16) for _ in range(2)]
        Vs_ = [M.alloc([128, NKT, 128], BF16) for _ in range(2)]
        Pb = [M.alloc([128, 512], BF16) for _ in range(3)]
        Rb = M.alloc([128, 512], F32)
        uu = M.alloc([128, 512], F32)
        vv = M.alloc([128, 512], F32)
        DMA("sp", tabM[64:96], ropeM_d)
        for s in range(2):
            MEMSET("pool", Qs_[s][96:128, :], 0.0)
            MEMSET("pool", Ks_[s][96:128, :], 0.0)
            DMA("pool", Qs_[s][96:105, :], qaug_d)
            DMA("pool", Ks_[s][96:105, :], kaug_d)
            MEMSET("pool", Vs_[s][:, :, 64:128], 1.0)
        def build_steps(h):
            s_ = h % 2
            Qh, Kh, Vh = Qs_[s_], Ks_[s_], Vs_[s_]
            steps = []
            for kc in range(5):
                def f(kc=kc):
                    pb = PB[5 + kc % 2]
                    MM(pb[0:64, :], wukv[:, h * 128:h * 128 + 64], ckvT[:, kc * 512:(kc + 1) * 512])
                    CP("dve", Kh[0:64, kc * 512:(kc + 1) * 512], pb[0:64, :])
                steps.append(f)
            steps.append(lambda: CP("pool", Kh[64:96, :], krT[64:96, :]))
            for k0 in range(0, NKT, 8):
                def f(k0=k0):
                    n = min(8, NKT - k0)
                    pb = PB[7]
                    for j in range(n):
                        MM(pb[:, j * 64:(j + 1) * 64], ckvT[:, (k0 + j) * 128:(k0 + j + 1) * 128],
                           wukv[:, h * 128 + 64:h * 128 + 128])
                    CP("dve", Vh[:, k0:k0 + n, 0:64], pb[:, 0:n * 64].re("p (j d) -> p j d", j=n))
                steps.append(f)
            for c in range(4):
                def f(c=c):
                    pa, pbb = PB[5], PB[6]
                    for kt in range(2):
                        MM(pa[0:96, :], wuq[:, kt, h * 192:h * 192 + 96], cqn[:, kt, c * 512:(c + 1) * 512], kt == 0, kt == 1)
                    for kt in range(2):
                        MM(pbb[0:96, :], wuq[:, kt, h * 192 + 96:h * 192 + 192], cqn[:, kt, c * 512:(c + 1) * 512], kt == 0, kt == 1)
                    CP("dve", Qh[0:64, c * 512:(c + 1) * 512], pa[0:64, :])
                    TT("dve", uu[64:96, :], pa[64:96, :], tabM[64:96, 0, c * 512:(c + 1) * 512], ALU.mult)
                    TT("dve", vv[64:96, :], pbb[64:96, :], tabM[64:96, 1, c * 512:(c + 1) * 512], ALU.mult)
                    TT("pool", Qh[64:96, c * 512:(c + 1) * 512], uu[64:96, :], vv[64:96, :], ALU.add)
                steps.append(f)
            return steps

        for f_ in build_steps(0):
            f_()
        for h in range(8):
            s = h % 2
            Qh, Kh, Vh = Qs_[s], Ks_[s], Vs_[s]
            pending = build_steps(h + 1) if h + 1 < 8 else []
            it = 0
            for c in range(4):
                po = PB[3 + (h * 4 + c) % 2]
                qv = Qh[:, c * 512:(c + 1) * 512]

                def score(kt):
                    MM(PB[kt % 3], Kh[:, kt * 128:(kt + 1) * 128], qv)
                score(0)
                score(1)
                for kt in range(NKT):
                    ACT(Pb[kt % 3], PB[kt % 3], AF.Exp, scale=MLA_SCALE)
                    if kt + 2 < NKT:
                        score(kt + 2)
                    MM(po, Vh[:, kt, :], Pb[kt % 3], kt == 0, kt == NKT - 1)
                    it += 1
                    if pending and it % 5 == 0:
                        pending.pop(0)()
                RECIP(Rb[64:128, :], po[64:128, :])
                p0 = (h % 2) * 64
                TT("dve", omix[p0:p0 + 64, h // 2, c * 512:(c + 1) * 512], po[0:64, :], Rb[64:128, :], ALU.mult)
            while pending:
                pending.pop(0)()

        if stop == 'BM':
            raise _Stop
        M.top = RS1_TOP
        qsw = M.alloc([128, 2, T], BF16)
        ksd = M.alloc([128, 2, NKEY], BF16)
        vsx = M.alloc([128, NKT, 2, 128], BF16)
        ypad = M.alloc([128, 2, 8, 286], BF16)
        A2OUT_TOP = M.top
        win2 = M.alloc([128, 8, 1152], BF16)
        hb = [M.alloc([128, 8, 2, 258], BF16) for _ in range(2)]
        tabS = [M.alloc([128, 2, 512], F32) for _ in range(1)]
        tmp = [M.alloc([128, 512], F32) for _ in range(2)]
        stf = M.alloc([128, 512], F32)
        uu = M.alloc([128, 512], F32)
        vv = M.alloc([128, 512], F32)
        ost = [M.alloc([128, 4, 128], F32) for _ in range(2)]
        winv = win_d[l].rearrange("(kt p) c -> p kt c", p=128)
        MEMSET("pool", vsx[:, :, :, 64:128], 1.0)
        MEMSET("pool", ypad[:, :, 0, 0:15], 0.0)
        MEMSET("pool", ypad[:, :, 7, 271:286], 0.0)
        tsi = 0
        def normc2(g_, c_):
            t0_ = g_ * 1024 + c_ * 512
            for ft in range(8):
                TT("dve", tmp[ft % 2], x[:, ft, t0_:t0_ + 512], rs1[:, t0_:t0_ + 512], ALU.mult)
                ACT(hb[c_][:, ft, :, 1:257], tmp[ft % 2].re("p (s n) -> p s n", s=2), AF.Identity,
                    bias=B1[:, ft:ft + 1], scale=A1[:, ft:ft + 1])
        normc2(0, 0)
        normc2(0, 1)
        for g in range(2):
            DMA("pool", win2, winv[:, :, C_QS:C_QS + 1152])
            for c in range(2):
                tok0 = g * 1024 + c * 512
                cb = g * 2 + c
                rhs = lambda kt: hb[c][:, kt, :, 1:257]
                tb_ = tabS[0]
                tsi += 1
                DMA("sp", tb_, ropeS_d[:, :, tok0:tok0 + 512])

                rtn = [0]

                def rope_tile(ca, cbb, dst, keep=None):
                    pa, pbb = (PB[0], PB[1]) if rtn[0] % 2 == 0 else (PB[2], PB[3])
                    rtn[0] += 1
                    for kt in range(8):
                        MM(pa, win2[:, kt, ca - C_QS:ca - C_QS + 128], rhs(kt), kt == 0, kt == 7)
                    for kt in range(8):
                        MM(pbb, win2[:, kt, cbb - C_QS:cbb - C_QS + 128], rhs(kt), kt == 0, kt == 7)
                    if keep is not None:
                        CP("dve", stf[keep:keep + 64, :], pa[keep:keep + 64, :])
                    u_, v_ = (uu, vv) if rtn[0] % 2 == 0 else (tmp[0], tmp[1])
                    TT("dve", u_, pa, tb_[:, 0, :], ALU.mult)
                    TT("dve", v_, pbb, tb_[:, 1, :], ALU.mult)
                    TT("pool", dst, u_, v_, ALU.add)
                rope_tile(C_QS, C_QSP, qsw[:, 0, tok0:tok0 + 512])
                rope_tile(C_QS + 128, C_QSP + 128, qsw[:, 1, tok0:tok0 + 512])
                rope_tile(C_K0, C_K0P, ksd[:, 0, tok0:tok0 + 512], keep=0)
                rope_tile(C_K1, C_K1P, ksd[:, 1, tok0:tok0 + 512], keep=64)
                out_transposed(stf, 128, 0, sk_d[l], tok0, ost[0], PB[6])
            DMA("pool", win2[:, :, 0:640], winv[:, :, C_VS:C_VS + 640])
            for c in range(2):
                tok0 = g * 1024 + c * 512
                rhs = lambda kt: hb[c][:, kt, :, 1:257]
                pb = PB[2]
                for kt in range(8):
                    MM(pb, win2[:, kt, 0:128], rhs(kt), kt == 0, kt == 7)
                CP("dve", stf, pb)
                for b in range(4):
                    TR(PB[6][:, b * 128:(b + 1) * 128], stf[:, b * 128:(b + 1) * 128], ident)
                ACT(ost[1], PB[6].re("p (b f) -> p b f", b=4), AF.Identity)
                DMA("sp", sv_d[l][tok0:tok0 + 512, :].rearrange("(b p) f -> p b f", p=128), ost[1])
                kt0 = tok0 // 128
                CP("dve", vsx[:, kt0:kt0 + 4, :, 0:64], ost[1].re("p b (k d) -> p b k d", k=2))
                for j in range(2):
                    pu, pg = (PB[0], PB[1]) if j == 0 else (PB[3], PB[4])
                    for kt in range(8):
                        MM(pu, win2[:, kt, 128 + j * 128:256 + j * 128], rhs(kt), kt == 0, kt == 7)
                    for kt in range(8):
                        MM(pg, win2[:, kt, 384 + j * 128:512 + j * 128], rhs(kt), kt == 0, kt == 7)
                    ACT(uu, pg, AF.Sigmoid)
                    sg0 = tok0 // 256
                    TT("dve", ypad[:, j, sg0:sg0 + 2, 15:271], pu.re("p (s n) -> p s n", s=2),
                       uu.re("p (s n) -> p s n", s=2), ALU.mult)
                if g == 0:
                    normc2(1, c)

        if stop == 'A2':
            raise _Stop
        M.top = A2OUT_TOP
        wout = M.alloc_at(TOPB - 16384, [128, 8, D], BF16)
        DMA("pool", wout, wout_d[l].rearrange("(kt p) c -> p kt c", p=128))
        cst = M.alloc([128, 4, 2, 2, 64], F32)
        P2 = [M.alloc([128, 256], BF16) for _ in range(3)]
        R2 = M.alloc([128, 256], F32)
        qzs = [M.alloc([128, 2, 128], BF16) for _ in range(2)]
        for q_ in qzs:
            MEMSET("pool", q_, 0.0)
        diag_t = [M.alloc([128, 128], BF16) for _ in range(62)]
        pw2 = M.alloc([128, 2, 256], BF16)
        ycv = M.alloc([128, 2, 512], F32)
        sqc = M.alloc([128, 1, 512], F32)
        mean = M.alloc([128, 512], F32)
        var = M.alloc([128, 512], F32)
        dd = M.alloc([128, 512], F32)
        yact = M.alloc([128, 2, 512], BF16)
        ckv_ = ck_d[l].rearrange("(b p) (k d) -> p b k d", p=128, k=2)
        for du in range(2):
            for kh in range(2):
                DMA("sp", cst[:, :, kh, du, :], ckv_[:, :, kh, :])
        for kh in range(2):
            for b in range(4):
                TR(PB[7][:, b * 128:(b + 1) * 128], cst[:, b, kh].re("p u d -> p (u d)"), ident)
            EVAC(ksd[:, kh, T:NKEY], PB[7])
        for kh in range(2):
            DMA("pool", vsx[:, 16:20, kh, 0:64], cv_d[l].rearrange("(b p) (k d) -> p b k d", p=128, k=2)[:, :, kh, :])
        DMA("pool", pw2, pw2_d[l].rearrange("(kt p) c -> p kt c", p=128))
        for tap in range(31):
            for ct in range(2):
                if (tap + ct) % 2 == 0:
                    ACT(diag_t[tap * 2 + ct], identb, AF.Identity, scale=vcol(vb + V_CDW + tap * 2 + ct))
                else:
                    TS("dve", diag_t[tap * 2 + ct], identb, vcol(vb + V_CDW + tap * 2 + ct), ALU.mult)
        if stop == 'cnt':
            print('count at BS-prep-end', P.count)
        si = 0
        for kh in range(2):
            for i in range(16):
                if stop == 'cnt' and kh == 0 and i < 2:
                    print('count at swa block', i, P.count)
                keys = []
                if i > 0:
                    keys.append((i - 1, (i % 2) * 2 + 0, False))
                keys.append((i, None, False))
                if i < 15:
                    keys.append((i + 1, (i % 2) * 2 + 1, False))
                for cb in range(4):
                    keys.append((16 + cb, None, True))
                po = PB[3 + (kh * 16 + i) % 2]
                nk = len(keys)

                qz = qzs[(kh * 16 + i) % 2]
                CP("pool", qz[0:64, 0, :], qsw[0:64, kh, i * 128:(i + 1) * 128])
                CP("pool", qz[64:128, 1, :], qsw[64:128, kh, i * 128:(i + 1) * 128])

                def score(n, sb):
                    kt, m, isctx = keys[n]
                    MM(sb, ksd[:, kh, kt * 128:(kt + 1) * 128], qz.re("p g n -> p (g n)"), True, m is None)
                    if m is not None:
                        MM(sb, identb, smask[:, m, :], False, True)
                sbs = [PB[0][:, 0:256], PB[1][:, 0:256], PB[2][:, 0:256]]
                score(0, sbs[si % 3])
                score(1, sbs[(si + 1) % 3])
                for n in range(nk):
                    kt, m, isctx = keys[n]
                    if isctx:
                        ACT(P2[(si + n) % 3], sbs[(si + n) % 3], AF.Exp, bias=ctxb, scale=SWA_SCALE)
                    else:
                        ACT(P2[(si + n) % 3], sbs[(si + n) % 3], AF.Exp, scale=SWA_SCALE)
                    if n + 2 < nk:
                        score(n + 2, sbs[(si + n + 2) % 3])
                    MM(po[:, 0:256], vsx[:, kt, kh, :], P2[(si + n) % 3], n == 0, n == nk - 1)
                si += nk
                for g in range(2):
                    TS("dve", R2[64:128, g * 128:(g + 1) * 128], po[64:128, g * 128:(g + 1) * 128],
                       esink[64:128, l * 4 + kh * 2 + g:l * 4 + kh * 2 + g + 1], ALU.add)
                RECIP(R2[64:128, :], R2[64:128, :])
                for g in range(2):
                    TT("dve", omix[g * 64:(g + 1) * 64, 4 + kh, i * 128:(i + 1) * 128],
                       po[0:64, g * 128:(g + 1) * 128], R2[64:128, g * 128:(g + 1) * 128], ALU.mult)
        if stop == 'cnt':
            print('count at swa-end', P.count)
        for ct in range(2):
            TS("pool", ypad[:, ct, 1:8, 0:15], ypad[:, ct, 0:7, 256:271], flag, ALU.mult)
            TS("pool", ypad[:, ct, 0:7, 271:286], ypad[:, ct, 1:8, 15:30], flag, ALU.mult)
        for sp_ in range(4):
            for ct in range(2):
                pc = PB[(5 if sp_ % 2 == 0 else 3) + ct]
                for tap in range(31):
                    MM(pc, diag_t[tap * 2 + ct], ypad[:, ct, 2 * sp_:2 * sp_ + 2, tap:tap + 256], tap == 0, tap == 30)
                ACT(ycv[:, ct, :], pc, AF.Identity, bias=vcol(vb + V_CDB + ct))
            for ct in range(2):
                MM(PB[0], ones_f, ycv[:, ct, :], ct == 0, ct == 1)
            for ct in range(2):
                ACT(sqc[:, 0, :], ycv[:, ct, :], AF.Square)
                MM(PB[1], ones_f, sqc[:, 0, :], ct == 0, ct == 1)
            TS("dve", mean, PB[0], 1.0 / 256.0, ALU.mult)
            TT("dve", dd, mean, mean, ALU.mult)
            STT("dve", var, PB[1], 1.0 / 256.0, dd, ALU.mult, ALU.subtract)
            ACT(var, var, AF.Ln, bias=epsc)
            ACT(var, var, AF.Exp, scale=-0.5)
            for ct in range(2):
                TT("dve", dd, ycv[:, ct, :], mean, ALU.subtract)
                TT("dve", dd, dd, var, ALU.mult)
                ACT(yact[:, ct, :], dd, AF.Silu, bias=vcol(vb + V_CLB + ct), scale=vcol(vb + V_CLG + ct))
            for co in range(2):
                pp = PB[2]
                for ct in range(2):
                    MM(pp, pw2[:, ct, co * 128:(co + 1) * 128], yact[:, ct, :], ct == 0, ct == 1)
                EVAC(omix[:, 6 + co, sp_ * 512:(sp_ + 1) * 512], pp)

        if stop == 'cnt':
            print('count at BS-end', P.count)
        if stop == 'BS':
            raise _Stop
        M.top = OMIX_TOP
        wupv = wup_d[l].rearrange("(kt p) c -> p kt c", p=128)
        wdnv = wdn_d[l].rearrange("(kt p) c -> p kt c", p=128)
        upseq = [(g_, j_) for g_ in range(2) for j_ in range(11)]
        dnseq = [(g_, q_) for g_ in range(2) for q_ in range(4)]
        upn = [0]
        dnn = [0]

        def up_dma():
            if upn[0] < len(upseq):
                g_, j_ = upseq[upn[0]]
                ws_ = wup[upn[0] % 2]
                upn[0] += 1
                DMA("pool", ws_[:, :, 0:256], wupv[:, :, j_ * 256:(j_ + 1) * 256])
                DMA("pool", ws_[:, :, 256:512], wupv[:, :, DFF + j_ * 256:DFF + (j_ + 1) * 256])

        def dn_dma():
            if dnn[0] < len(dnseq):
                g_, q_ = dnseq[dnn[0]]
                ws_ = wdn[dnn[0] % 2]
                dnn[0] += 1
                DMA("pool", ws_, wdnv[:, :, q_ * 256:(q_ + 1) * 256])
        up_dma()
        up_dma()
        modq = mod_steps(l + 1) if l + 1 < depth else []
        if modq:
            modq.pop(0)()
        n = 0
        hbuf = M.alloc_at(PERSIST_TOP + 45056, [128, 8, 4, 258], BF16)
        rsall = M.alloc_at(PERSIST_TOP + 16512 + 45056, [128, T], F32)
        tmp = [M.alloc_at(PERSIST_TOP + 16512 + 45056 + 8192 + 16384, [128, 512], F32)] * 2
        ht = M.alloc_at(PERSIST_TOP + 16512 + 45056 + 8192 + 16384 + 2048, [128, 8], F32)
        hsave = M.alloc_at(PERSIST_TOP + 16512 + 45056 + 8192 + 16384 + 2048 + 64, [128, 16], F32)

        def norm_apply(g):
            for c in range(2):
                tok0 = g * 1024 + c * 512
                for ft in range(8):
                    TT("dve", tmp[ft % 2], x[:, ft, tok0:tok0 + 512], rsall[:, tok0:tok0 + 512], ALU.mult)
                    ACT(hbuf[:, ft, 2 * c:2 * c + 2, 1:257], tmp[ft % 2].re("p (s n) -> p s n", s=2), AF.Identity,
                        bias=B2[:, ft:ft + 1], scale=A2[:, ft:ft + 1])
            TS("pool", hbuf[:, :, 1:4, 0:1], hbuf[:, :, 0:3, 256:257], flag, ALU.mult)
            TS("pool", hbuf[:, :, 0:3, 257:258], hbuf[:, :, 1:4, 1:2], flag, ALU.mult)
            if g == 0:
                MEMSET("pool", hbuf[:, :, 0, 0:1], 0.0)
                CP("dve", hbuf[:, :, 3, 257], hsave[:, 8:16])
            else:
                CP("dve", hbuf[:, :, 0, 0], hsave[:, 0:8])
                MEMSET("pool", hbuf[:, :, 3, 257:258], 0.0)
        sqc_ = [M.alloc([128, 512], BF16) for _ in range(4)]
        for c in range(4):
            for dt in range(8):
                pb = PB[n % 4]
                n += 1
                for kt in range(8):
                    MM(pb, wout[:, kt, dt * 128:(dt + 1) * 128], omix[:, kt, c * 512:(c + 1) * 512], kt == 0, kt == 7)
                STT("dve", x[:, dt, c * 512:(c + 1) * 512], pb, G1[:, dt:dt + 1], x[:, dt, c * 512:(c + 1) * 512],
                    ALU.mult, ALU.add)
            for ft in range(8):
                if ft % 2 == 0:
                    ACT(sqc_[ft % 4], x[:, ft, c * 512:(c + 1) * 512], AF.Square)
                else:
                    TT("pool", sqc_[ft % 4], x[:, ft, c * 512:(c + 1) * 512], x[:, ft, c * 512:(c + 1) * 512], ALU.mult)
                MM(PB[4 + c % 2], onesb, sqc_[ft % 4], ft == 0, ft == 7)
            rstd_from(PB[4 + c % 2], rsall[:, c * 512:(c + 1) * 512], float(D))
            if c == 2:
                for hi_, t in enumerate((1023, 1024)):
                    TT("dve", ht, x[:, :, t], V(rsall.ap[:, t:t + 1].to_broadcast([128, 8]), rsall), ALU.mult)
                    TT("dve", ht, ht, A2f, ALU.mult)
                    TT("dve", hsave[:, hi_ * 8:(hi_ + 1) * 8], ht, B2f, ALU.add)
                norm_apply(0)

        dn_dma()
        dn_dma()
        if stop == 'C':
            raise _Stop
        M.top = PERSIST_TOP
        pbuf = M.alloc([128, 22, 1024], BF16)
        M.top += 16512 + 8192
        a1us = [M.alloc([128, 4, 256], F32) for _ in range(2)]
        a1gs = [M.alloc([128, 4, 256], F32) for _ in range(2)]
        M.top += 2048 + 64 + 64
        assert M.top <= TOP_LIMIT, M.top
        wi = 0
        di = 0
        for g in range(2):
            for j in range(11):
                ws = wup[wi % 2]
                wi += 1
                if wi >= 2:
                    pass
                for jj in range(2):
                    f = 2 * j + jj
                    a1u, a1g = a1us[f % 2], a1gs[f % 2]
                    for s in range(4):
                        for kt in range(8):
                            MM(PBN[s], ws[:, kt, jj * 128:(jj + 1) * 128], hbuf[:, kt, s, :], kt == 0, kt == 7)
                    if modq:
                        modq.pop(0)()
                    for s in range(4):
                        for kt in range(8):
                            MM(PBN[4 + s], ws[:, kt, 256 + jj * 128:256 + (jj + 1) * 128], hbuf[:, kt, s, :],
                               kt == 0, kt == 7)
                    for (ps_, a1, fc) in ((PS03, a1u, f), (PS47, a1g, 22 + f)):
                        ACT(a1, ps_[:, :, 1:257], AF.Identity, bias=vcol(vb + V_FDB + fc),
                            scale=vcol(vb + V_FDW + 44 + fc))
                        STT("dve", a1, ps_[:, :, 0:256], vcol(vb + V_FDW + fc), a1, ALU.mult, ALU.add)
                        STT("dve", a1, ps_[:, :, 2:258], vcol(vb + V_FDW + 88 + fc), a1, ALU.mult, ALU.add)
                    ACT(a1g, a1g, AF.Silu)
                    if jj == 1:
                        up_dma()
                    TT("pool", pbuf[:, f, :].re("p (s n) -> p s n", s=4), a1u, a1g, ALU.mult)
            if g == 0:
                norm_apply(1)
            for dq in range(4):
                ws = wdn[di % 2]
                di += 1
                for dd_ in range(2):
                    dt = 2 * dq + dd_
                    for c in range(2):
                        pb = PB[(dt * 2 + c) % 4]
                        tok0 = g * 1024 + c * 512
                        for f in range(22):
                            MM(pb, ws[:, f, dd_ * 128:(dd_ + 1) * 128], pbuf[:, f, c * 512:(c + 1) * 512], f == 0, f == 21)
                        STT("dve", x[:, dt, tok0:tok0 + 512], pb, G2[:, dt:dt + 1], x[:, dt, tok0:tok0 + 512],
                            ALU.mult, ALU.add)
                dn_dma()
            if g == 1:
                while modq:
                    modq.pop(0)()

    try:
        layers()
    except _Stop:
        P.limit = None

    M.top = PERSIST_TOP
    sq = [M.alloc([128, 512], BF16) for _ in range(2)]
    rs = M.alloc([128, 512], F32)
    yfc = M.alloc([128, 8, 512], F32)
    yst = [M.alloc([128, D], F32) for _ in range(2)]
    oi = 0
    for c in range(4):
        for ft in range(8):
            ACT(sq[ft % 2], x[:, ft, c * 512:(c + 1) * 512], AF.Square)
            MM(PB[4], onesb, sq[ft % 2], ft == 0, ft == 7)
        rstd_from(PB[4], rs, float(D))
        for ft in range(8):
            STT("dve", yfc[:, ft, :], x[:, ft, c * 512:(c + 1) * 512], vcol(V_FNG + ft), rs, ALU.mult, ALU.mult)
        for b in range(4):
            st = yst[oi % 2]
            oi += 1
            for hb in range(2):
                pb = PB[(b * 2 + hb) % 4]
                for j in range(4):
                    TR(pb[:, j * 128:(j + 1) * 128], yfc[:, hb * 4 + j, b * 128:(b + 1) * 128], ident)
                EVAC(st[:, hb * 512:(hb + 1) * 512], pb)
            tok = c * 512 + b * 128
            DMA("sp", y_d[tok:tok + 128, :], st)

    P.emit(stack)
    stack.close()
    return nc


def _rope_tables(sample):
    t = np.arange(T)
    row = (t // 64).astype(np.float32)
    col = (t % 64).astype(np.float32)

    def tab(nd, blk):
        qd = blk
        freqs = (10000.0 ** (-np.arange(qd, dtype=np.float32) / qd)).astype(np.float32)
        cos = np.ones((nd, T), np.float32)
        sin = np.zeros((nd, T), np.float32)
        if sample:
            for i in range(nd):
                b, j = i // blk, i % blk
                pos = row if b < 2 else col
                ang = (pos * freqs[j]).astype(np.float32)
                cos[i] = np.cos(ang)
                sin[i] = np.sin(ang) * (-1.0 if b % 2 == 0 else 1.0)
        return cos, sin
    cs, ss = tab(64, 16)
    cm, sm = tab(32, 8)
    ropeS = np.zeros((128, 2, T), np.float32)
    ropeS[0:64, 0], ropeS[64:128, 0] = cs, cs
    ropeS[0:64, 1], ropeS[64:128, 1] = ss, ss
    ropeM = np.stack([cm, sm], axis=1).astype(np.float32)
    return ropeS, ropeM


def _perm(n, blk):
    idx = np.arange(n)
    b = idx // blk
    return np.where(b % 2 == 0, idx + blk, idx - blk)


def _host_prep(inp):
    f = lambda a: np.ascontiguousarray(np.asarray(a, dtype=np.float32))
    w_in = f(inp["w_in"])
    pS = _perm(64, 16)
    pM = _perm(32, 8)
    cols = []
    cols += list(range(0, 256))
    cols += list(range(256, 384))
    cols += list(range(320, 384)) + list(range(384, 416))
    cols += list(range(320, 384)) + [384 + int(p) for p in pM]
    qs = np.arange(416, 672)
    cols += list(qs)
    cols += [416 + (i // 64) * 64 + int(pS[i % 64]) for i in range(256)]
    for kh in range(2):
        k0 = 672 + kh * 64
        cols += list(range(k0, k0 + 64)) * 2
        cols += [k0 + int(p) for p in pS] * 2
    cols += list(range(800, 928))
    cols += list(range(928, 1440))
    assert len(cols) == NWIN
    w_in_ext = np.ascontiguousarray(w_in[:, :, np.array(cols)])
    w_uq = f(inp["mla_w_uq"])
    ucols = []
    for h in range(8):
        b0 = h * 96
        ucols += list(range(b0, b0 + 96))
        ucols += list(range(b0, b0 + 64)) + [b0 + 64 + int(p) for p in pM]
    w_uq_ext = np.ascontiguousarray(w_uq[:, :, np.array(ucols)])

    def vec_layers():
        out = np.zeros((128, DEPTH, NL), np.float32)
        for l in range(DEPTH):
            out[:, l, V_ADAB:V_ADAB + 48] = f(inp["ada_b"])[l].reshape(48, 128).T
            out[:, l, V_N1G:V_N1G + 8] = f(inp["norm1_g"])[l].reshape(8, 128).T
            out[:, l, V_N2G:V_N2G + 8] = f(inp["norm2_g"])[l].reshape(8, 128).T
            out[:, l, V_QNG:V_QNG + 2] = f(inp["mla_q_norm_g"])[l].reshape(2, 128).T
            out[:, l, V_KVG] = f(inp["mla_kv_norm_g"])[l]
            out[:, l, V_CDW:V_CDW + 62] = f(inp["conv_dw_w"])[l].reshape(31, 2, 128).transpose(2, 0, 1).reshape(128, 62)
            out[:, l, V_CDB:V_CDB + 2] = f(inp["conv_dw_b"])[l].reshape(2, 128).T
            out[:, l, V_CLG:V_CLG + 2] = f(inp["conv_ln_g"])[l].reshape(2, 128).T
            out[:, l, V_CLB:V_CLB + 2] = f(inp["conv_ln_b"])[l].reshape(2, 128).T
            out[:, l, V_FDW:V_FDW + 132] = f(inp["ffn_dw_w"])[l].reshape(3, 44, 128).transpose(2, 0, 1).reshape(128, 132)
            out[:, l, V_FDB:V_FDB + 44] = f(inp["ffn_dw_b"])[l].reshape(44, 128).T
            out[:, l, V_SINK:V_SINK + 4] = f(inp["swa_sink"])[l][None, :]
        return out.reshape(128, DEPTH * NL)
    vl = vec_layers()
    fng = f(inp["final_norm_g"]).reshape(8, 128).T

    shared = dict(
        ident=np.eye(128, dtype=np.float32),
        ada_w=f(inp["ada_w"]), w_in_ext=w_in_ext, w_uq_ext=w_uq_ext, w_ukv=f(inp["mla_w_ukv"]),
        pw2=f(inp["conv_w_pw2"]), w_out=f(inp["w_out"]), w_up=f(inp["ffn_w_up"]), w_down=f(inp["ffn_w_down"]),
    )
    kaug = np.zeros((9, NKEY), np.float32)
    for j in range(8):
        kaug[j, j * 256:(j + 1) * 256] = 1.0
    kaug[8, :] = -BIGM
    kl = np.arange(128)[:, None]
    ql = np.arange(128)[None, :]

    def core_inputs(sample, xtok, cvec, caches):
        vecs = np.zeros((128, NV), np.float32)
        vecs[:, :DEPTH * NL] = vl
        vecs[:, V_FNG:V_FNG + 8] = fng
        vecs[:, V_CVEC:V_CVEC + 8] = cvec.reshape(8, 128).T
        vecs[:, V_FLAG] = 1.0 if sample else 0.0
        vecs[:, V_CTXB] = 0.0 if sample else NEG
        ropeS, ropeM = _rope_tables(sample)
        qaug = np.zeros((9, T), np.float32)
        smask = np.zeros((128, 4, 128), np.float32)
        if sample:
            lo = np.where(ql <= kl, 0.0, NEG).astype(np.float32)
            hi = np.where(kl <= ql, 0.0, NEG).astype(np.float32)
            smask[:, 0], smask[:, 1], smask[:, 2], smask[:, 3] = lo, hi, lo, hi
        else:
            for j in range(8):
                qaug[j, j * 256:(j + 1) * 256] = BIGM
            qaug[8, :] = 1.0
            smask[:, 0] = NEG
            smask[:, 3] = NEG
        smask = np.ascontiguousarray(np.concatenate([smask, smask], axis=2))
        d = dict(x=np.ascontiguousarray(xtok), vecs=vecs, ropeS=ropeS, ropeM=ropeM, qaug=qaug, kaug=kaug,
                 smask=smask, cckv=caches[0], ckr=caches[1], ck=caches[2], cv=caches[3])
        d.update(shared)
        return d

    xp = f(inp["x_prompt"])
    xs = f(inp["x_sample"])
    cc = [f(inp["cache_mla_ckv"]), f(inp["cache_mla_krope"]),
          f(inp["cache_swa_k"]).reshape(2, DEPTH, L, 128), f(inp["cache_swa_v"]).reshape(2, DEPTH, L, 128)]
    zc = [np.zeros((DEPTH, L, 128), np.float32), np.zeros((DEPTH, L, 32), np.float32),
          np.zeros((DEPTH, L, 128), np.float32), np.zeros((DEPTH, L, 128), np.float32)]
    c = f(inp["c"])
    cctx = f(inp["c_ctx"])
    maps = []
    for b in range(2):
        maps.append(core_inputs(True, xs[b], c[b], [a[b] for a in cc]))
    for q in range(4):
        maps.append(core_inputs(False, xp[q * 8:(q + 1) * 8].reshape(T, D), cctx, zc))
    maps.append(maps[2])
    maps.append(maps[3])
    return maps


_NC_CACHE = {}
_TEST_CORES = None


def kernel(**inputs):
    maps = _host_prep(inputs)
    if "nc" not in _NC_CACHE:
        _NC_CACHE["nc"] = build_program(DEPTH)
    nc = _NC_CACHE["nc"]
    if _TEST_CORES is not None:
        sub = [maps[i] for i in _TEST_CORES]
        rr = run_bass_kernel_spmd(nc, sub, core_ids=list(range(len(sub)))).results
        r = [rr[_TEST_CORES.index(i)] if i in _TEST_CORES else rr[0 if i < 2 else len(sub) - 1] for i in range(8)]
    else:
        res = run_bass_kernel_spmd(nc, maps, core_ids=list(range(8)))
        r = res.results
    y_sample = np.stack([r[0]["y"], r[1]["y"]], axis=0).astype(np.float32)
    y_prompt = np.concatenate([r[2 + q]["y"].reshape(8, 256, D) for q in range(4)], axis=0).astype(np.float32)

    def st(name, w):
        parts = []
        for q in range(4):
            a = r[2 + q][name].reshape(DEPTH, 8, 256, w).transpose(1, 0, 2, 3)
            parts.append(a)
        return np.concatenate(parts, axis=0).astype(np.float32)
    s_ckv = st("st_ckv", 128)
    s_kr = st("st_kr", 32)
    s_k = st("st_k", 128).reshape(32, DEPTH, 256, 2, 64)
    s_v = st("st_v", 128).reshape(32, DEPTH, 256, 2, 64)
    return (y_prompt, y_sample, s_ckv, s_kr, s_k, s_v)
```

```python
import numpy as np
import concourse.bass as bass
import concourse.mybir as mybir
from concourse.bass_utils import run_bass_kernel_spmd

F32 = mybir.dt.float32
BF16 = mybir.dt.bfloat16
AF = mybir.ActivationFunctionType
ALU = mybir.AluOpType

D = 1024
T = 2048
DEPTH = 4
L = 512
NKEY = T + L
NKT = NKEY // 128
EPS = 1e-6
MLA_SCALE = 96.0 ** -0.5
SWA_SCALE = 64.0 ** -0.5
DFF = 2816
NEG = -30000.0
BIGM = 512.0

C_CQ, C_CKV, C_KRT, C_KRPT, C_QS, C_QSP, C_K0, C_K0P, C_K1, C_K1P, C_VS, C_CU = (
    0, 256, 384, 480, 576, 832, 1088, 1216, 1344, 1472, 1600, 1728)
NWIN = 2240
V_ADAB, V_N1G, V_N2G, V_QNG, V_KVG, V_CDW, V_CDB, V_CLG, V_CLB, V_FDW, V_FDB, V_SINK = (
    0, 48, 56, 64, 66, 67, 129, 131, 133, 135, 267, 311)
NL = 315
V_FNG, V_CVEC, V_FLAG, V_CTXB = 4 * NL, 4 * NL + 8, 4 * NL + 16, 4 * NL + 17
NV = 4 * NL + 18

ENGS = ("pe", "act", "dve", "pool", "sp")


class _Stop(Exception):
    pass


class V:
    def __init__(self, ap, t):
        self.ap, self.t = ap, t

    def __getitem__(self, k):
        return V(self.ap[k], self.t)

    def re(self, pat, **kw):
        return V(self.ap.rearrange(pat, **kw), self.t)

    def bc(self, axis, shape):
        return V(self.ap.unsqueeze(axis).to_broadcast(list(shape)), self.t)


class Tile(V):
    def __init__(self, ap, space, lo, hi, ranges=None):
        V.__init__(self, ap, self)
        self.space, self.lo, self.hi = space, lo, hi
        self.ranges = ranges if ranges is not None else [(lo, hi)]
        self.w, self.r = {}, {}
        self.ov = [self]
        self.dsem = None
        self.dcum = 0


def _ap(x):
    return x.ap if isinstance(x, V) else x


class Prog:
    def __init__(self, nc):
        self.nc = nc
        self.ins = {e: [] for e in ENGS}
        self.waited = {e: {} for e in ENGS}
        self.tiles = []
        self.ndsem = 0
        self.dstate = {}
        self.count = 0
        self.limit = None

    def register(self, t):
        for u in self.tiles:
            if u.space == t.space and u.lo < t.hi and t.lo < u.hi:
                if any(a < d and c < b for (a, b) in u.ranges for (c, d) in t.ranges):
                    u.ov.append(t)
                    t.ov.append(u)
        self.tiles.append(t)
        return t

    def unregister(self, ts):
        pass

    def _deps(self, reads, writes):
        d = {}

        def add(dic):
            for k, v in dic.items():
                if d.get(k, -1) < v:
                    d[k] = v
        for t in reads:
            for u in t.ov:
                add(u.w)
        for t in writes:
            for u in t.ov:
                add(u.w)
                add(u.r)
        return d

    def _waits(self, eng, d):
        waits = []
        for k, v in d.items():
            if k == eng and eng in ("pe", "sp"):
                continue
            if self.waited[eng].get(k, -1) >= v:
                continue
            self.waited[eng][k] = v
            waits.append((k, v))
            if k in ENGS:
                self.ins[k][v]["marked"] = True
        return waits

    def _tick(self):
        self.count += 1
        if self.limit is not None and self.count > self.limit:
            raise _Stop

    def op(self, eng, fn, reads, writes):
        self._tick()
        reads = [x.t for x in reads if isinstance(x, V)]
        writes = [x.t for x in writes if isinstance(x, V)]
        waits = self._waits(eng, self._deps(reads, writes))
        idx = len(self.ins[eng])
        self.ins[eng].append(dict(fn=fn, waits=waits, marked=False, dma=None))
        for t in reads:
            t.r[eng] = idx
        for t in writes:
            t.w[eng] = idx

    def dma(self, q, out, in_):
        self._tick()
        reads = [in_.t] if isinstance(in_, V) else []
        writes = [out.t] if isinstance(out, V) else []
        waits = self._waits(q, self._deps(reads, writes))
        st = writes[0] if writes else reads[0]
        rk = (st.space, st.lo, st.hi)
        if rk not in self.dstate:
            self.dstate[rk] = [self.ndsem, 0]
            self.ndsem += 1
        ds = self.dstate[rk]
        ds[1] += 16
        st.dsem, st.dcum = ds[0], ds[1]
        key = ("d", st.dsem)
        o, i = _ap(out), _ap(in_)
        self.ins[q].append(dict(fn=lambda e: e.dma_start(out=o, in_=i), waits=waits, marked=False, dma=st.dsem))
        for t in reads:
            t.r[key] = max(t.r.get(key, 0), st.dcum)
        for t in writes:
            t.w[key] = max(t.w.get(key, 0), st.dcum)

    def emit(self, stack):
        nc = self.nc
        sems = {e: stack.enter_context(nc.semaphore("s_" + e)) for e in ENGS}
        dsems = [stack.enter_context(nc.semaphore("d%d" % i)) for i in range(self.ndsem)]
        for e in ENGS:
            c = 0
            for rec in self.ins[e]:
                if rec["marked"]:
                    c += 1
                rec["sv"] = c
        final = {v[0]: v[1] for v in self.dstate.values()}
        block = stack.enter_context(nc.Block())

        def run(e, eo):
            for rec in self.ins[e]:
                for k, v in rec["waits"]:
                    if k in ENGS:
                        eo.wait_ge(sems[k], self.ins[k][v]["sv"])
                    else:
                        eo.wait_ge(dsems[k[1]], v)
                ins = rec["fn"](eo)
                if rec["dma"] is not None:
                    ins.then_inc(dsems[rec["dma"]], 16)
                elif rec["marked"]:
                    ins.then_inc(sems[e], 1)
            if e == "sp":
                for k, v in final.items():
                    eo.wait_ge(dsems[k], v)
                for k in ("pe", "act", "dve", "pool"):
                    n = self.ins[k][-1]["sv"] if self.ins[k] else 0
                    if n:
                        eo.wait_ge(sems[k], n)

        @block.tensor
        def _(eo):
            run("pe", eo)

        @block.scalar
        def _(eo):
            run("act", eo)

        @block.vector
        def _(eo):
            run("dve", eo)

        @block.gpsimd
        def _(eo):
            run("pool", eo)

        @block.sync
        def _(eo):
            run("sp", eo)


def build_program(depth=DEPTH, stop=None):
    nc = bass.Bass("TRN2", target_bir_lowering=False)
    P = Prog(nc)
    if stop is not None and stop.startswith('n'):
        P.limit = int(stop[1:])

    def din(name, shape):
        return nc.dram_tensor(name, list(shape), F32, kind="ExternalInput").ap()

    def dout(name, shape):
        return nc.dram_tensor(name, list(shape), F32, kind="ExternalOutput").ap()

    x_d = din("x", [T, D])
    vecs_d = din("vecs", [128, NV])
    ident_d = din("ident", [128, 128])
    ropeS_d = din("ropeS", [128, 2, T])
    ropeM_d = din("ropeM", [32, 2, T])
    qaug_d = din("qaug", [9, T])
    kaug_d = din("kaug", [9, NKEY])
    smask_d = din("smask", [128, 4, 256])
    cckv_d = din("cckv", [DEPTH, L, 128])
    ckr_d = din("ckr", [DEPTH, L, 32])
    ck_d = din("ck", [DEPTH, L, 128])
    cv_d = din("cv", [DEPTH, L, 128])
    adaw_d = din("ada_w", [DEPTH, D, 6 * D])
    win_d = din("w_in_ext", [DEPTH, D, NWIN])
    wuq_d = din("w_uq_ext", [DEPTH, 256, 1536])
    wukv_d = din("w_ukv", [DEPTH, 128, 1024])
    pw2_d = din("pw2", [DEPTH, 256, 256])
    wout_d = din("w_out", [DEPTH, D, D])
    wup_d = din("w_up", [DEPTH, D, 2 * DFF])
    wdn_d = din("w_down", [DEPTH, DFF, D])
    y_d = dout("y", [T, D])
    sckv_d = dout("st_ckv", [DEPTH, T, 128])
    skr_d = dout("st_kr", [DEPTH, T, 32])
    sk_d = dout("st_k", [DEPTH, T, 128])
    sv_d = dout("st_v", [DEPTH, T, 128])

    from contextlib import ExitStack
    stack = ExitStack()
    ARENA = 211200
    arena_h = stack.enter_context(nc.sbuf_tensor("arena", [128, ARENA // 4], F32))
    base0 = nc.sbuf_base - (ARENA // 4) * 4
    arena_t = P.register(Tile(arena_h[:], "sb", 0, ARENA))
    ps_t = stack.enter_context(nc.psum_tensor("ps", [128, 4096], F32))
    ps_all = ps_t[:]

    class Mem:
        def __init__(self):
            self.top = 0
            self.n = 0

        def alloc(self, shape, dt):
            nb = int(np.prod(shape[1:])) * (2 if dt == BF16 else 4)
            nb = (nb + 63) // 64 * 64
            off = self.top
            self.top += nb
            assert self.top <= ARENA, ("SBUF overflow", self.top)
            self.n += 1
            h = nc.alloc_sbuf_tensor_at("t%d" % self.n, list(shape), dt, offset=base0 + off)
            ap = h.ap() if hasattr(h, "ap") else h[:]
            return P.register(Tile(ap, "sb", off, off + nb))

        def alloc_at(self, off, shape, dt):
            nb = int(np.prod(shape[1:])) * (2 if dt == BF16 else 4)
            nb = (nb + 63) // 64 * 64
            assert off % 64 == 0 and off + nb <= ARENA
            self.n += 1
            h = nc.alloc_sbuf_tensor_at("t%d" % self.n, list(shape), dt, offset=base0 + off)
            ap = h.ap() if hasattr(h, "ap") else h[:]
            return P.register(Tile(ap, "sb", off, off + nb))

    M = Mem()

    def psum(b0, nb=1, c0=0, c1=512):
        if nb == 1:
            ap = ps_all[:, b0 * 512 + c0: b0 * 512 + c1]
            return P.register(Tile(ap, "ps", b0 * 2048 + c0 * 4, b0 * 2048 + c1 * 4))
        ap = ps_all.rearrange("p (b n) -> p b n", b=8)[:, b0:b0 + nb, :]
        return P.register(Tile(ap, "ps", b0 * 2048, (b0 + nb) * 2048,
                               ranges=[(b * 2048 + c0 * 4, b * 2048 + c1 * 4) for b in range(b0, b0 + nb)]))

    PB = [psum(b) for b in range(8)]
    PBN = [psum(b, 1, 0, 258) for b in range(8)]
    PS03 = psum(0, 4, 0, 258)
    PS47 = psum(4, 4, 0, 258)

    def MM(out, lhsT, rhs, start=True, stop=True):
        o, l, r = _ap(out), _ap(lhsT), _ap(rhs)
        P.op("pe", lambda e: e.matmul(o, lhsT=l, rhs=r, start=start, stop=stop), [lhsT, rhs], [out])

    def TR(out, in_, ident):
        o, i, d = _ap(out), _ap(in_), _ap(ident)
        P.op("pe", lambda e: e.transpose(o, i, d), [in_, ident], [out])

    def ACT(out, in_, func, bias=None, scale=None):
        o, i = _ap(out), _ap(in_)
        kw = {}
        if bias is not None:
            kw["bias"] = _ap(bias)
        if scale is not None:
            kw["scale"] = _ap(scale)
        P.op("act", lambda e: e.activation(out=o, in_=i, func=func, **kw), [in_, bias, scale], [out])

    EO = {"dve": "dve", "pool": "pool"}

    def TT(eng, out, in0, in1, op):
        o, a, b = _ap(out), _ap(in0), _ap(in1)
        P.op(eng, lambda e: e.tensor_tensor(out=o, in0=a, in1=b, op=op), [in0, in1], [out])

    def TS(eng, out, in0, s1, op0, s2=None, op1=None):
        o, a, c1, c2 = _ap(out), _ap(in0), _ap(s1), _ap(s2)
        if op1 is None:
            P.op(eng, lambda e: e.tensor_scalar(out=o, in0=a, scalar1=c1, scalar2=None, op0=op0), [in0, s1], [out])
        else:
            P.op(eng, lambda e: e.tensor_scalar(out=o, in0=a, scalar1=c1, scalar2=c2, op0=op0, op1=op1),
                 [in0, s1, s2], [out])

    def STT(eng, out, in0, scalar, in1, op0, op1):
        o, a, s, b = _ap(out), _ap(in0), _ap(scalar), _ap(in1)
        P.op(eng, lambda e: e.scalar_tensor_tensor(out=o, in0=a, scalar=s, in1=b, op0=op0, op1=op1),
             [in0, scalar, in1], [out])

    def CP(eng, out, in_):
        o, i = _ap(out), _ap(in_)
        P.op(eng, lambda e: e.tensor_copy(out=o, in_=i), [in_], [out])

    def RECIP(out, in_):
        o, i = _ap(out), _ap(in_)
        P.op("dve", lambda e: e.reciprocal(out=o, in_=i), [in_], [out])

    def MEMSET(eng, out, val):
        o = _ap(out)
        P.op(eng, lambda e: e.memset(o, val), [], [out])

    def DMA(q, out, in_):
        P.dma(q, out, in_)

    cpt = [0]

    def EVAC(out, in_):
        cpt[0] += 1
        if cpt[0] % 2:
            ACT(out, in_, AF.Identity)
        else:
            CP("dve", out, in_)

    x = M.alloc([128, 8, T], F32)
    vecs = M.alloc([128, NV], F32)
    ident = M.alloc([128, 128], F32)
    identb = M.alloc([128, 128], BF16)
    ones_f = M.alloc([128, 128], F32)
    smask = M.alloc([128, 4, 256], BF16)
    modall = M.alloc([128, DEPTH * 48], F32)
    lv = M.alloc([128, 48], F32)
    esink = M.alloc([128, 16], F32)
    scb = M.alloc([128, 8], BF16)
    epsc = M.alloc([128, 1], F32)
    onesb = M.alloc([128, 128], BF16)
    PERSIST_TOP = M.top

    for a0 in range(0, ARENA // 4, 8192):
        a1 = min(ARENA // 4, a0 + 8192)
        eng_ = "dve" if (a0 // 8192) % 2 == 0 else "pool"
        o_ = arena_h[:, a0:a1]
        P.op(eng_, (lambda o: (lambda e: e.memset(o, 0.0)))(o_), [], [arena_t])
    DMA("sp", vecs, vecs_d)
    DMA("sp", ident, ident_d)
    DMA("pool", identb, ident_d)
    DMA("pool", smask, smask_d)
    MEMSET("dve", ones_f, 1.0)
    MEMSET("dve", epsc, EPS)
    MEMSET("dve", onesb, 1.0)

    def vcol(c, n=1):
        return vecs[:, c:c + n]

    flag = vcol(V_FLAG)
    ctxb = vcol(V_CTXB)

    ACT(scb, vcol(V_CVEC, 8), AF.Silu)
    TOPB = (ARENA // 64) * 64
    o_ = TOPB
    top_tiles = {}
    for nm, shp in (("wdn0", [128, 22, 256]), ("wdn1", [128, 22, 256]), ("wup0", [128, 8, 512]), ("wup1", [128, 8, 512]),
                    ("ada0", [128, 8, 256]), ("ada1", [128, 8, 256])):
        nb_ = int(np.prod(shp[1:])) * 2
        o_ -= nb_
        top_tiles[nm] = M.alloc_at(o_, shp, BF16)
    TOP_LIMIT = o_
    wup = [top_tiles["wup0"], top_tiles["wup1"]]
    wdn = [top_tiles["wdn0"], top_tiles["wdn1"]]
    ada_slots = [top_tiles["ada0"], top_tiles["ada1"]]
    modps = P.register(Tile(ps_all[:, 7 * 512 + 300: 7 * 512 + 300 + DEPTH * 48], "ps", 7 * 2048, 8 * 2048))
    adak = [0]

    def mod_steps(l):
        aw = adaw_d[l].rearrange("(kt p) c -> p kt c", p=128)
        k0 = adak[0]
        adak[0] += 24

        def dma(ch):
            DMA("pool", ada_slots[(k0 + ch) % 2], aw[:, :, ch * 256:(ch + 1) * 256])

        def mm(ch):
            sl = ada_slots[(k0 + ch) % 2]
            for j in range(2):
                col = l * 48 + ch * 2 + j
                for kt in range(8):
                    MM(modps[:, col:col + 1], sl[:, kt, j * 128:(j + 1) * 128], scb[:, kt:kt + 1],
                       start=(kt == 0), stop=(kt == 7))
        steps = [lambda: (dma(0), dma(1))]
        for ch in range(24):
            def f(ch=ch):
                mm(ch)
                if ch + 2 < 24:
                    dma(ch + 2)
                if ch == 23:
                    o_, a_, b_ = modall[:, l * 48:(l + 1) * 48], modps[:, l * 48:(l + 1) * 48], vcol(l * NL + V_ADAB, 48)
                    P.op("dve", lambda e: e.tensor_tensor(out=o_.ap, in0=a_.ap, in1=b_.ap, op=ALU.add),
                         [a_, b_], [o_, PB[7]])
            steps.append(f)
        return steps

    def modulation(l):
        for f_ in mod_steps(l):
            f_()
    M.top = PERSIST_TOP
    xs = [M.alloc([128, D], F32) for _ in range(2)]
    for tb in range(16):
        st = xs[tb % 2]
        DMA("sp", st, x_d[tb * 128:(tb + 1) * 128, :])
        for hb in range(2):
            pb = PB[(tb * 2 + hb) % 4]
            for j in range(4):
                ft = hb * 4 + j
                TR(pb[:, j * 128:(j + 1) * 128], st[:, ft * 128:(ft + 1) * 128], ident)
            EVAC(x[:, hb * 4:(hb + 1) * 4, tb * 128:(tb + 1) * 128], pb.re("p (j n) -> p j n", j=4))

    modulation(0)
    for l in range(depth):
        ACT(esink[:, l * 4:(l + 1) * 4], vcol(l * NL + V_SINK, 4), AF.Exp)

    def rstd_from(ps_sum, out, nfeat):
        ACT(out, ps_sum, AF.Ln, bias=epsc, scale=1.0 / nfeat)
        ACT(out, out, AF.Exp, scale=-0.5)

    def norm_chunk(tok0, A, B, hdst, sq, tmp, rs, psn):
        for ft in range(8):
            if ft % 2 == 0:
                ACT(sq[ft % 4], x[:, ft, tok0:tok0 + 512], AF.Square)
            else:
                TT("dve", sq[ft % 4], x[:, ft, tok0:tok0 + 512], x[:, ft, tok0:tok0 + 512], ALU.mult)
            MM(psn, onesb, sq[ft % 4], start=(ft == 0), stop=(ft == 7))
        rstd_from(psn, rs, float(D))
        for ft in range(8):
            TT("dve", tmp[ft % 2], x[:, ft, tok0:tok0 + 512], rs, ALU.mult)
            ACT(hdst(ft), tmp[ft % 2].re("p (s n) -> p s n", s=2), AF.Identity, bias=B[:, ft:ft + 1],
                scale=A[:, ft:ft + 1])

    def out_transposed(src, rows, r0, dst_d, tok0, ost, pbank):
        for b in range(4):
            TR(pbank[:, b * rows:(b + 1) * rows], src[r0:r0 + rows, b * 128:(b + 1) * 128], ident[r0:r0 + rows, r0:r0 + rows])
        EVAC(ost[:, :, 0:rows], pbank[:, 0:4 * rows].re("p (b f) -> p b f", b=4))
        DMA("sp", dst_d[tok0:tok0 + 512, :].rearrange("(b p) f -> p b f", p=128), ost[:, :, 0:rows])

    def layers():
      for l in range(depth):
        if stop == 'pro':
            raise _Stop
        vb = l * NL
        mod = modall[:, l * 48:(l + 1) * 48]
        B1, G1, G2 = mod[:, 0:8], mod[:, 16:24], mod[:, 40:48]
        A1, A2, A2f, B2f = lv[:, 0:8], lv[:, 8:16], lv[:, 16:24], lv[:, 24:32]
        B2 = mod[:, 24:32]
        STT("dve", A1, mod[:, 8:16], 1.0, vcol(vb + V_N1G, 8), ALU.add, ALU.mult)
        STT("dve", A2, mod[:, 32:40], 1.0, vcol(vb + V_N2G, 8), ALU.add, ALU.mult)
        TS("dve", A2f, A2, flag, ALU.mult)
        TS("dve", B2f, B2, flag, ALU.mult)

        M.top = PERSIST_TOP
        omix = M.alloc([128, 8, T], BF16)
        OMIX_TOP = M.top
        rs1 = M.alloc([128, T], F32)
        RS1_TOP = M.top
        cqn = M.alloc([128, 2, T], BF16)
        ckvT = M.alloc([128, NKEY], BF16)
        krT = M.alloc([128, NKEY], BF16)
        wuq = M.alloc([128, 2, 1536], BF16)
        wukv = M.alloc([128, 1024], BF16)
        cst = M.alloc([128, 4, 128], F32)
        cst2 = M.alloc([128, 4, 96], F32)
        A1OUT_TOP = M.top
        win1 = M.alloc([128, 8, 576], BF16)
        hb = [M.alloc([128, 8, 2, 258], BF16) for _ in range(2)]
        tabMc = M.alloc([128, 2, 512], F32)
        sq = [M.alloc([128, 512], BF16) for _ in range(4)]
        tmp = [M.alloc([128, 512], F32) for _ in range(2)]
        rs = M.alloc([128, 512], F32)
        cqf = M.alloc([128, 2, 512], F32)
        stf = M.alloc([128, 512], F32)
        uu = M.alloc([128, 512], F32)
        vv = M.alloc([128, 512], F32)
        ost = [M.alloc([128, 4, 128], F32) for _ in range(2)]
        DMA("pool", win1, win_d[l].rearrange("(kt p) c -> p kt c", p=128)[:, :, 0:576])
        DMA("pool", wuq, wuq_d[l].rearrange("(kt p) c -> p kt c", p=128))
        DMA("pool", wukv, wukv_d[l])
        DMA("sp", cst, cckv_d[l].rearrange("(b p) f -> p b f", p=128))
        MEMSET("dve", cst2, 0.0)
        DMA("sp", cst2[:, :, 64:96], ckr_d[l].rearrange("(b p) f -> p b f", p=128))
        for b in range(4):
            TR(PB[7][:, b * 128:(b + 1) * 128], cst[:, b, :], ident)
        EVAC(ckvT[:, T:NKEY], PB[7])
        for b in range(4):
            TR(PB[6][0:96, b * 128:(b + 1) * 128], cst2[:, b, :], ident)
        EVAC(krT[64:96, T:NKEY], PB[6][64:96, :])

        def normc1(g_, c_):
            t0_ = g_ * 1024 + c_ * 512
            norm_chunk(t0_, A1, B1, lambda ft: hb[c_][:, ft, :, 1:257], sq, tmp, rs1[:, t0_:t0_ + 512], PB[4])
        normc1(0, 0)
        normc1(0, 1)
        for g in range(2):
            for c in range(2):
                tok0 = g * 1024 + c * 512
                rhs = lambda kt: hb[c][:, kt, :, 1:257]
                for j in range(2):
                    pb = PB[j]
                    for kt in range(8):
                        MM(pb, win1[:, kt, C_CQ + j * 128:C_CQ + (j + 1) * 128], rhs(kt), kt == 0, kt == 7)
                    CP("dve", cqf[:, j, :], pb)
                    ACT(sq[j], cqf[:, j, :], AF.Square)
                for j in range(2):
                    MM(PB[5], onesb, sq[j], j == 0, j == 1)
                rstd_from(PB[5], rs, 256.0)
                for j in range(2):
                    STT("dve", cqn[:, j, tok0:tok0 + 512], cqf[:, j, :], vcol(vb + V_QNG + j), rs, ALU.mult, ALU.mult)
                if stop in ('A1b', 'cnt'):
                    print('count at A1b', P.count)
                if stop == 'A1b':
                    raise _Stop
                pb = PB[2]
                for kt in range(8):
                    MM(pb, win1[:, kt, C_CKV:C_CKV + 128], rhs(kt), kt == 0, kt == 7)
                ACT(sq[0], pb, AF.Square)
                MM(PB[5], onesb, sq[0], True, True)
                rstd_from(PB[5], rs, 128.0)
                STT("dve", stf, pb, vcol(vb + V_KVG), rs, ALU.mult, ALU.mult)
                CP("pool", ckvT[:, tok0:tok0 + 512], stf)
                out_transposed(stf, 128, 0, sckv_d[l], tok0, ost[0], PB[6])
                if stop in ('A1c', 'cnt'):
                    print('count at A1c', P.count)
                if stop == 'A1c':
                    raise _Stop
                pa, pbb = PB[3], PB[7]
                for kt in range(8):
                    MM(pa[0:96, :], win1[:, kt, C_KRT:C_KRT + 96], rhs(kt), kt == 0, kt == 7)
                for kt in range(8):
                    MM(pbb[0:96, :], win1[:, kt, C_KRPT:C_KRPT + 96], rhs(kt), kt == 0, kt == 7)
                DMA("sp", tabMc[64:96], ropeM_d[:, :, tok0:tok0 + 512])
                CP("dve", stf[64:96, :], pa[64:96, :])
                TT("dve", uu[64:96, :], pa[64:96, :], tabMc[64:96, 0, :], ALU.mult)
                TT("dve", vv[64:96, :], pbb[64:96, :], tabMc[64:96, 1, :], ALU.mult)
                TT("pool", krT[64:96, tok0:tok0 + 512], uu[64:96, :], vv[64:96, :], ALU.add)
                out_transposed(stf, 32, 64, skr_d[l], tok0, ost[1], PB[6])
                if g == 0:
                    normc1(1, c)

        if stop in ('A1', 'cnt'):
            print('count at A1', P.count)
        if stop == 'A1':
            raise _Stop
        M.top = A1OUT_TOP
        tabM = M.alloc([128, 2, T], F32)
        Qs_ = [M.alloc([128, T], BF16) for _ in range(2)]
        Ks_ = [M.alloc([128, NKEY], BF16) for _ in range(2)]
        Vs_ = [M.alloc([128, NKT, 128], BF16) for _ in range(2)]
        Pb = [M.alloc([128, 512], BF16) for _ in range(3)]
        Rb = M.alloc([128, 512], F32)
        uu = M.alloc([128, 512], F32)
        vv = M.alloc([128, 512], F32)
        DMA("sp", tabM[64:96], ropeM_d)
        winv = win_d[l].rearrange("(kt p) c -> p kt c", p=128)
        win2a = M.alloc_at(PERSIST_TOP + 16384, [128, 8, 1024], BF16)
        win2b = M.alloc_at(TOPB - 10240, [128, 8, 640], BF16)
        assert M.top <= TOPB - 10240, M.top
        DMA("pool", win2a, winv[:, :, C_QS:C_QS + 1024])
        DMA("pool", win2b, winv[:, :, C_VS:C_VS + 640])
        for s in range(2):
            MEMSET("pool", Qs_[s][96:128, :], 0.0)
            MEMSET("pool", Ks_[s][96:128, :], 0.0)
            DMA("pool", Qs_[s][96:105, :], qaug_d)
            DMA("pool", Ks_[s][96:105, :], kaug_d)
            MEMSET("pool", Vs_[s][:, :, 64:128], 1.0)
        def build_steps(h):
            s_ = h % 2
            Qh, Kh, Vh = Qs_[s_], Ks_[s_], Vs_[s_]
            steps = []
            for kc in range(5):
                def f(kc=kc):
                    pb = PB[5 + kc % 2]
                    MM(pb[0:64, :], wukv[:, h * 128:h * 128 + 64], ckvT[:, kc * 512:(kc + 1) * 512])
                    CP("dve", Kh[0:64, kc * 512:(kc + 1) * 512], pb[0:64, :])
                steps.append(f)
            steps.append(lambda: CP("pool", Kh[64:96, :], krT[64:96, :]))
            for k0 in range(0, NKT, 8):
                def f(k0=k0):
                    n = min(8, NKT - k0)
                    pb = PB[7]
                    for j in range(n):
                        MM(pb[:, j * 64:(j + 1) * 64], ckvT[:, (k0 + j) * 128:(k0 + j + 1) * 128],
                           wukv[:, h * 128 + 64:h * 128 + 128])
                    CP("dve", Vh[:, k0:k0 + n, 0:64], pb[:, 0:n * 64].re("p (j d) -> p j d", j=n))
                steps.append(f)
            for c in range(4):
                def f(c=c):
                    pa, pbb = PB[5], PB[6]
                    for kt in range(2):
                        MM(pa[0:96, :], wuq[:, kt, h * 192:h * 192 + 96], cqn[:, kt, c * 512:(c + 1) * 512], kt == 0, kt == 1)
                    for kt in range(2):
                        MM(pbb[0:96, :], wuq[:, kt, h * 192 + 96:h * 192 + 192], cqn[:, kt, c * 512:(c + 1) * 512], kt == 0, kt == 1)
                    CP("dve", Qh[0:64, c * 512:(c + 1) * 512], pa[0:64, :])
                    TT("dve", uu[64:96, :], pa[64:96, :], tabM[64:96, 0, c * 512:(c + 1) * 512], ALU.mult)
                    TT("dve", vv[64:96, :], pbb[64:96, :], tabM[64:96, 1, c * 512:(c + 1) * 512], ALU.mult)
                    TT("pool", Qh[64:96, c * 512:(c + 1) * 512], uu[64:96, :], vv[64:96, :], ALU.add)
                steps.append(f)
            return steps

        for f_ in build_steps(0):
            f_()
        for h in range(8):
            s = h % 2
            Qh, Kh, Vh = Qs_[s], Ks_[s], Vs_[s]
            pending = build_steps(h + 1) if h + 1 < 8 else []
            it = 0
            for c in range(4):
                po = PB[3 + (h * 4 + c) % 2]
                qv = Qh[:, c * 512:(c + 1) * 512]

                def score(kt):
                    MM(PB[kt % 3], Kh[:, kt * 128:(kt + 1) * 128], qv)
                score(0)
                score(1)
                for kt in range(NKT):
                    ACT(Pb[kt % 3], PB[kt % 3], AF.Exp, scale=MLA_SCALE)
                    if kt + 2 < NKT:
                        score(kt + 2)
                    MM(po, Vh[:, kt, :], Pb[kt % 3], kt == 0, kt == NKT - 1)
                    it += 1
                    if pending and it % 5 == 0:
                        pending.pop(0)()
                RECIP(Rb[64:128, :], po[64:128, :])
                p0 = (h % 2) * 64
                TT("dve", omix[p0:p0 + 64, h // 2, c * 512:(c + 1) * 512], po[0:64, :], Rb[64:128, :], ALU.mult)
            while pending:
                pending.pop(0)()

        if stop == 'BM':
            raise _Stop
        M.top = RS1_TOP
        qsw = M.alloc([128, 2, T], BF16)
        ksd = M.alloc([128, 2, NKEY], BF16)
        vsx = M.alloc([128, NKT, 2, 128], BF16)
        ypad = M.alloc([128, 2, 8, 286], BF16)
        A2OUT_TOP = M.top
        hb = [M.alloc([128, 8, 2, 258], BF16) for _ in range(2)]
        tabS = [M.alloc([128, 2, 512], F32) for _ in range(1)]
        tmp = [M.alloc([128, 512], F32) for _ in range(2)]
        stf = M.alloc([128, 512], F32)
        uu = M.alloc([128, 512], F32)
        vv = M.alloc([128, 512], F32)
        ost = [M.alloc([128, 4, 128], F32) for _ in range(2)]
        MEMSET("pool", vsx[:, :, :, 64:128], 1.0)
        MEMSET("pool", ypad[:, :, 0, 0:15], 0.0)
        MEMSET("pool", ypad[:, :, 7, 271:286], 0.0)
        tsi = 0
        def normc2(g_, c_):
            t0_ = g_ * 1024 + c_ * 512
            for ft in range(8):
                TT("dve", tmp[ft % 2], x[:, ft, t0_:t0_ + 512], rs1[:, t0_:t0_ + 512], ALU.mult)
                ACT(hb[c_][:, ft, :, 1:257], tmp[ft % 2].re("p (s n) -> p s n", s=2), AF.Identity,
                    bias=B1[:, ft:ft + 1], scale=A1[:, ft:ft + 1])
        normc2(0, 0)
        normc2(0, 1)
        for g in range(2):
            for c in range(2):
                tok0 = g * 1024 + c * 512
                cb = g * 2 + c
                rhs = lambda kt: hb[c][:, kt, :, 1:257]
                tb_ = tabS[0]
                tsi += 1
                DMA("sp", tb_, ropeS_d[:, :, tok0:tok0 + 512])

                rtn = [0]

                def rope_tile(ca, cbb, dst, keep=None):
                    pa, pbb = (PB[0], PB[1]) if rtn[0] % 2 == 0 else (PB[2], PB[3])
                    rtn[0] += 1
                    for kt in range(8):
                        MM(pa, win2a[:, kt, ca - C_QS:ca - C_QS + 128], rhs(kt), kt == 0, kt == 7)
                    for kt in range(8):
                        MM(pbb, win2a[:, kt, cbb - C_QS:cbb - C_QS + 128], rhs(kt), kt == 0, kt == 7)
                    if keep is not None:
                        CP("dve", stf[keep:keep + 64, :], pa[keep:keep + 64, :])
                    u_, v_ = (uu, vv) if rtn[0] % 2 == 0 else (tmp[0], tmp[1])
                    TT("dve", u_, pa, tb_[:, 0, :], ALU.mult)
                    TT("dve", v_, pbb, tb_[:, 1, :], ALU.mult)
                    TT("pool", dst, u_, v_, ALU.add)
                rope_tile(C_QS, C_QSP, qsw[:, 0, tok0:tok0 + 512])
                rope_tile(C_QS + 128, C_QSP + 128, qsw[:, 1, tok0:tok0 + 512])
                rope_tile(C_K0, C_K0P, ksd[:, 0, tok0:tok0 + 512], keep=0)
                rope_tile(C_K1, C_K1P, ksd[:, 1, tok0:tok0 + 512], keep=64)
                out_transposed(stf, 128, 0, sk_d[l], tok0, ost[0], PB[6])
            for c in range(2):
                tok0 = g * 1024 + c * 512
                rhs = lambda kt: hb[c][:, kt, :, 1:257]
                pb = PB[2]
                for kt in range(8):
                    MM(pb, win2b[:, kt, 0:128], rhs(kt), kt == 0, kt == 7)
                CP("dve", stf, pb)
                for b in range(4):
                    TR(PB[6][:, b * 128:(b + 1) * 128], stf[:, b * 128:(b + 1) * 128], ident)
                ACT(ost[1], PB[6].re("p (b f) -> p b f", b=4), AF.Identity)
                DMA("sp", sv_d[l][tok0:tok0 + 512, :].rearrange("(b p) f -> p b f", p=128), ost[1])
                kt0 = tok0 // 128
                CP("dve", vsx[:, kt0:kt0 + 4, :, 0:64], ost[1].re("p b (k d) -> p b k d", k=2))
                for j in range(2):
                    pu, pg = (PB[0], PB[1]) if j == 0 else (PB[3], PB[4])
                    for kt in range(8):
                        MM(pu, win2b[:, kt, 128 + j * 128:256 + j * 128], rhs(kt), kt == 0, kt == 7)
                    for kt in range(8):
                        MM(pg, win2b[:, kt, 384 + j * 128:512 + j * 128], rhs(kt), kt == 0, kt == 7)
                    ACT(uu, pg, AF.Sigmoid)
                    sg0 = tok0 // 256
                    TT("dve", ypad[:, j, sg0:sg0 + 2, 15:271], pu.re("p (s n) -> p s n", s=2),
                       uu.re("p (s n) -> p s n", s=2), ALU.mult)
                if g == 0:
                    normc2(1, c)

        if stop == 'A2':
            raise _Stop
        M.top = A2OUT_TOP
        wout = M.alloc_at(TOPB - 16384, [128, 8, D], BF16)
        DMA("pool", wout, wout_d[l].rearrange("(kt p) c -> p kt c", p=128))
        cst = M.alloc([128, 4, 2, 2, 64], F32)
        P2 = [M.alloc([128, 256], BF16) for _ in range(3)]
        R2 = M.alloc([128, 256], F32)
        qzs = [M.alloc([128, 2, 128], BF16) for _ in range(2)]
        for q_ in qzs:
            MEMSET("pool", q_, 0.0)
        diag_t = [M.alloc([128, 128], BF16) for _ in range(62)]
        pw2 = M.alloc([128, 2, 256], BF16)
        ycv = M.alloc([128, 2, 512], F32)
        sqc = M.alloc([128, 1, 512], F32)
        mean = M.alloc([128, 512], F32)
        var = M.alloc([128, 512], F32)
        dd = M.alloc([128, 512], F32)
        yact = M.alloc([128, 2, 512], BF16)
        ckv_ = ck_d[l].rearrange("(b p) (k d) -> p b k d", p=128, k=2)
        for du in range(2):
            for kh in range(2):
                DMA("sp", cst[:, :, kh, du, :], ckv_[:, :, kh, :])
        for kh in range(2):
            for b in range(4):
                TR(PB[7][:, b * 128:(b + 1) * 128], cst[:, b, kh].re("p u d -> p (u d)"), ident)
            EVAC(ksd[:, kh, T:NKEY], PB[7])
        for kh in range(2):
            DMA("pool", vsx[:, 16:20, kh, 0:64], cv_d[l].rearrange("(b p) (k d) -> p b k d", p=128, k=2)[:, :, kh, :])
        DMA("pool", pw2, pw2_d[l].rearrange("(kt p) c -> p kt c", p=128))
        for tap in range(31):
            for ct in range(2):
                if (tap + ct) % 2 == 0:
                    ACT(diag_t[tap * 2 + ct], identb, AF.Identity, scale=vcol(vb + V_CDW + tap * 2 + ct))
                else:
                    TS("dve", diag_t[tap * 2 + ct], identb, vcol(vb + V_CDW + tap * 2 + ct), ALU.mult)
        if stop == 'cnt':
            print('count at BS-prep-end', P.count)
        si = 0
        for kh in range(2):
            for i in range(16):
                if stop == 'cnt' and kh == 0 and i < 2:
                    print('count at swa block', i, P.count)
                keys = []
                if i > 0:
                    keys.append((i - 1, (i % 2) * 2 + 0, False))
                keys.append((i, None, False))
                if i < 15:
                    keys.append((i + 1, (i % 2) * 2 + 1, False))
                for cb in range(4):
                    keys.append((16 + cb, None, True))
                po = PB[3 + (kh * 16 + i) % 2]
                nk = len(keys)

                qz = qzs[(kh * 16 + i) % 2]
                CP("pool", qz[0:64, 0, :], qsw[0:64, kh, i * 128:(i + 1) * 128])
                CP("pool", qz[64:128, 1, :], qsw[64:128, kh, i * 128:(i + 1) * 128])

                def score(n, sb):
                    kt, m, isctx = keys[n]
                    MM(sb, ksd[:, kh, kt * 128:(kt + 1) * 128], qz.re("p g n -> p (g n)"), True, m is None)
                    if m is not None:
                        MM(sb, identb, smask[:, m, :], False, True)
                sbs = [PB[0][:, 0:256], PB[1][:, 0:256], PB[2][:, 0:256]]
                score(0, sbs[si % 3])
                score(1, sbs[(si + 1) % 3])
                for n in range(nk):
                    kt, m, isctx = keys[n]
                    if isctx:
                        ACT(P2[(si + n) % 3], sbs[(si + n) % 3], AF.Exp, bias=ctxb, scale=SWA_SCALE)
                    else:
                        ACT(P2[(si + n) % 3], sbs[(si + n) % 3], AF.Exp, scale=SWA_SCALE)
                    if n + 2 < nk:
                        score(n + 2, sbs[(si + n + 2) % 3])
                    MM(po[:, 0:256], vsx[:, kt, kh, :], P2[(si + n) % 3], n == 0, n == nk - 1)
                si += nk
                for g in range(2):
                    TS("dve", R2[64:128, g * 128:(g + 1) * 128], po[64:128, g * 128:(g + 1) * 128],
                       esink[64:128, l * 4 + kh * 2 + g:l * 4 + kh * 2 + g + 1], ALU.add)
                RECIP(R2[64:128, :], R2[64:128, :])
                for g in range(2):
                    TT("dve", omix[g * 64:(g + 1) * 64, 4 + kh, i * 128:(i + 1) * 128],
                       po[0:64, g * 128:(g + 1) * 128], R2[64:128, g * 128:(g + 1) * 128], ALU.mult)
        if stop == 'cnt':
            print('count at swa-end', P.count)
        for ct in range(2):
            TS("pool", ypad[:, ct, 1:8, 0:15], ypad[:, ct, 0:7, 256:271], flag, ALU.mult)
            TS("pool", ypad[:, ct, 0:7, 271:286], ypad[:, ct, 1:8, 15:30], flag, ALU.mult)
        for sp_ in range(4):
            for ct in range(2):
                pc = PB[(5 if sp_ % 2 == 0 else 3) + ct]
                for tap in range(31):
                    MM(pc, diag_t[tap * 2 + ct], ypad[:, ct, 2 * sp_:2 * sp_ + 2, tap:tap + 256], tap == 0, tap == 30)
                ACT(ycv[:, ct, :], pc, AF.Identity, bias=vcol(vb + V_CDB + ct))
            for ct in range(2):
                MM(PB[0], ones_f, ycv[:, ct, :], ct == 0, ct == 1)
            for ct in range(2):
                ACT(sqc[:, 0, :], ycv[:, ct, :], AF.Square)
                MM(PB[1], ones_f, sqc[:, 0, :], ct == 0, ct == 1)
            TS("dve", mean, PB[0], 1.0 / 256.0, ALU.mult)
            TT("dve", dd, mean, mean, ALU.mult)
            STT("dve", var, PB[1], 1.0 / 256.0, dd, ALU.mult, ALU.subtract)
            ACT(var, var, AF.Ln, bias=epsc)
            ACT(var, var, AF.Exp, scale=-0.5)
            for ct in range(2):
                TT("dve", dd, ycv[:, ct, :], mean, ALU.subtract)
                TT("dve", dd, dd, var, ALU.mult)
                ACT(yact[:, ct, :], dd, AF.Silu, bias=vcol(vb + V_CLB + ct), scale=vcol(vb + V_CLG + ct))
            for co in range(2):
                pp = PB[2]
                for ct in range(2):
                    MM(pp, pw2[:, ct, co * 128:(co + 1) * 128], yact[:, ct, :], ct == 0, ct == 1)
                EVAC(omix[:, 6 + co, sp_ * 512:(sp_ + 1) * 512], pp)

        if stop == 'cnt':
            print('count at BS-end', P.count)
        if stop == 'BS':
            raise _Stop
        M.top = OMIX_TOP
        wupv = wup_d[l].rearrange("(kt p) c -> p kt c", p=128)
        wdnv = wdn_d[l].rearrange("(kt p) c -> p kt c", p=128)
        upseq = [(g_, j_) for g_ in range(2) for j_ in range(11)]
        dnseq = [(g_, q_) for g_ in range(2) for q_ in range(4)]
        upn = [0]
        dnn = [0]

        def up_dma():
            if upn[0] < len(upseq):
                g_, j_ = upseq[upn[0]]
                ws_ = wup[upn[0] % 2]
                upn[0] += 1
                DMA("pool", ws_[:, :, 0:256], wupv[:, :, j_ * 256:(j_ + 1) * 256])
                DMA("pool", ws_[:, :, 256:512], wupv[:, :, DFF + j_ * 256:DFF + (j_ + 1) * 256])

        def dn_dma():
            if dnn[0] < len(dnseq):
                g_, q_ = dnseq[dnn[0]]
                ws_ = wdn[dnn[0] % 2]
                dnn[0] += 1
                DMA("pool", ws_, wdnv[:, :, q_ * 256:(q_ + 1) * 256])
        up_dma()
        up_dma()
        modq = mod_steps(l + 1) if l + 1 < depth else []
        if modq:
            modq.pop(0)()
        n = 0
        hbuf = M.alloc_at(PERSIST_TOP + 45056, [128, 8, 4, 258], BF16)
        rsall = M.alloc_at(PERSIST_TOP + 16512 + 45056, [128, T], F32)
        tmp = [M.alloc_at(PERSIST_TOP + 16512 + 45056 + 8192 + 16384, [128, 512], F32)] * 2
        ht = M.alloc_at(PERSIST_TOP + 16512 + 45056 + 8192 + 16384 + 2048, [128, 8], F32)
        hsave = M.alloc_at(PERSIST_TOP + 16512 + 45056 + 8192 + 16384 + 2048 + 64, [128, 16], F32)

        def norm_apply(g):
            for c in range(2):
                tok0 = g * 1024 + c * 512
                for ft in range(8):
                    TT("dve", tmp[ft % 2], x[:, ft, tok0:tok0 + 512], rsall[:, tok0:tok0 + 512], ALU.mult)
                    ACT(hbuf[:, ft, 2 * c:2 * c + 2, 1:257], tmp[ft % 2].re("p (s n) -> p s n", s=2), AF.Identity,
                        bias=B2[:, ft:ft + 1], scale=A2[:, ft:ft + 1])
            TS("pool", hbuf[:, :, 1:4, 0:1], hbuf[:, :, 0:3, 256:257], flag, ALU.mult)
            TS("pool", hbuf[:, :, 0:3, 257:258], hbuf[:, :, 1:4, 1:2], flag, ALU.mult)
            if g == 0:
                MEMSET("pool", hbuf[:, :, 0, 0:1], 0.0)
                CP("dve", hbuf[:, :, 3, 257], hsave[:, 8:16])
            else:
                CP("dve", hbuf[:, :, 0, 0], hsave[:, 0:8])
                MEMSET("pool", hbuf[:, :, 3, 257:258], 0.0)
        sqc_ = [M.alloc([128, 512], BF16) for _ in range(4)]
        for c in range(4):
            for dt in range(8):
                pb = PB[n % 4]
                n += 1
                for kt in range(8):
                    MM(pb, wout[:, kt, dt * 128:(dt + 1) * 128], omix[:, kt, c * 512:(c + 1) * 512], kt == 0, kt == 7)
                STT("dve", x[:, dt, c * 512:(c + 1) * 512], pb, G1[:, dt:dt + 1], x[:, dt, c * 512:(c + 1) * 512],
                    ALU.mult, ALU.add)
            for ft in range(8):
                if ft % 2 == 0:
                    ACT(sqc_[ft % 4], x[:, ft, c * 512:(c + 1) * 512], AF.Square)
                else:
                    TT("pool", sqc_[ft % 4], x[:, ft, c * 512:(c + 1) * 512], x[:, ft, c * 512:(c + 1) * 512], ALU.mult)
                MM(PB[4 + c % 2], onesb, sqc_[ft % 4], ft == 0, ft == 7)
            rstd_from(PB[4 + c % 2], rsall[:, c * 512:(c + 1) * 512], float(D))
            if c == 2:
                for hi_, t in enumerate((1023, 1024)):
                    TT("dve", ht, x[:, :, t], V(rsall.ap[:, t:t + 1].to_broadcast([128, 8]), rsall), ALU.mult)
                    TT("dve", ht, ht, A2f, ALU.mult)
                    TT("dve", hsave[:, hi_ * 8:(hi_ + 1) * 8], ht, B2f, ALU.add)
                norm_apply(0)

        dn_dma()
        dn_dma()
        if stop == 'C':
            raise _Stop
        M.top = PERSIST_TOP
        pbuf = M.alloc([128, 22, 1024], BF16)
        M.top += 16512 + 8192
        a1us = [M.alloc([128, 4, 256], F32) for _ in range(2)]
        a1gs = [M.alloc([128, 4, 256], F32) for _ in range(2)]
        M.top += 2048 + 64 + 64
        assert M.top <= TOP_LIMIT, M.top
        wi = 0
        di = 0
        for g in range(2):
            for j in range(11):
                ws = wup[wi % 2]
                wi += 1
                if wi >= 2:
                    pass
                for jj in range(2):
                    f = 2 * j + jj
                    a1u, a1g = a1us[f % 2], a1gs[f % 2]
                    for s in range(4):
                        for kt in range(8):
                            MM(PBN[s], ws[:, kt, jj * 128:(jj + 1) * 128], hbuf[:, kt, s, :], kt == 0, kt == 7)
                    if modq:
                        modq.pop(0)()
                    for s in range(4):
                        for kt in range(8):
                            MM(PBN[4 + s], ws[:, kt, 256 + jj * 128:256 + (jj + 1) * 128], hbuf[:, kt, s, :],
                               kt == 0, kt == 7)
                    for (ps_, a1, fc) in ((PS03, a1u, f), (PS47, a1g, 22 + f)):
                        ACT(a1, ps_[:, :, 1:257], AF.Identity, bias=vcol(vb + V_FDB + fc),
                            scale=vcol(vb + V_FDW + 44 + fc))
                        STT("dve", a1, ps_[:, :, 0:256], vcol(vb + V_FDW + fc), a1, ALU.mult, ALU.add)
                        STT("dve", a1, ps_[:, :, 2:258], vcol(vb + V_FDW + 88 + fc), a1, ALU.mult, ALU.add)
                    ACT(a1g, a1g, AF.Silu)
                    if jj == 1:
                        up_dma()
                    TT("pool", pbuf[:, f, :].re("p (s n) -> p s n", s=4), a1u, a1g, ALU.mult)
            if g == 0:
                norm_apply(1)
            for dq in range(4):
                ws = wdn[di % 2]
                di += 1
                for dd_ in range(2):
                    dt = 2 * dq + dd_
                    for c in range(2):
                        pb = PB[(dt * 2 + c) % 4]
                        tok0 = g * 1024 + c * 512
                        for f in range(22):
                            MM(pb, ws[:, f, dd_ * 128:(dd_ + 1) * 128], pbuf[:, f, c * 512:(c + 1) * 512], f == 0, f == 21)
                        STT("dve", x[:, dt, tok0:tok0 + 512], pb, G2[:, dt:dt + 1], x[:, dt, tok0:tok0 + 512],
                            ALU.mult, ALU.add)
                dn_dma()
            if g == 1:
                while modq:
                    modq.pop(0)()

    try:
        layers()
    except _Stop:
        P.limit = None

    M.top = PERSIST_TOP
    sq = [M.alloc([128, 512], BF16) for _ in range(2)]
    rs = M.alloc([128, 512], F32)
    yfc = M.alloc([128, 8, 512], F32)
    yst = [M.alloc([128, D], F32) for _ in range(2)]
    oi = 0
    for c in range(4):
        for ft in range(8):
            ACT(sq[ft % 2], x[:, ft, c * 512:(c + 1) * 512], AF.Square)
            MM(PB[4], onesb, sq[ft % 2], ft == 0, ft == 7)
        rstd_from(PB[4], rs, float(D))
        for ft in range(8):
            STT("dve", yfc[:, ft, :], x[:, ft, c * 512:(c + 1) * 512], vcol(V_FNG + ft), rs, ALU.mult, ALU.mult)
        for b in range(4):
            st = yst[oi % 2]
            oi += 1
            for hb in range(2):
                pb = PB[(b * 2 + hb) % 4]
                for j in range(4):
                    TR(pb[:, j * 128:(j + 1) * 128], yfc[:, hb * 4 + j, b * 128:(b + 1) * 128], ident)
                EVAC(st[:, hb * 512:(hb + 1) * 512], pb)
            tok = c * 512 + b * 128
            DMA("sp", y_d[tok:tok + 128, :], st)

    P.emit(stack)
    stack.close()
    return nc


def _rope_tables(sample):
    t = np.arange(T)
    row = (t // 64).astype(np.float32)
    col = (t % 64).astype(np.float32)

    def tab(nd, blk):
        qd = blk
        freqs = (10000.0 ** (-np.arange(qd, dtype=np.float32) / qd)).astype(np.float32)
        cos = np.ones((nd, T), np.float32)
        sin = np.zeros((nd, T), np.float32)
        if sample:
            for i in range(nd):
                b, j = i // blk, i % blk
                pos = row if b < 2 else col
                ang = (pos * freqs[j]).astype(np.float32)
                cos[i] = np.cos(ang)
                sin[i] = np.sin(ang) * (-1.0 if b % 2 == 0 else 1.0)
        return cos, sin
    cs, ss = tab(64, 16)
    cm, sm = tab(32, 8)
    ropeS = np.zeros((128, 2, T), np.float32)
    ropeS[0:64, 0], ropeS[64:128, 0] = cs, cs
    ropeS[0:64, 1], ropeS[64:128, 1] = ss, ss
    ropeM = np.stack([cm, sm], axis=1).astype(np.float32)
    return ropeS, ropeM


def _perm(n, blk):
    idx = np.arange(n)
    b = idx // blk
    return np.where(b % 2 == 0, idx + blk, idx - blk)


def _host_prep(inp):
    f = lambda a: np.ascontiguousarray(np.asarray(a, dtype=np.float32))
    w_in = f(inp["w_in"])
    pS = _perm(64, 16)
    pM = _perm(32, 8)
    cols = []
    cols += list(range(0, 256))
    cols += list(range(256, 384))
    cols += list(range(320, 384)) + list(range(384, 416))
    cols += list(range(320, 384)) + [384 + int(p) for p in pM]
    qs = np.arange(416, 672)
    cols += list(qs)
    cols += [416 + (i // 64) * 64 + int(pS[i % 64]) for i in range(256)]
    for kh in range(2):
        k0 = 672 + kh * 64
        cols += list(range(k0, k0 + 64)) * 2
        cols += [k0 + int(p) for p in pS] * 2
    cols += list(range(800, 928))
    cols += list(range(928, 1440))
    assert len(cols) == NWIN
    w_in_ext = np.ascontiguousarray(w_in[:, :, np.array(cols)])
    w_uq = f(inp["mla_w_uq"])
    ucols = []
    for h in range(8):
        b0 = h * 96
        ucols += list(range(b0, b0 + 96))
        ucols += list(range(b0, b0 + 64)) + [b0 + 64 + int(p) for p in pM]
    w_uq_ext = np.ascontiguousarray(w_uq[:, :, np.array(ucols)])

    def vec_layers():
        out = np.zeros((128, DEPTH, NL), np.float32)
        for l in range(DEPTH):
            out[:, l, V_ADAB:V_ADAB + 48] = f(inp["ada_b"])[l].reshape(48, 128).T
            out[:, l, V_N1G:V_N1G + 8] = f(inp["norm1_g"])[l].reshape(8, 128).T
            out[:, l, V_N2G:V_N2G + 8] = f(inp["norm2_g"])[l].reshape(8, 128).T
            out[:, l, V_QNG:V_QNG + 2] = f(inp["mla_q_norm_g"])[l].reshape(2, 128).T
            out[:, l, V_KVG] = f(inp["mla_kv_norm_g"])[l]
            out[:, l, V_CDW:V_CDW + 62] = f(inp["conv_dw_w"])[l].reshape(31, 2, 128).transpose(2, 0, 1).reshape(128, 62)
            out[:, l, V_CDB:V_CDB + 2] = f(inp["conv_dw_b"])[l].reshape(2, 128).T
            out[:, l, V_CLG:V_CLG + 2] = f(inp["conv_ln_g"])[l].reshape(2, 128).T
            out[:, l, V_CLB:V_CLB + 2] = f(inp["conv_ln_b"])[l].reshape(2, 128).T
            out[:, l, V_FDW:V_FDW + 132] = f(inp["ffn_dw_w"])[l].reshape(3, 44, 128).transpose(2, 0, 1).reshape(128, 132)
            out[:, l, V_FDB:V_FDB + 44] = f(inp["ffn_dw_b"])[l].reshape(44, 128).T
            out[:, l, V_SINK:V_SINK + 4] = f(inp["swa_sink"])[l][None, :]
        return out.reshape(128, DEPTH * NL)
    vl = vec_layers()
    fng = f(inp["final_norm_g"]).reshape(8, 128).T

    shared = dict(
        ident=np.eye(128, dtype=np.float32),
        ada_w=f(inp["ada_w"]), w_in_ext=w_in_ext, w_uq_ext=w_uq_ext, w_ukv=f(inp["mla_w_ukv"]),
        pw2=f(inp["conv_w_pw2"]), w_out=f(inp["w_out"]), w_up=f(inp["ffn_w_up"]), w_down=f(inp["ffn_w_down"]),
    )
    kaug = np.zeros((9, NKEY), np.float32)
    for j in range(8):
        kaug[j, j * 256:(j + 1) * 256] = 1.0
    kaug[8, :] = -BIGM
    kl = np.arange(128)[:, None]
    ql = np.arange(128)[None, :]

    def core_inputs(sample, xtok, cvec, caches):
        vecs = np.zeros((128, NV), np.float32)
        vecs[:, :DEPTH * NL] = vl
        vecs[:, V_FNG:V_FNG + 8] = fng
        vecs[:, V_CVEC:V_CVEC + 8] = cvec.reshape(8, 128).T
        vecs[:, V_FLAG] = 1.0 if sample else 0.0
        vecs[:, V_CTXB] = 0.0 if sample else NEG
        ropeS, ropeM = _rope_tables(sample)
        qaug = np.zeros((9, T), np.float32)
        smask = np.zeros((128, 4, 128), np.float32)
        if sample:
            lo = np.where(ql <= kl, 0.0, NEG).astype(np.float32)
            hi = np.where(kl <= ql, 0.0, NEG).astype(np.float32)
            smask[:, 0], smask[:, 1], smask[:, 2], smask[:, 3] = lo, hi, lo, hi
        else:
            for j in range(8):
                qaug[j, j * 256:(j + 1) * 256] = BIGM
            qaug[8, :] = 1.0
            smask[:, 0] = NEG
            smask[:, 3] = NEG
        smask = np.ascontiguousarray(np.concatenate([smask, smask], axis=2))
        d = dict(x=np.ascontiguousarray(xtok), vecs=vecs, ropeS=ropeS, ropeM=ropeM, qaug=qaug, kaug=kaug,
                 smask=smask, cckv=caches[0], ckr=caches[1], ck=caches[2], cv=caches[3])
        d.update(shared)
        return d

    xp = f(inp["x_prompt"])
    xs = f(inp["x_sample"])
    cc = [f(inp["cache_mla_ckv"]), f(inp["cache_mla_krope"]),
          f(inp["cache_swa_k"]).reshape(2, DEPTH, L, 128), f(inp["cache_swa_v"]).reshape(2, DEPTH, L, 128)]
    zc = [np.zeros((DEPTH, L, 128), np.float32), np.zeros((DEPTH, L, 32), np.float32),
          np.zeros((DEPTH, L, 128), np.float32), np.zeros((DEPTH, L, 128), np.float32)]
    c = f(inp["c"])
    cctx = f(inp["c_ctx"])
    maps = []
    for b in range(2):
        maps.append(core_inputs(True, xs[b], c[b], [a[b] for a in cc]))
    for q in range(4):
        maps.append(core_inputs(False, xp[q * 8:(q + 1) * 8].reshape(T, D), cctx, zc))
    maps.append(maps[2])
    maps.append(maps[3])
    return maps


_NC_CACHE = {}
_TEST_CORES = None


def kernel(**inputs):
    maps = _host_prep(inputs)
    if "nc" not in _NC_CACHE:
        _NC_CACHE["nc"] = build_program(DEPTH)
    nc = _NC_CACHE["nc"]
    if _TEST_CORES is not None:
        sub = [maps[i] for i in _TEST_CORES]
        rr = run_bass_kernel_spmd(nc, sub, core_ids=list(range(len(sub)))).results
        r = [rr[_TEST_CORES.index(i)] if i in _TEST_CORES else rr[0 if i < 2 else len(sub) - 1] for i in range(8)]
    else:
        res = run_bass_kernel_spmd(nc, maps, core_ids=list(range(8)))
        r = res.results
    y_sample = np.stack([r[0]["y"], r[1]["y"]], axis=0).astype(np.float32)
    y_prompt = np.concatenate([r[2 + q]["y"].reshape(8, 256, D) for q in range(4)], axis=0).astype(np.float32)

    def st(name, w):
        parts = []
        for q in range(4):
            a = r[2 + q][name].reshape(DEPTH, 8, 256, w).transpose(1, 0, 2, 3)
            parts.append(a)
        return np.concatenate(parts, axis=0).astype(np.float32)
    s_ckv = st("st_ckv", 128)
    s_kr = st("st_kr", 32)
    s_k = st("st_k", 128).reshape(32, DEPTH, 256, 2, 64)
    s_v = st("st_v", 128).reshape(32, DEPTH, 256, 2, 64)
    return (y_prompt, y_sample, s_ckv, s_kr, s_k, s_v)
```

```python
import numpy as np
import concourse.bass as bass
import concourse.mybir as mybir
from concourse.bass_utils import run_bass_kernel_spmd

F32 = mybir.dt.float32
BF16 = mybir.dt.bfloat16
AF = mybir.ActivationFunctionType
ALU = mybir.AluOpType

D = 1024
T = 2048
DEPTH = 4
L = 512
NKEY = T + L
NKT = NKEY // 128
EPS = 1e-6
MLA_SCALE = 96.0 ** -0.5
SWA_SCALE = 64.0 ** -0.5
DFF = 2816
NEG = -30000.0
BIGM = 512.0

C_CQ, C_CKV, C_KRT, C_KRPT, C_QS, C_QSP, C_K0, C_K0P, C_K1, C_K1P, C_VS, C_CU = (
    0, 256, 384, 480, 576, 832, 1088, 1216, 1344, 1472, 1600, 1728)
NWIN = 2240
V_ADAB, V_N1G, V_N2G, V_QNG, V_KVG, V_CDW, V_CDB, V_CLG, V_CLB, V_FDW, V_FDB, V_SINK = (
    0, 48, 56, 64, 66, 67, 129, 131, 133, 135, 267, 311)
NL = 315
V_FNG, V_CVEC, V_FLAG, V_CTXB = 4 * NL, 4 * NL + 8, 4 * NL + 16, 4 * NL + 17
NV = 4 * NL + 18

ENGS = ("pe", "act", "dve", "pool", "sp")


class _Stop(Exception):
    pass


class V:
    def __init__(self, ap, t):
        self.ap, self.t = ap, t

    def __getitem__(self, k):
        return V(self.ap[k], self.t)

    def re(self, pat, **kw):
        return V(self.ap.rearrange(pat, **kw), self.t)

    def bc(self, axis, shape):
        return V(self.ap.unsqueeze(axis).to_broadcast(list(shape)), self.t)


class Tile(V):
    def __init__(self, ap, space, lo, hi, ranges=None):
        V.__init__(self, ap, self)
        self.space, self.lo, self.hi = space, lo, hi
        self.ranges = ranges if ranges is not None else [(lo, hi)]
        self.w, self.r = {}, {}
        self.ov = [self]
        self.dsem = None
        self.dcum = 0


def _ap(x):
    return x.ap if isinstance(x, V) else x


class Prog:
    def __init__(self, nc):
        self.nc = nc
        self.ins = {e: [] for e in ENGS}
        self.waited = {e: {} for e in ENGS}
        self.tiles = []
        self.ndsem = 0
        self.dstate = {}
        self.count = 0
        self.limit = None

    def register(self, t):
        for u in self.tiles:
            if u.space == t.space and u.lo < t.hi and t.lo < u.hi:
                if any(a < d and c < b for (a, b) in u.ranges for (c, d) in t.ranges):
                    u.ov.append(t)
                    t.ov.append(u)
        self.tiles.append(t)
        return t

    def unregister(self, ts):
        pass

    def _deps(self, reads, writes):
        d = {}

        def add(dic):
            for k, v in dic.items():
                if d.get(k, -1) < v:
                    d[k] = v
        for t in reads:
            for u in t.ov:
                add(u.w)
        for t in writes:
            for u in t.ov:
                add(u.w)
                add(u.r)
        return d

    def _waits(self, eng, d):
        waits = []
        for k, v in d.items():
            if k == eng and eng in ("pe", "sp"):
                continue
            if self.waited[eng].get(k, -1) >= v:
                continue
            self.waited[eng][k] = v
            waits.append((k, v))
            if k in ENGS:
                self.ins[k][v]["marked"] = True
        return waits

    def _tick(self):
        self.count += 1
        if self.limit is not None and self.count > self.limit:
            raise _Stop

    def op(self, eng, fn, reads, writes):
        self._tick()
        reads = [x.t for x in reads if isinstance(x, V)]
        writes = [x.t for x in writes if isinstance(x, V)]
        waits = self._waits(eng, self._deps(reads, writes))
        idx = len(self.ins[eng])
        self.ins[eng].append(dict(fn=fn, waits=waits, marked=False, dma=None))
        for t in reads:
            t.r[eng] = idx
        for t in writes:
            t.w[eng] = idx

    def dma(self, q, out, in_):
        self._tick()
        reads = [in_.t] if isinstance(in_, V) else []
        writes = [out.t] if isinstance(out, V) else []
        waits = self._waits(q, self._deps(reads, writes))
        st = writes[0] if writes else reads[0]
        rk = (st.space, st.lo, st.hi)
        if rk not in self.dstate:
            self.dstate[rk] = [self.ndsem, 0]
            self.ndsem += 1
        ds = self.dstate[rk]
        ds[1] += 16
        st.dsem, st.dcum = ds[0], ds[1]
        key = ("d", st.dsem)
        o, i = _ap(out), _ap(in_)
        self.ins[q].append(dict(fn=lambda e: e.dma_start(out=o, in_=i), waits=waits, marked=False, dma=st.dsem))
        for t in reads:
            t.r[key] = max(t.r.get(key, 0), st.dcum)
        for t in writes:
            t.w[key] = max(t.w.get(key, 0), st.dcum)

    def emit(self, stack):
        nc = self.nc
        sems = {e: stack.enter_context(nc.semaphore("s_" + e)) for e in ENGS}
        dsems = [stack.enter_context(nc.semaphore("d%d" % i)) for i in range(self.ndsem)]
        for e in ENGS:
            c = 0
            for rec in self.ins[e]:
                if rec["marked"]:
                    c += 1
                rec["sv"] = c
        final = {v[0]: v[1] for v in self.dstate.values()}
        block = stack.enter_context(nc.Block())

        def run(e, eo):
            for rec in self.ins[e]:
                for k, v in rec["waits"]:
                    if k in ENGS:
                        eo.wait_ge(sems[k], self.ins[k][v]["sv"])
                    else:
                        eo.wait_ge(dsems[k[1]], v)
                ins = rec["fn"](eo)
                if rec["dma"] is not None:
                    ins.then_inc(dsems[rec["dma"]], 16)
                elif rec["marked"]:
                    ins.then_inc(sems[e], 1)
            if e == "sp":
                for k, v in final.items():
                    eo.wait_ge(dsems[k], v)
                for k in ("pe", "act", "dve", "pool"):
                    n = self.ins[k][-1]["sv"] if self.ins[k] else 0
                    if n:
                        eo.wait_ge(sems[k], n)

        @block.tensor
        def _(eo):
            run("pe", eo)

        @block.scalar
        def _(eo):
            run("act", eo)

        @block.vector
        def _(eo):
            run("dve", eo)

        @block.gpsimd
        def _(eo):
            run("pool", eo)

        @block.sync
        def _(eo):
            run("sp", eo)


def build_program(depth=DEPTH, stop=None):
    nc = bass.Bass("TRN2", target_bir_lowering=False)
    P = Prog(nc)
    if stop is not None and stop.startswith('n'):
        P.limit = int(stop[1:])

    def din(name, shape):
        return nc.dram_tensor(name, list(shape), F32, kind="ExternalInput").ap()

    def dout(name, shape):
        return nc.dram_tensor(name, list(shape), F32, kind="ExternalOutput").ap()

    x_d = din("x", [T, D])
    vecs_d = din("vecs", [128, NV])
    ident_d = din("ident", [128, 128])
    ropeS_d = din("ropeS", [128, 2, T])
    ropeM_d = din("ropeM", [32, 2, T])
    qaug_d = din("qaug", [9, T])
    kaug_d = din("kaug", [9, NKEY])
    smask_d = din("smask", [128, 4, 256])
    cckv_d = din("cckv", [DEPTH, L, 128])
    ckr_d = din("ckr", [DEPTH, L, 32])
    ck_d = din("ck", [DEPTH, L, 128])
    cv_d = din("cv", [DEPTH, L, 128])
    adaw_d = din("ada_w", [DEPTH, D, 6 * D])
    win_d = din("w_in_ext", [DEPTH, D, NWIN])
    wuq_d = din("w_uq_ext", [DEPTH, 256, 1536])
    wukv_d = din("w_ukv", [DEPTH, 128, 1024])
    pw2_d = din("pw2", [DEPTH, 256, 256])
    wout_d = din("w_out", [DEPTH, D, D])
    wup_d = din("w_up", [DEPTH, D, 2 * DFF])
    wdn_d = din("w_down", [DEPTH, DFF, D])
    y_d = dout("y", [T, D])
    sckv_d = dout("st_ckv", [DEPTH, T, 128])
    skr_d = dout("st_kr", [DEPTH, T, 32])
    sk_d = dout("st_k", [DEPTH, T, 128])
    sv_d = dout("st_v", [DEPTH, T, 128])

    from contextlib import ExitStack
    stack = ExitStack()
    ARENA = 211200
    arena_h = stack.enter_context(nc.sbuf_tensor("arena", [128, ARENA // 4], F32))
    base0 = nc.sbuf_base - (ARENA // 4) * 4
    arena_t = P.register(Tile(arena_h[:], "sb", 0, ARENA))
    ps_t = stack.enter_context(nc.psum_tensor("ps", [128, 4096], F32))
    ps_all = ps_t[:]

    class Mem:
        def __init__(self):
            self.top = 0
            self.n = 0

        def alloc(self, shape, dt):
            nb = int(np.prod(shape[1:])) * (2 if dt == BF16 else 4)
            nb = (nb + 63) // 64 * 64
            off = self.top
            self.top += nb
            assert self.top <= ARENA, ("SBUF overflow", self.top)
            self.n += 1
            h = nc.alloc_sbuf_tensor_at("t%d" % self.n, list(shape), dt, offset=base0 + off)
            ap = h.ap() if hasattr(h, "ap") else h[:]
            return P.register(Tile(ap, "sb", off, off + nb))

        def alloc_at(self, off, shape, dt):
            nb = int(np.prod(shape[1:])) * (2 if dt == BF16 else 4)
            nb = (nb + 63) // 64 * 64
            assert off % 64 == 0 and off + nb <= ARENA
            self.n += 1
            h = nc.alloc_sbuf_tensor_at("t%d" % self.n, list(shape), dt, offset=base0 + off)
            ap = h.ap() if hasattr(h, "ap") else h[:]
            return P.register(Tile(ap, "sb", off, off + nb))

    M = Mem()

    def psum(b0, nb=1, c0=0, c1=512):
        if nb == 1:
            ap = ps_all[:, b0 * 512 + c0: b0 * 512 + c1]
            return P.register(Tile(ap, "ps", b0 * 2048 + c0 * 4, b0 * 2048 + c1 * 4))
        ap = ps_all.rearrange("p (b n) -> p b n", b=8)[:, b0:b0 + nb, :]
        return P.register(Tile(ap, "ps", b0 * 2048, (b0 + nb) * 2048,
                               ranges=[(b * 2048 + c0 * 4, b * 2048 + c1 * 4) for b in range(b0, b0 + nb)]))

    PB = [psum(b) for b in range(8)]
    PBN = [psum(b, 1, 0, 258) for b in range(8)]
    PS03 = psum(0, 4, 0, 258)
    PS47 = psum(4, 4, 0, 258)

    def MM(out, lhsT, rhs, start=True, stop=True):
        o, l, r = _ap(out), _ap(lhsT), _ap(rhs)
        P.op("pe", lambda e: e.matmul(o, lhsT=l, rhs=r, start=start, stop=stop), [lhsT, rhs], [out])

    def TR(out, in_, ident):
        o, i, d = _ap(out), _ap(in_), _ap(ident)
        P.op("pe", lambda e: e.transpose(o, i, d), [in_, ident], [out])

    def ACT(out, in_, func, bias=None, scale=None):
        o, i = _ap(out), _ap(in_)
        kw = {}
        if bias is not None:
            kw["bias"] = _ap(bias)
        if scale is not None:
            kw["scale"] = _ap(scale)
        P.op("act", lambda e: e.activation(out=o, in_=i, func=func, **kw), [in_, bias, scale], [out])

    EO = {"dve": "dve", "pool": "pool"}

    def TT(eng, out, in0, in1, op):
        o, a, b = _ap(out), _ap(in0), _ap(in1)
        P.op(eng, lambda e: e.tensor_tensor(out=o, in0=a, in1=b, op=op), [in0, in1], [out])

    def TS(eng, out, in0, s1, op0, s2=None, op1=None):
        o, a, c1, c2 = _ap(out), _ap(in0), _ap(s1), _ap(s2)
        if op1 is None:
            P.op(eng, lambda e: e.tensor_scalar(out=o, in0=a, scalar1=c1, scalar2=None, op0=op0), [in0, s1], [out])
        else:
            P.op(eng, lambda e: e.tensor_scalar(out=o, in0=a, scalar1=c1, scalar2=c2, op0=op0, op1=op1),
                 [in0, s1, s2], [out])

    def STT(eng, out, in0, scalar, in1, op0, op1):
        o, a, s, b = _ap(out), _ap(in0), _ap(scalar), _ap(in1)
        P.op(eng, lambda e: e.scalar_tensor_tensor(out=o, in0=a, scalar=s, in1=b, op0=op0, op1=op1),
             [in0, scalar, in1], [out])

    def CP(eng, out, in_):
        o, i = _ap(out), _ap(in_)
        P.op(eng, lambda e: e.tensor_copy(out=o, in_=i), [in_], [out])

    def RECIP(out, in_):
        o, i = _ap(out), _ap(in_)
        P.op("dve", lambda e: e.reciprocal(out=o, in_=i), [in_], [out])

    def MEMSET(eng, out, val):
        o = _ap(out)
        P.op(eng, lambda e: e.memset(o, val), [], [out])

    def DMA(q, out, in_):
        P.dma(q, out, in_)

    cpt = [0]

    def EVAC(out, in_):
        cpt[0] += 1
        if cpt[0] % 2:
            ACT(out, in_, AF.Identity)
        else:
            CP("dve", out, in_)

    x = M.alloc([128, 8, T], F32)
    vecs = M.alloc([128, NV], F32)
    ident = M.alloc([128, 128], F32)
    identb = M.alloc([128, 128], BF16)
    ones_f = M.alloc([128, 128], F32)
    smask = M.alloc([128, 4, 256], BF16)
    modall = M.alloc([128, DEPTH * 48], F32)
    lv = M.alloc([128, 48], F32)
    esink = M.alloc([128, 16], F32)
    scb = M.alloc([128, 8], BF16)
    epsc = M.alloc([128, 1], F32)
    onesb = M.alloc([128, 128], BF16)
    PERSIST_TOP = M.top

    for a0 in range(0, ARENA // 4, 8192):
        a1 = min(ARENA // 4, a0 + 8192)
        eng_ = "dve" if (a0 // 8192) % 2 == 0 else "pool"
        o_ = arena_h[:, a0:a1]
        P.op(eng_, (lambda o: (lambda e: e.memset(o, 0.0)))(o_), [], [arena_t])
    DMA("sp", vecs, vecs_d)
    DMA("sp", ident, ident_d)
    DMA("pool", identb, ident_d)
    DMA("pool", smask, smask_d)
    MEMSET("dve", ones_f, 1.0)
    MEMSET("dve", epsc, EPS)
    MEMSET("dve", onesb, 1.0)

    def vcol(c, n=1):
        return vecs[:, c:c + n]

    flag = vcol(V_FLAG)
    ctxb = vcol(V_CTXB)

    ACT(scb, vcol(V_CVEC, 8), AF.Silu)
    TOPB = (ARENA // 64) * 64
    o_ = TOPB
    top_tiles = {}
    for nm, shp in (("wdn0", [128, 22, 256]), ("wdn1", [128, 22, 256]), ("wup0", [128, 8, 512]), ("wup1", [128, 8, 512]),
                    ("ada0", [128, 8, 256]), ("ada1", [128, 8, 256])):
        nb_ = int(np.prod(shp[1:])) * 2
        o_ -= nb_
        top_tiles[nm] = M.alloc_at(o_, shp, BF16)
    TOP_LIMIT = o_
    wup = [top_tiles["wup0"], top_tiles["wup1"]]
    wdn = [top_tiles["wdn0"], top_tiles["wdn1"]]
    ada_slots = [top_tiles["ada0"], top_tiles["ada1"]]
    modps = P.register(Tile(ps_all[:, 7 * 512 + 300: 7 * 512 + 300 + DEPTH * 48], "ps", 7 * 2048, 8 * 2048))
    adak = [0]

    def mod_steps(l):
        aw = adaw_d[l].rearrange("(kt p) c -> p kt c", p=128)
        k0 = adak[0]
        adak[0] += 24

        def dma(ch):
            DMA("pool", ada_slots[(k0 + ch) % 2], aw[:, :, ch * 256:(ch + 1) * 256])

        def mm(ch):
            sl = ada_slots[(k0 + ch) % 2]
            for j in range(2):
                col = l * 48 + ch * 2 + j
                for kt in range(8):
                    MM(modps[:, col:col + 1], sl[:, kt, j * 128:(j + 1) * 128], scb[:, kt:kt + 1],
                       start=(kt == 0), stop=(kt == 7))
        steps = [lambda: (dma(0), dma(1))]
        for ch in range(24):
            def f(ch=ch):
                mm(ch)
                if ch + 2 < 24:
                    dma(ch + 2)
                if ch == 23:
                    o_, a_, b_ = modall[:, l * 48:(l + 1) * 48], modps[:, l * 48:(l + 1) * 48], vcol(l * NL + V_ADAB, 48)
                    P.op("dve", lambda e: e.tensor_tensor(out=o_.ap, in0=a_.ap, in1=b_.ap, op=ALU.add),
                         [a_, b_], [o_, PB[7]])
            steps.append(f)
        return steps

    def modulation(l):
        for f_ in mod_steps(l):
            f_()
    M.top = PERSIST_TOP
    xs = [M.alloc([128, D], F32) for _ in range(2)]
    for tb in range(16):
        st = xs[tb % 2]
        DMA("sp", st, x_d[tb * 128:(tb + 1) * 128, :])
        for hb in range(2):
            pb = PB[(tb * 2 + hb) % 4]
            for j in range(4):
                ft = hb * 4 + j
                TR(pb[:, j * 128:(j + 1) * 128], st[:, ft * 128:(ft + 1) * 128], ident)
            EVAC(x[:, hb * 4:(hb + 1) * 4, tb * 128:(tb + 1) * 128], pb.re("p (j n) -> p j n", j=4))

    modulation(0)
    for l in range(depth):
        ACT(esink[:, l * 4:(l + 1) * 4], vcol(l * NL + V_SINK, 4), AF.Exp)

    def rstd_from(ps_sum, out, nfeat):
        ACT(out, ps_sum, AF.Ln, bias=epsc, scale=1.0 / nfeat)
        ACT(out, out, AF.Exp, scale=-0.5)

    def norm_chunk(tok0, A, B, hdst, sq, tmp, rs, psn):
        for ft in range(8):
            if ft % 2 == 0:
                ACT(sq[ft % 4], x[:, ft, tok0:tok0 + 512], AF.Square)
            else:
                TT("dve", sq[ft % 4], x[:, ft, tok0:tok0 + 512], x[:, ft, tok0:tok0 + 512], ALU.mult)
            MM(psn, onesb, sq[ft % 4], start=(ft == 0), stop=(ft == 7))
        rstd_from(psn, rs, float(D))
        for ft in range(8):
            TT("dve", tmp[ft % 2], x[:, ft, tok0:tok0 + 512], rs, ALU.mult)
            ACT(hdst(ft), tmp[ft % 2].re("p (s n) -> p s n", s=2), AF.Identity, bias=B[:, ft:ft + 1],
                scale=A[:, ft:ft + 1])

    def out_transposed(src, rows, r0, dst_d, tok0, ost, pbank):
        for b in range(4):
            TR(pbank[:, b * rows:(b + 1) * rows], src[r0:r0 + rows, b * 128:(b + 1) * 128], ident[r0:r0 + rows, r0:r0 + rows])
        EVAC(ost[:, :, 0:rows], pbank[:, 0:4 * rows].re("p (b f) -> p b f", b=4))
        DMA("sp", dst_d[tok0:tok0 + 512, :].rearrange("(b p) f -> p b f", p=128), ost[:, :, 0:rows])

    def layers():
      for l in range(depth):
        if stop == 'pro':
            raise _Stop
        vb = l * NL
        mod = modall[:, l * 48:(l + 1) * 48]
        B1, G1, G2 = mod[:, 0:8], mod[:, 16:24], mod[:, 40:48]
        A1, A2, A2f, B2f = lv[:, 0:8], lv[:, 8:16], lv[:, 16:24], lv[:, 24:32]
        B2 = mod[:, 24:32]
        STT("dve", A1, mod[:, 8:16], 1.0, vcol(vb + V_N1G, 8), ALU.add, ALU.mult)
        STT("dve", A2, mod[:, 32:40], 1.0, vcol(vb + V_N2G, 8), ALU.add, ALU.mult)
        TS("dve", A2f, A2, flag, ALU.mult)
        TS("dve", B2f, B2, flag, ALU.mult)

        M.top = PERSIST_TOP
        omix = M.alloc([128, 8, T], BF16)
        OMIX_TOP = M.top
        rs1 = M.alloc([128, T], F32)
        RS1_TOP = M.top
        cqn = M.alloc([128, 2, T], BF16)
        ckvT = M.alloc([128, NKEY], BF16)
        krT = M.alloc([128, NKEY], BF16)
        wuq = M.alloc([128, 2, 1536], BF16)
        wukv = M.alloc([128, 1024], BF16)
        cst = M.alloc([128, 4, 128], F32)
        cst2 = M.alloc([128, 4, 96], F32)
        A1OUT_TOP = M.top
        win1 = M.alloc([128, 8, 576], BF16)
        hb = [M.alloc([128, 8, 2, 258], BF16) for _ in range(2)]
        tabMc = M.alloc([128, 2, 512], F32)
        sq = [M.alloc([128, 512], BF16) for _ in range(4)]
        tmp = [M.alloc([128, 512], F32) for _ in range(2)]
        rs = M.alloc([128, 512], F32)
        cqf = M.alloc([128, 2, 512], F32)
        stf = M.alloc([128, 512], F32)
        uu = M.alloc([128, 512], F32)
        vv = M.alloc([128, 512], F32)
        ost = [M.alloc([128, 4, 128], F32) for _ in range(2)]
        DMA("pool", win1, win_d[l].rearrange("(kt p) c -> p kt c", p=128)[:, :, 0:576])
        DMA("pool", wuq, wuq_d[l].rearrange("(kt p) c -> p kt c", p=128))
        DMA("pool", wukv, wukv_d[l])
        DMA("sp", cst, cckv_d[l].rearrange("(b p) f -> p b f", p=128))
        MEMSET("dve", cst2, 0.0)
        DMA("sp", cst2[:, :, 64:96], ckr_d[l].rearrange("(b p) f -> p b f", p=128))
        for b in range(4):
            TR(PB[7][:, b * 128:(b + 1) * 128], cst[:, b, :], ident)
        EVAC(ckvT[:, T:NKEY], PB[7])
        for b in range(4):
            TR(PB[6][0:96, b * 128:(b + 1) * 128], cst2[:, b, :], ident)
        EVAC(krT[64:96, T:NKEY], PB[6][64:96, :])

        def normc1(g_, c_):
            t0_ = g_ * 1024 + c_ * 512
            norm_chunk(t0_, A1, B1, lambda ft: hb[c_][:, ft, :, 1:257], sq, tmp, rs1[:, t0_:t0_ + 512], PB[4])
        normc1(0, 0)
        normc1(0, 1)
        for g in range(2):
            for c in range(2):
                tok0 = g * 1024 + c * 512
                rhs = lambda kt: hb[c][:, kt, :, 1:257]
                for j in range(2):
                    pb = PB[j]
                    for kt in range(8):
                        MM(pb, win1[:, kt, C_CQ + j * 128:C_CQ + (j + 1) * 128], rhs(kt), kt == 0, kt == 7)
                    CP("dve", cqf[:, j, :], pb)
                    ACT(sq[j], cqf[:, j, :], AF.Square)
                for j in range(2):
                    MM(PB[5], onesb, sq[j], j == 0, j == 1)
                rstd_from(PB[5], rs, 256.0)
                for j in range(2):
                    STT("dve", cqn[:, j, tok0:tok0 + 512], cqf[:, j, :], vcol(vb + V_QNG + j), rs, ALU.mult, ALU.mult)
                if stop in ('A1b', 'cnt'):
                    print('count at A1b', P.count)
                if stop == 'A1b':
                    raise _Stop
                pb = PB[2]
                for kt in range(8):
                    MM(pb, win1[:, kt, C_CKV:C_CKV + 128], rhs(kt), kt == 0, kt == 7)
                ACT(sq[0], pb, AF.Square)
                MM(PB[5], onesb, sq[0], True, True)
                rstd_from(PB[5], rs, 128.0)
                STT("dve", stf, pb, vcol(vb + V_KVG), rs, ALU.mult, ALU.mult)
                CP("pool", ckvT[:, tok0:tok0 + 512], stf)
                out_transposed(stf, 128, 0, sckv_d[l], tok0, ost[0], PB[6])
                if stop in ('A1c', 'cnt'):
                    print('count at A1c', P.count)
                if stop == 'A1c':
                    raise _Stop
                pa, pbb = PB[3], PB[7]
                for kt in range(8):
                    MM(pa[0:96, :], win1[:, kt, C_KRT:C_KRT + 96], rhs(kt), kt == 0, kt == 7)
                for kt in range(8):
                    MM(pbb[0:96, :], win1[:, kt, C_KRPT:C_KRPT + 96], rhs(kt), kt == 0, kt == 7)
                DMA("sp", tabMc[64:96], ropeM_d[:, :, tok0:tok0 + 512])
                CP("dve", stf[64:96, :], pa[64:96, :])
                TT("dve", uu[64:96, :], pa[64:96, :], tabMc[64:96, 0, :], ALU.mult)
                TT("dve", vv[64:96, :], pbb[64:96, :], tabMc[64:96, 1, :], ALU.mult)
                TT("pool", krT[64:96, tok0:tok0 + 512], uu[64:96, :], vv[64:96, :], ALU.add)
                out_transposed(stf, 32, 64, skr_d[l], tok0, ost[1], PB[6])
                if g == 0:
                    normc1(1, c)

        if stop in ('A1', 'cnt'):
            print('count at A1', P.count)
        if stop == 'A1':
            raise _Stop
        M.top = A1OUT_TOP
        tabM = M.alloc([128, 2, T], F32)
        Qs_ = [M.alloc([128, T], BF16) for _ in range(2)]
        Ks_ = [M.alloc([128, NKEY], BF16) for _ in range(2)]
        Vs_ = [M.alloc([128, NKT, 128], BF16) for _ in range(2)]
        Pb = [M.alloc([128, 512], BF16) for _ in range(3)]
        Rb = M.alloc([128, 512], F32)
        uu = M.alloc([128, 512], F32)
        vv = M.alloc([128, 512], F32)
        DMA("sp", tabM[64:96], ropeM_d)
        winv = win_d[l].rearrange("(kt p) c -> p kt c", p=128)
        win2a = M.alloc_at(PERSIST_TOP + 16384, [128, 8, 1024], BF16)
        win2b = M.alloc_at(TOPB - 10240, [128, 8, 640], BF16)
        assert M.top <= TOPB - 10240, M.top
        DMA("pool", win2a, winv[:, :, C_QS:C_QS + 1024])
        DMA("pool", win2b, winv[:, :, C_VS:C_VS + 640])
        for s in range(2):
            MEMSET("pool", Qs_[s][96:128, :], 0.0)
            MEMSET("pool", Ks_[s][96:128, :], 0.0)
            DMA("pool", Qs_[s][96:105, :], qaug_d)
            DMA("pool", Ks_[s][96:105, :], kaug_d)
            MEMSET("pool", Vs_[s][:, :, 64:128], 1.0)
        def build_steps(h):
            s_ = h % 2
            Qh, Kh, Vh = Qs_[s_], Ks_[s_], Vs_[s_]
            steps = []
            for kc in range(5):
                def f(kc=kc):
                    pb = PB[5 + kc % 2]
                    MM(pb[0:64, :], wukv[:, h * 128:h * 128 + 64], ckvT[:, kc * 512:(kc + 1) * 512])
                    CP("dve", Kh[0:64, kc * 512:(kc + 1) * 512], pb[0:64, :])
                steps.append(f)
            steps.append(lambda: CP("pool", Kh[64:96, :], krT[64:96, :]))
            for k0 in range(0, NKT, 8):
                def f(k0=k0):
                    n = min(8, NKT - k0)
                    pb = PB[7]
                    for j in range(n):
                        MM(pb[:, j * 64:(j + 1) * 64], ckvT[:, (k0 + j) * 128:(k0 + j + 1) * 128],
                           wukv[:, h * 128 + 64:h * 128 + 128])
                    CP("dve", Vh[:, k0:k0 + n, 0:64], pb[:, 0:n * 64].re("p (j d) -> p j d", j=n))
                steps.append(f)
            for c in range(4):
                def f(c=c):
                    pa, pbb = PB[5], PB[6]
                    for kt in range(2):
                        MM(pa[0:96, :], wuq[:, kt, h * 192:h * 192 + 96], cqn[:, kt, c * 512:(c + 1) * 512], kt == 0, kt == 1)
                    for kt in range(2):
                        MM(pbb[0:96, :], wuq[:, kt, h * 192 + 96:h * 192 + 192], cqn[:, kt, c * 512:(c + 1) * 512], kt == 0, kt == 1)
                    CP("dve", Qh[0:64, c * 512:(c + 1) * 512], pa[0:64, :])
                    TT("dve", uu[64:96, :], pa[64:96, :], tabM[64:96, 0, c * 512:(c + 1) * 512], ALU.mult)
                    TT("dve", vv[64:96, :], pbb[64:96, :], tabM[64:96, 1, c * 512:(c + 1) * 512], ALU.mult)
                    TT("pool", Qh[64:96, c * 512:(c + 1) * 512], uu[64:96, :], vv[64:96, :], ALU.add)
                steps.append(f)
            return steps

        for f_ in build_steps(0):
            f_()
        for h in range(8):
            s = h % 2
            Qh, Kh, Vh = Qs_[s], Ks_[s], Vs_[s]
            pending = build_steps(h + 1) if h + 1 < 8 else []
            it = 0
            for c in range(4):
                po = PB[3 + (h * 4 + c) % 2]
                qv = Qh[:, c * 512:(c + 1) * 512]

                def score(kt):
                    MM(PB[kt % 3], Kh[:, kt * 128:(kt + 1) * 128], qv)
                score(0)
                score(1)
                for kt in range(NKT):
                    ACT(Pb[kt % 3], PB[kt % 3], AF.Exp, scale=MLA_SCALE)
                    if kt + 2 < NKT:
                        score(kt + 2)
                    MM(po, Vh[:, kt, :], Pb[kt % 3], kt == 0, kt == NKT - 1)
                    it += 1
                    if pending and it % 5 == 0:
                        pending.pop(0)()
                RECIP(Rb[64:128, :], po[64:128, :])
                p0 = (h % 2) * 64
                TT("dve", omix[p0:p0 + 64, h // 2, c * 512:(c + 1) * 512], po[0:64, :], Rb[64:128, :], ALU.mult)
            while pending:
                pending.pop(0)()

        if stop == 'BM':
            raise _Stop
        M.top = RS1_TOP
        qsw = M.alloc([128, 2, T], BF16)
        ksd = M.alloc([128, 2, NKEY], BF16)
        vsx = M.alloc([128, NKT, 2, 128], BF16)
        ypad = M.alloc([128, 2, 8, 286], BF16)
        A2OUT_TOP = M.top
        hb = [M.alloc([128, 8, 2, 258], BF16) for _ in range(2)]
        tabS = [M.alloc([128, 2, 512], F32) for _ in range(1)]
        tmp = [M.alloc([128, 512], F32) for _ in range(2)]
        stf = M.alloc([128, 512], F32)
        uu = M.alloc([128, 512], F32)
        vv = M.alloc([128, 512], F32)
        ost = [M.alloc([128, 4, 128], F32) for _ in range(2)]
        MEMSET("pool", vsx[:, :, :, 64:128], 1.0)
        MEMSET("pool", ypad[:, :, 0, 0:15], 0.0)
        MEMSET("pool", ypad[:, :, 7, 271:286], 0.0)
        tsi = 0
        def normc2(g_, c_):
            t0_ = g_ * 1024 + c_ * 512
            for ft in range(8):
                TT("dve", tmp[ft % 2], x[:, ft, t0_:t0_ + 512], rs1[:, t0_:t0_ + 512], ALU.mult)
                ACT(hb[c_][:, ft, :, 1:257], tmp[ft % 2].re("p (s n) -> p s n", s=2), AF.Identity,
                    bias=B1[:, ft:ft + 1], scale=A1[:, ft:ft + 1])
        normc2(0, 0)
        normc2(0, 1)
        for g in range(2):
            for c in range(2):
                tok0 = g * 1024 + c * 512
                cb = g * 2 + c
                rhs = lambda kt: hb[c][:, kt, :, 1:257]
                tb_ = tabS[0]
                tsi += 1
                DMA("sp", tb_, ropeS_d[:, :, tok0:tok0 + 512])

                rtn = [0]

                def rope_tile(ca, cbb, dst, keep=None):
                    pa, pbb = (PB[0], PB[1]) if rtn[0] % 2 == 0 else (PB[2], PB[3])
                    rtn[0] += 1
                    for kt in range(8):
                        MM(pa, win2a[:, kt, ca - C_QS:ca - C_QS + 128], rhs(kt), kt == 0, kt == 7)
                    for kt in range(8):
                        MM(pbb, win2a[:, kt, cbb - C_QS:cbb - C_QS + 128], rhs(kt), kt == 0, kt == 7)
                    if keep is not None:
                        CP("dve", stf[keep:keep + 64, :], pa[keep:keep + 64, :])
                    u_, v_ = (uu, vv) if rtn[0] % 2 == 0 else (tmp[0], tmp[1])
                    TT("dve", u_, pa, tb_[:, 0, :], ALU.mult)
                    TT("dve", v_, pbb, tb_[:, 1, :], ALU.mult)
                    TT("pool", dst, u_, v_, ALU.add)
                rope_tile(C_QS, C_QSP, qsw[:, 0, tok0:tok0 + 512])
                rope_tile(C_QS + 128, C_QSP + 128, qsw[:, 1, tok0:tok0 + 512])
                rope_tile(C_K0, C_K0P, ksd[:, 0, tok0:tok0 + 512], keep=0)
                rope_tile(C_K1, C_K1P, ksd[:, 1, tok0:tok0 + 512], keep=64)
                out_transposed(stf, 128, 0, sk_d[l], tok0, ost[0], PB[6])
            for c in range(2):
                tok0 = g * 1024 + c * 512
                rhs = lambda kt: hb[c][:, kt, :, 1:257]
                pb = PB[2]
                for kt in range(8):
                    MM(pb, win2b[:, kt, 0:128], rhs(kt), kt == 0, kt == 7)
                CP("dve", stf, pb)
                for b in range(4):
                    TR(PB[6][:, b * 128:(b + 1) * 128], stf[:, b * 128:(b + 1) * 128], ident)
                ACT(ost[1], PB[6].re("p (b f) -> p b f", b=4), AF.Identity)
                DMA("sp", sv_d[l][tok0:tok0 + 512, :].rearrange("(b p) f -> p b f", p=128), ost[1])
                kt0 = tok0 // 128
                CP("dve", vsx[:, kt0:kt0 + 4, :, 0:64], ost[1].re("p b (k d) -> p b k d", k=2))
                for j in range(2):
                    pu, pg = (PB[0], PB[1]) if j == 0 else (PB[3], PB[4])
                    for kt in range(8):
                        MM(pu, win2b[:, kt, 128 + j * 128:256 + j * 128], rhs(kt), kt == 0, kt == 7)
                    for kt in range(8):
                        MM(pg, win2b[:, kt, 384 + j * 128:512 + j * 128], rhs(kt), kt == 0, kt == 7)
                    ACT(uu, pg, AF.Sigmoid)
                    sg0 = tok0 // 256
                    TT("dve", ypad[:, j, sg0:sg0 + 2, 15:271], pu.re("p (s n) -> p s n", s=2),
                       uu.re("p (s n) -> p s n", s=2), ALU.mult)
                if g == 0:
                    normc2(1, c)

        if stop == 'A2':
            raise _Stop
        M.top = A2OUT_TOP
        wout = M.alloc_at(TOPB - 16384, [128, 8, D], BF16)
        DMA("pool", wout, wout_d[l].rearrange("(kt p) c -> p kt c", p=128))
        cst = M.alloc([128, 4, 2, 2, 64], F32)
        P2 = [M.alloc([128, 256], BF16) for _ in range(3)]
        R2 = M.alloc([128, 256], F32)
        qzs = [M.alloc([128, 2, 128], BF16) for _ in range(2)]
        for q_ in qzs:
            MEMSET("pool", q_, 0.0)
        diag_t = [M.alloc([128, 128], BF16) for _ in range(62)]
        pw2 = M.alloc([128, 2, 256], BF16)
        ycv = M.alloc([128, 2, 512], F32)
        sqc = M.alloc([128, 1, 512], F32)
        mean = M.alloc([128, 512], F32)
        var = M.alloc([128, 512], F32)
        dd = M.alloc([128, 512], F32)
        yact = M.alloc([128, 2, 512], BF16)
        ckv_ = ck_d[l].rearrange("(b p) (k d) -> p b k d", p=128, k=2)
        for du in range(2):
            for kh in range(2):
                DMA("sp", cst[:, :, kh, du, :], ckv_[:, :, kh, :])
        for kh in range(2):
            for b in range(4):
                TR(PB[7][:, b * 128:(b + 1) * 128], cst[:, b, kh].re("p u d -> p (u d)"), ident)
            EVAC(ksd[:, kh, T:NKEY], PB[7])
        for kh in range(2):
            DMA("pool", vsx[:, 16:20, kh, 0:64], cv_d[l].rearrange("(b p) (k d) -> p b k d", p=128, k=2)[:, :, kh, :])
        DMA("pool", pw2, pw2_d[l].rearrange("(kt p) c -> p kt c", p=128))
        for tap in range(31):
            for ct in range(2):
                if (tap + ct) % 2 == 0:
                    ACT(diag_t[tap * 2 + ct], identb, AF.Identity, scale=vcol(vb + V_CDW + tap * 2 + ct))
                else:
                    TS("dve", diag_t[tap * 2 + ct], identb, vcol(vb + V_CDW + tap * 2 + ct), ALU.mult)
        if stop == 'cnt':
            print('count at BS-prep-end', P.count)
        si = 0
        for kh in range(2):
            for i in range(16):
                if stop == 'cnt' and kh == 0 and i < 2:
                    print('count at swa block', i, P.count)
                keys = []
                if i > 0:
                    keys.append((i - 1, (i % 2) * 2 + 0, False))
                keys.append((i, None, False))
                if i < 15:
                    keys.append((i + 1, (i % 2) * 2 + 1, False))
                for cb in range(4):
                    keys.append((16 + cb, None, True))
                po = PB[3 + (kh * 16 + i) % 2]
                nk = len(keys)

                qz = qzs[(kh * 16 + i) % 2]
                CP("pool", qz[0:64, 0, :], qsw[0:64, kh, i * 128:(i + 1) * 128])
                CP("pool", qz[64:128, 1, :], qsw[64:128, kh, i * 128:(i + 1) * 128])

                def score(n, sb):
                    kt, m, isctx = keys[n]
                    MM(sb, ksd[:, kh, kt * 128:(kt + 1) * 128], qz.re("p g n -> p (g n)"), True, m is None)
                    if m is not None:
                        MM(sb, identb, smask[:, m, :], False, True)
                sbs = [PB[0][:, 0:256], PB[1][:, 0:256], PB[2][:, 0:256]]
                score(0, sbs[si % 3])
                score(1, sbs[(si + 1) % 3])
                for n in range(nk):
                    kt, m, isctx = keys[n]
                    if isctx:
                        ACT(P2[(si + n) % 3], sbs[(si + n) % 3], AF.Exp, bias=ctxb, scale=SWA_SCALE)
                    else:
                        ACT(P2[(si + n) % 3], sbs[(si + n) % 3], AF.Exp, scale=SWA_SCALE)
                    if n + 2 < nk:
                        score(n + 2, sbs[(si + n + 2) % 3])
                    MM(po[:, 0:256], vsx[:, kt, kh, :], P2[(si + n) % 3], n == 0, n == nk - 1)
                si += nk
                for g in range(2):
                    TS("dve", R2[64:128, g * 128:(g + 1) * 128], po[64:128, g * 128:(g + 1) * 128],
                       esink[64:128, l * 4 + kh * 2 + g:l * 4 + kh * 2 + g + 1], ALU.add)
                RECIP(R2[64:128, :], R2[64:128, :])
                for g in range(2):
                    TT("dve", omix[g * 64:(g + 1) * 64, 4 + kh, i * 128:(i + 1) * 128],
                       po[0:64, g * 128:(g + 1) * 128], R2[64:128, g * 128:(g + 1) * 128], ALU.mult)
        if stop == 'cnt':
            print('count at swa-end', P.count)
        for ct in range(2):
            TS("pool", ypad[:, ct, 1:8, 0:15], ypad[:, ct, 0:7, 256:271], flag, ALU.mult)
            TS("pool", ypad[:, ct, 0:7, 271:286], ypad[:, ct, 1:8, 15:30], flag, ALU.mult)
        def conv_mm(sp_):
            for ct in range(2):
                pc = PB[(5 if sp_ % 2 == 0 else 3) + ct]
                for tap in range(31):
                    MM(pc, diag_t[tap * 2 + ct], ypad[:, ct, 2 * sp_:2 * sp_ + 2, tap:tap + 256], tap == 0, tap == 30)
                ACT(ycv[:, ct, :], pc, AF.Identity, bias=vcol(vb + V_CDB + ct))

        def conv_ln(sp_):
            for ct in range(2):
                MM(PB[0], ones_f, ycv[:, ct, :], ct == 0, ct == 1)
            for ct in range(2):
                ACT(sqc[:, 0, :], ycv[:, ct, :], AF.Square)
                MM(PB[1], ones_f, sqc[:, 0, :], ct == 0, ct == 1)
            TS("dve", mean, PB[0], 1.0 / 256.0, ALU.mult)
            TT("dve", dd, mean, mean, ALU.mult)
            STT("dve", var, PB[1], 1.0 / 256.0, dd, ALU.mult, ALU.subtract)
            ACT(var, var, AF.Ln, bias=epsc)
            ACT(var, var, AF.Exp, scale=-0.5)
            for ct in range(2):
                TT("dve", dd, ycv[:, ct, :], mean, ALU.subtract)
                TT("dve", dd, dd, var, ALU.mult)
                ACT(yact[:, ct, :], dd, AF.Silu, bias=vcol(vb + V_CLB + ct), scale=vcol(vb + V_CLG + ct))

        def conv_pw(sp_):
            for co in range(2):
                pp = PB[2]
                for ct in range(2):
                    MM(pp, pw2[:, ct, co * 128:(co + 1) * 128], yact[:, ct, :], ct == 0, ct == 1)
                EVAC(omix[:, 6 + co, sp_ * 512:(sp_ + 1) * 512], pp)
        conv_mm(0)
        for sp_ in range(4):
            conv_ln(sp_)
            if sp_ + 1 < 4:
                conv_mm(sp_ + 1)
            conv_pw(sp_)

        if stop == 'cnt':
            print('count at BS-end', P.count)
        if stop == 'BS':
            raise _Stop
        M.top = OMIX_TOP
        wupv = wup_d[l].rearrange("(kt p) c -> p kt c", p=128)
        wdnv = wdn_d[l].rearrange("(kt p) c -> p kt c", p=128)
        upseq = [(g_, j_) for g_ in range(2) for j_ in range(11)]
        dnseq = [(g_, q_) for g_ in range(2) for q_ in range(4)]
        upn = [0]
        dnn = [0]

        def up_dma():
            if upn[0] < len(upseq):
                g_, j_ = upseq[upn[0]]
                ws_ = wup[upn[0] % 2]
                upn[0] += 1
                DMA("pool", ws_[:, :, 0:256], wupv[:, :, j_ * 256:(j_ + 1) * 256])
                DMA("pool", ws_[:, :, 256:512], wupv[:, :, DFF + j_ * 256:DFF + (j_ + 1) * 256])

        def dn_dma():
            if dnn[0] < len(dnseq):
                g_, q_ = dnseq[dnn[0]]
                ws_ = wdn[dnn[0] % 2]
                dnn[0] += 1
                DMA("pool", ws_, wdnv[:, :, q_ * 256:(q_ + 1) * 256])
        up_dma()
        up_dma()
        modq = mod_steps(l + 1) if l + 1 < depth else []
        if modq:
            modq.pop(0)()
        n = 0
        hbuf = M.alloc_at(PERSIST_TOP + 45056, [128, 8, 4, 258], BF16)
        rsall = M.alloc_at(PERSIST_TOP + 16512 + 45056, [128, T], F32)
        tmp = [M.alloc_at(PERSIST_TOP + 16512 + 45056 + 8192 + 16384, [128, 512], F32)] * 2
        ht = M.alloc_at(PERSIST_TOP + 16512 + 45056 + 8192 + 16384 + 2048, [128, 8], F32)
        hsave = M.alloc_at(PERSIST_TOP + 16512 + 45056 + 8192 + 16384 + 2048 + 64, [128, 16], F32)

        def norm_apply(g):
            for c in range(2):
                tok0 = g * 1024 + c * 512
                for ft in range(8):
                    TT("dve", tmp[ft % 2], x[:, ft, tok0:tok0 + 512], rsall[:, tok0:tok0 + 512], ALU.mult)
                    ACT(hbuf[:, ft, 2 * c:2 * c + 2, 1:257], tmp[ft % 2].re("p (s n) -> p s n", s=2), AF.Identity,
                        bias=B2[:, ft:ft + 1], scale=A2[:, ft:ft + 1])
            TS("pool", hbuf[:, :, 1:4, 0:1], hbuf[:, :, 0:3, 256:257], flag, ALU.mult)
            TS("pool", hbuf[:, :, 0:3, 257:258], hbuf[:, :, 1:4, 1:2], flag, ALU.mult)
            if g == 0:
                MEMSET("pool", hbuf[:, :, 0, 0:1], 0.0)
                CP("dve", hbuf[:, :, 3, 257], hsave[:, 8:16])
            else:
                CP("dve", hbuf[:, :, 0, 0], hsave[:, 0:8])
                MEMSET("pool", hbuf[:, :, 3, 257:258], 0.0)
        sqc_ = [M.alloc([128, 512], BF16) for _ in range(4)]
        for c in range(4):
            for dt in range(8):
                pb = PB[n % 4]
                n += 1
                for kt in range(8):
                    MM(pb, wout[:, kt, dt * 128:(dt + 1) * 128], omix[:, kt, c * 512:(c + 1) * 512], kt == 0, kt == 7)
                STT("dve", x[:, dt, c * 512:(c + 1) * 512], pb, G1[:, dt:dt + 1], x[:, dt, c * 512:(c + 1) * 512],
                    ALU.mult, ALU.add)
            for ft in range(8):
                if ft % 2 == 0:
                    ACT(sqc_[ft % 4], x[:, ft, c * 512:(c + 1) * 512], AF.Square)
                else:
                    TT("pool", sqc_[ft % 4], x[:, ft, c * 512:(c + 1) * 512], x[:, ft, c * 512:(c + 1) * 512], ALU.mult)
                MM(PB[4 + c % 2], onesb, sqc_[ft % 4], ft == 0, ft == 7)
            rstd_from(PB[4 + c % 2], rsall[:, c * 512:(c + 1) * 512], float(D))
            if c == 2:
                for hi_, t in enumerate((1023, 1024)):
                    TT("dve", ht, x[:, :, t], V(rsall.ap[:, t:t + 1].to_broadcast([128, 8]), rsall), ALU.mult)
                    TT("dve", ht, ht, A2f, ALU.mult)
                    TT("dve", hsave[:, hi_ * 8:(hi_ + 1) * 8], ht, B2f, ALU.add)
                norm_apply(0)

        dn_dma()
        dn_dma()
        if stop == 'C':
            raise _Stop
        M.top = PERSIST_TOP
        pbuf = M.alloc([128, 22, 1024], BF16)
        M.top += 16512 + 8192
        a1us = [M.alloc([128, 4, 256], F32) for _ in range(2)]
        a1gs = [M.alloc([128, 4, 256], F32) for _ in range(2)]
        M.top += 2048 + 64 + 64
        assert M.top <= TOP_LIMIT, M.top
        wi = 0
        di = 0
        for g in range(2):
            for j in range(11):
                ws = wup[wi % 2]
                wi += 1
                if wi >= 2:
                    pass
                for jj in range(2):
                    f = 2 * j + jj
                    a1u, a1g = a1us[f % 2], a1gs[f % 2]
                    for s in range(4):
                        for kt in range(8):
                            MM(PBN[s], ws[:, kt, jj * 128:(jj + 1) * 128], hbuf[:, kt, s, :], kt == 0, kt == 7)
                    if modq:
                        modq.pop(0)()
                    for s in range(4):
                        for kt in range(8):
                            MM(PBN[4 + s], ws[:, kt, 256 + jj * 128:256 + (jj + 1) * 128], hbuf[:, kt, s, :],
                               kt == 0, kt == 7)
                    for (ps_, a1, fc) in ((PS03, a1u, f), (PS47, a1g, 22 + f)):
                        ACT(a1, ps_[:, :, 1:257], AF.Identity, bias=vcol(vb + V_FDB + fc),
                            scale=vcol(vb + V_FDW + 44 + fc))
                        STT("dve", a1, ps_[:, :, 0:256], vcol(vb + V_FDW + fc), a1, ALU.mult, ALU.add)
                        STT("dve", a1, ps_[:, :, 2:258], vcol(vb + V_FDW + 88 + fc), a1, ALU.mult, ALU.add)
                    ACT(a1g, a1g, AF.Silu)
                    if jj == 1:
                        up_dma()
                    TT("pool", pbuf[:, f, :].re("p (s n) -> p s n", s=4), a1u, a1g, ALU.mult)
            if g == 0:
                norm_apply(1)
            for dq in range(4):
                ws = wdn[di % 2]
                di += 1
                for dd_ in range(2):
                    dt = 2 * dq + dd_
                    for c in range(2):
                        pb = PB[(dt * 2 + c) % 4]
                        tok0 = g * 1024 + c * 512
                        for f in range(22):
                            MM(pb, ws[:, f, dd_ * 128:(dd_ + 1) * 128], pbuf[:, f, c * 512:(c + 1) * 512], f == 0, f == 21)
                        STT("dve", x[:, dt, tok0:tok0 + 512], pb, G2[:, dt:dt + 1], x[:, dt, tok0:tok0 + 512],
                            ALU.mult, ALU.add)
                dn_dma()
            if g == 1:
                while modq:
                    modq.pop(0)()

    try:
        layers()
    except _Stop:
        P.limit = None

    M.top = PERSIST_TOP
    sq = [M.alloc([128, 512], BF16) for _ in range(2)]
    rs = M.alloc([128, 512], F32)
    yfc = M.alloc([128, 8, 512], F32)
    yst = [M.alloc([128, D], F32) for _ in range(2)]
    oi = 0
    for c in range(4):
        for ft in range(8):
            ACT(sq[ft % 2], x[:, ft, c * 512:(c + 1) * 512], AF.Square)
            MM(PB[4], onesb, sq[ft % 2], ft == 0, ft == 7)
        rstd_from(PB[4], rs, float(D))
        for ft in range(8):
            STT("dve", yfc[:, ft, :], x[:, ft, c * 512:(c + 1) * 512], vcol(V_FNG + ft), rs, ALU.mult, ALU.mult)
        for b in range(4):
            st = yst[oi % 2]
            oi += 1
            for hb in range(2):
                pb = PB[(b * 2 + hb) % 4]
                for j in range(4):
                    TR(pb[:, j * 128:(j + 1) * 128], yfc[:, hb * 4 + j, b * 128:(b + 1) * 128], ident)
                EVAC(st[:, hb * 512:(hb + 1) * 512], pb)
            tok = c * 512 + b * 128
            DMA("sp", y_d[tok:tok + 128, :], st)

    P.emit(stack)
    stack.close()
    return nc


def _rope_tables(sample):
    t = np.arange(T)
    row = (t // 64).astype(np.float32)
    col = (t % 64).astype(np.float32)

    def tab(nd, blk):
        qd = blk
        freqs = (10000.0 ** (-np.arange(qd, dtype=np.float32) / qd)).astype(np.float32)
        cos = np.ones((nd, T), np.float32)
        sin = np.zeros((nd, T), np.float32)
        if sample:
            for i in range(nd):
                b, j = i // blk, i % blk
                pos = row if b < 2 else col
                ang = (pos * freqs[j]).astype(np.float32)
                cos[i] = np.cos(ang)
                sin[i] = np.sin(ang) * (-1.0 if b % 2 == 0 else 1.0)
        return cos, sin
    cs, ss = tab(64, 16)
    cm, sm = tab(32, 8)
    ropeS = np.zeros((128, 2, T), np.float32)
    ropeS[0:64, 0], ropeS[64:128, 0] = cs, cs
    ropeS[0:64, 1], ropeS[64:128, 1] = ss, ss
    ropeM = np.stack([cm, sm], axis=1).astype(np.float32)
    return ropeS, ropeM


def _perm(n, blk):
    idx = np.arange(n)
    b = idx // blk
    return np.where(b % 2 == 0, idx + blk, idx - blk)


def _host_prep(inp):
    f = lambda a: np.ascontiguousarray(np.asarray(a, dtype=np.float32))
    w_in = f(inp["w_in"])
    pS = _perm(64, 16)
    pM = _perm(32, 8)
    cols = []
    cols += list(range(0, 256))
    cols += list(range(256, 384))
    cols += list(range(320, 384)) + list(range(384, 416))
    cols += list(range(320, 384)) + [384 + int(p) for p in pM]
    qs = np.arange(416, 672)
    cols += list(qs)
    cols += [416 + (i // 64) * 64 + int(pS[i % 64]) for i in range(256)]
    for kh in range(2):
        k0 = 672 + kh * 64
        cols += list(range(k0, k0 + 64)) * 2
        cols += [k0 + int(p) for p in pS] * 2
    cols += list(range(800, 928))
    cols += list(range(928, 1440))
    assert len(cols) == NWIN
    w_in_ext = np.ascontiguousarray(w_in[:, :, np.array(cols)])
    w_uq = f(inp["mla_w_uq"])
    ucols = []
    for h in range(8):
        b0 = h * 96
        ucols += list(range(b0, b0 + 96))
        ucols += list(range(b0, b0 + 64)) + [b0 + 64 + int(p) for p in pM]
    w_uq_ext = np.ascontiguousarray(w_uq[:, :, np.array(ucols)])

    def vec_layers():
        out = np.zeros((128, DEPTH, NL), np.float32)
        for l in range(DEPTH):
            out[:, l, V_ADAB:V_ADAB + 48] = f(inp["ada_b"])[l].reshape(48, 128).T
            out[:, l, V_N1G:V_N1G + 8] = f(inp["norm1_g"])[l].reshape(8, 128).T
            out[:, l, V_N2G:V_N2G + 8] = f(inp["norm2_g"])[l].reshape(8, 128).T
            out[:, l, V_QNG:V_QNG + 2] = f(inp["mla_q_norm_g"])[l].reshape(2, 128).T
            out[:, l, V_KVG] = f(inp["mla_kv_norm_g"])[l]
            out[:, l, V_CDW:V_CDW + 62] = f(inp["conv_dw_w"])[l].reshape(31, 2, 128).transpose(2, 0, 1).reshape(128, 62)
            out[:, l, V_CDB:V_CDB + 2] = f(inp["conv_dw_b"])[l].reshape(2, 128).T
            out[:, l, V_CLG:V_CLG + 2] = f(inp["conv_ln_g"])[l].reshape(2, 128).T
            out[:, l, V_CLB:V_CLB + 2] = f(inp["conv_ln_b"])[l].reshape(2, 128).T
            out[:, l, V_FDW:V_FDW + 132] = f(inp["ffn_dw_w"])[l].reshape(3, 44, 128).transpose(2, 0, 1).reshape(128, 132)
            out[:, l, V_FDB:V_FDB + 44] = f(inp["ffn_dw_b"])[l].reshape(44, 128).T
            out[:, l, V_SINK:V_SINK + 4] = f(inp["swa_sink"])[l][None, :]
        return out.reshape(128, DEPTH * NL)
    vl = vec_layers()
    fng = f(inp["final_norm_g"]).reshape(8, 128).T

    shared = dict(
        ident=np.eye(128, dtype=np.float32),
        ada_w=f(inp["ada_w"]), w_in_ext=w_in_ext, w_uq_ext=w_uq_ext, w_ukv=f(inp["mla_w_ukv"]),
        pw2=f(inp["conv_w_pw2"]), w_out=f(inp["w_out"]), w_up=f(inp["ffn_w_up"]), w_down=f(inp["ffn_w_down"]),
    )
    kaug = np.zeros((9, NKEY), np.float32)
    for j in range(8):
        kaug[j, j * 256:(j + 1) * 256] = 1.0
    kaug[8, :] = -BIGM
    kl = np.arange(128)[:, None]
    ql = np.arange(128)[None, :]

    def core_inputs(sample, xtok, cvec, caches):
        vecs = np.zeros((128, NV), np.float32)
        vecs[:, :DEPTH * NL] = vl
        vecs[:, V_FNG:V_FNG + 8] = fng
        vecs[:, V_CVEC:V_CVEC + 8] = cvec.reshape(8, 128).T
        vecs[:, V_FLAG] = 1.0 if sample else 0.0
        vecs[:, V_CTXB] = 0.0 if sample else NEG
        ropeS, ropeM = _rope_tables(sample)
        qaug = np.zeros((9, T), np.float32)
        smask = np.zeros((128, 4, 128), np.float32)
        if sample:
            lo = np.where(ql <= kl, 0.0, NEG).astype(np.float32)
            hi = np.where(kl <= ql, 0.0, NEG).astype(np.float32)
            smask[:, 0], smask[:, 1], smask[:, 2], smask[:, 3] = lo, hi, lo, hi
        else:
            for j in range(8):
                qaug[j, j * 256:(j + 1) * 256] = BIGM
            qaug[8, :] = 1.0
            smask[:, 0] = NEG
            smask[:, 3] = NEG
        smask = np.ascontiguousarray(np.concatenate([smask, smask], axis=2))
        d = dict(x=np.ascontiguousarray(xtok), vecs=vecs, ropeS=ropeS, ropeM=ropeM, qaug=qaug, kaug=kaug,
                 smask=smask, cckv=caches[0], ckr=caches[1], ck=caches[2], cv=caches[3])
        d.update(shared)
        return d

    xp = f(inp["x_prompt"])
    xs = f(inp["x_sample"])
    cc = [f(inp["cache_mla_ckv"]), f(inp["cache_mla_krope"]),
          f(inp["cache_swa_k"]).reshape(2, DEPTH, L, 128), f(inp["cache_swa_v"]).reshape(2, DEPTH, L, 128)]
    zc = [np.zeros((DEPTH, L, 128), np.float32), np.zeros((DEPTH, L, 32), np.float32),
          np.zeros((DEPTH, L, 128), np.float32), np.zeros((DEPTH, L, 128), np.float32)]
    c = f(inp["c"])
    cctx = f(inp["c_ctx"])
    maps = []
    for b in range(2):
        maps.append(core_inputs(True, xs[b], c[b], [a[b] for a in cc]))
    for q in range(4):
        maps.append(core_inputs(False, xp[q * 8:(q + 1) * 8].reshape(T, D), cctx, zc))
    maps.append(maps[2])
    maps.append(maps[3])
    return maps


_NC_CACHE = {}
_TEST_CORES = None


def kernel(**inputs):
    maps = _host_prep(inputs)
    if "nc" not in _NC_CACHE:
        _NC_CACHE["nc"] = build_program(DEPTH)
    nc = _NC_CACHE["nc"]
    if _TEST_CORES is not None:
        sub = [maps[i] for i in _TEST_CORES]
        rr = run_bass_kernel_spmd(nc, sub, core_ids=list(range(len(sub)))).results
        r = [rr[_TEST_CORES.index(i)] if i in _TEST_CORES else rr[0 if i < 2 else len(sub) - 1] for i in range(8)]
    else:
        res = run_bass_kernel_spmd(nc, maps, core_ids=list(range(8)))
        r = res.results
    y_sample = np.stack([r[0]["y"], r[1]["y"]], axis=0).astype(np.float32)
    y_prompt = np.concatenate([r[2 + q]["y"].reshape(8, 256, D) for q in range(4)], axis=0).astype(np.float32)

    def st(name, w):
        parts = []
        for q in range(4):
            a = r[2 + q][name].reshape(DEPTH, 8, 256, w).transpose(1, 0, 2, 3)
            parts.append(a)
        return np.concatenate(parts, axis=0).astype(np.float32)
    s_ckv = st("st_ckv", 128)
    s_kr = st("st_kr", 32)
    s_k = st("st_k", 128).reshape(32, DEPTH, 256, 2, 64)
    s_v = st("st_v", 128).reshape(32, DEPTH, 256, 2, 64)
    return (y_prompt, y_sample, s_ckv, s_kr, s_k, s_v)
```
